# Optimizing a Trainium2 kernel written in Bass

```python
import math
import jax, jax.numpy as jnp
from jax import lax
import numpy as np

D_MODEL = 2048
BATCH = 8
SEQ = 2048
DEPTH = 2

W_RWKV = D_MODEL // 4
HEAD_DIM_RWKV = 64
N_HEADS_RWKV = W_RWKV // HEAD_DIM_RWKV
LORA_DECAY = 64
LORA_ICLR = 64
LORA_VRES = 32
LORA_GATE = 128
RWKV_GN_EPS = 64e-5
W_GDN = D_MODEL // 4
HEAD_DIM_GDN = 128
N_HEADS_GDN = W_GDN // HEAD_DIM_GDN
CONV_WIDTH = 4
GDN_CHUNK = 64
W_DIFF = D_MODEL // 2
HEAD_DIM_DIFF = 64
N_HEADS_DIFF = W_DIFF // (2 * HEAD_DIM_DIFF)
Q_BLOCK = 128
RWKV_COLS = 3 * W_RWKV + LORA_DECAY + LORA_ICLR + LORA_GATE
GDN_COLS = 4 * W_GDN + 2 * N_HEADS_GDN
DIFF_COLS = 3 * W_DIFF
N_IN = RWKV_COLS + GDN_COLS + DIFF_COLS
MIX_WIDTH = W_RWKV + W_GDN + W_DIFF
FFN_HIDDEN = -(-8 * D_MODEL // (3 * 256)) * 256
NORM_EPS = 1e-6

kernel_name = "hybrid_rwkv7_gdn_diffattn_block"


def rms_norm(x, w, eps=NORM_EPS):
    xf = x.astype(jnp.float32)
    y = xf * lax.rsqrt(jnp.mean(xf * xf, axis=-1, keepdims=True) + eps)
    return (y * w.astype(jnp.float32)).astype(x.dtype)


def l2_normalize(x, eps=NORM_EPS):
    xf = x.astype(jnp.float32)
    return xf * lax.rsqrt(jnp.sum(xf * xf, axis=-1, keepdims=True) + eps)


def token_shift(x):
    return jnp.pad(x, ((0, 0), (1, 0), (0, 0)))[:, :-1]


def causal_depthwise_conv(x, w):
    k, c = w.shape
    return lax.conv_general_dilated(
        x, w[:, None, :].astype(x.dtype), window_strides=(1,), padding=[(k - 1, 0)],
        dimension_numbers=("NWC", "WIO", "NWC"), feature_group_count=c)


def wkv7_scan(r, decay, k, v, a, b):
    bsz, _, h, n = r.shape

    def step(s, inp):
        r_t, w_t, k_t, v_t, a_t, b_t = inp
        sa = jnp.einsum("bhij,bhj->bhi", s, a_t)
        s = s * w_t[:, :, None, :] + sa[..., None] * b_t[:, :, None, :] + v_t[..., None] * k_t[:, :, None, :]
        return s, jnp.einsum("bhij,bhj->bhi", s, r_t)

    xs = tuple(jnp.moveaxis(t, 1, 0) for t in (r, decay, k, v, a, b))
    _, ys = lax.scan(step, jnp.zeros((bsz, h, n, n), jnp.float32), xs)
    return jnp.moveaxis(ys, 0, 1)


def rwkv7_time_mix(p, v_first, w0, w_lora_b, a0, a_lora_b, g_lora_b, k_k, k_a, r_k,
                   ln_w, ln_b, v0, v_lora_b):
    bsz, t, _ = p.shape
    h, n, wd = N_HEADS_RWKV, HEAD_DIM_RWKV, W_RWKV
    r = p[..., 0:wd]
    k = p[..., wd:2 * wd]
    v = p[..., 2 * wd:3 * wd]
    o = 3 * wd
    xw = p[..., o:o + LORA_DECAY]
    o += LORA_DECAY
    xa = p[..., o:o + LORA_ICLR]
    o += LORA_ICLR
    xg = p[..., o:o + LORA_GATE]
    w_log = -jax.nn.softplus(-(w0 + jnp.tanh(xw) @ w_lora_b)) - 0.5
    if v0 is None:
        v_first = v
    else:
        xv = p[..., RWKV_COLS:]
        v = v + (v_first - v) * jax.nn.sigmoid(v0 + xv @ v_lora_b)
    a = jax.nn.sigmoid(a0 + xa @ a_lora_b)
    g = jax.nn.sigmoid(xg) @ g_lora_b
    heads = lambda z: z.reshape(bsz, t, h, n).astype(jnp.float32)
    kk = l2_normalize(heads(k * k_k))
    k = k * (1.0 + (a - 1.0) * k_a)
    rh, kh, vh, ah = heads(r), heads(k), heads(v), heads(a)
    decay = jnp.exp(-jnp.exp(heads(w_log)))
    y = wkv7_scan(rh, decay, kh, vh, -kk, kk * ah)
    mean = jnp.mean(y, axis=-1, keepdims=True)
    var = jnp.mean(jnp.square(y - mean), axis=-1, keepdims=True)
    y = (y - mean) * lax.rsqrt(var + RWKV_GN_EPS)
    y = y * ln_w.reshape(h, n).astype(jnp.float32) + ln_b.reshape(h, n).astype(jnp.float32)
    bonus = jnp.sum(rh * kh * r_k.astype(jnp.float32), axis=-1, keepdims=True) * vh
    out = (y + bonus).reshape(bsz, t, wd).astype(p.dtype) * g
    return out, v_first


def gated_delta_rule_chunked(q, k, v, g, beta):
    bsz, t, h, dk = q.shape
    dv = v.shape[-1]
    c = GDN_CHUNK
    nc = t // c
    ch4 = lambda z: z.reshape(bsz, nc, c, h, z.shape[-1]).transpose(0, 3, 1, 2, 4)
    ch3 = lambda z: z.reshape(bsz, nc, c, h).transpose(0, 3, 1, 2)
    q, k, v = ch4(q * dk ** -0.5), ch4(k), ch4(v)
    g, beta = ch3(g), ch3(beta)
    gc = jnp.cumsum(g, axis=-1)
    idx = jnp.arange(c)
    causal = idx[:, None] >= idx[None, :]
    strict = idx[:, None] > idx[None, :]
    decay = jnp.exp(jnp.where(causal, gc[..., :, None] - gc[..., None, :], -jnp.inf))
    kb = k * beta[..., None]
    m = jnp.where(strict, jnp.einsum("bhnik,bhnjk->bhnij", kb, k) * decay, 0.0)
    lhs = jnp.eye(c, dtype=jnp.float32) + m
    rhs = jnp.concatenate([v * beta[..., None], kb * jnp.exp(gc)[..., None]], axis=-1)
    sol = lax.linalg.triangular_solve(lhs, rhs, left_side=True, lower=True, unit_diagonal=True)
    u, w = sol[..., :dv], sol[..., dv:]
    attn = jnp.einsum("bhnik,bhnjk->bhnij", q, k) * decay
    q_dec = q * jnp.exp(gc)[..., None]
    k_dec = k * jnp.exp(gc[..., -1:] - gc)[..., None]
    g_last = jnp.exp(gc[..., -1])

    def step(s, inp):
        u_c, w_c, q_c, k_c, a_c, gl = inp
        v_new = u_c - jnp.einsum("bhck,bhkv->bhcv", w_c, s)
        o = jnp.einsum("bhck,bhkv->bhcv", q_c, s) + jnp.einsum("bhcj,bhjv->bhcv", a_c, v_new)
        s = s * gl[..., None, None] + jnp.einsum("bhck,bhcv->bhkv", k_c, v_new)
        return s, o

    xs = tuple(jnp.moveaxis(z, 2, 0) for z in (u, w, q_dec, k_dec, attn, g_last))
    _, o = lax.scan(step, jnp.zeros((bsz, h, dk, dv), jnp.float32), xs)
    return o.transpose(1, 0, 3, 2, 4).reshape(bsz, t, h, dv)


def gated_deltanet(p, conv_w, a_log, dt_bias, norm_w):
    bsz, t, _ = p.shape
    h, d, wd = N_HEADS_GDN, HEAD_DIM_GDN, W_GDN
    qkv = jax.nn.silu(causal_depthwise_conv(p[..., :3 * wd], conv_w))
    z = p[..., 3 * wd:4 * wd].reshape(bsz, t, h, d)
    a = p[..., 4 * wd:4 * wd + h].astype(jnp.float32)
    b = p[..., 4 * wd + h:4 * wd + 2 * h].astype(jnp.float32)
    q = l2_normalize(qkv[..., :wd].reshape(bsz, t, h, d))
    k = l2_normalize(qkv[..., wd:2 * wd].reshape(bsz, t, h, d))
    v = qkv[..., 2 * wd:].reshape(bsz, t, h, d).astype(jnp.float32)
    beta = jax.nn.sigmoid(b)
    g = -jnp.exp(a_log.astype(jnp.float32)) * jax.nn.softplus(a + dt_bias.astype(jnp.float32))
    o = gated_delta_rule_chunked(q, k, v, g, beta).astype(p.dtype)
    o = rms_norm(o, norm_w) * jax.nn.silu(z)
    return o.reshape(bsz, t, wd)


def differential_attention(p, q_norm_w, k_norm_w, lq1, lk1, lq2, lk2, subln_w, lambda_init):
    bsz, t, _ = p.shape
    h, d, wd = N_HEADS_DIFF, HEAD_DIM_DIFF, W_DIFF
    q = rms_norm(p[..., :wd].reshape(bsz, t, h, 2, d), q_norm_w)
    k = rms_norm(p[..., wd:2 * wd].reshape(bsz, t, h, 2, d), k_norm_w)
    v = p[..., 2 * wd:3 * wd].reshape(bsz, t, h, 2 * d)
    f32 = jnp.float32
    lam = (jnp.exp(jnp.sum(lq1.astype(f32) * lk1.astype(f32)))
           - jnp.exp(jnp.sum(lq2.astype(f32) * lk2.astype(f32))) + lambda_init)
    scale = d ** -0.5
    outs = []
    for i in range(t // Q_BLOCK):
        s0 = i * Q_BLOCK
        end = s0 + Q_BLOCK
        s = jnp.einsum("bqhmd,bkhmd->bhmqk", q[:, s0:end], k[:, :end]).astype(f32) * scale
        mask = jnp.arange(end)[None, :] <= (s0 + jnp.arange(Q_BLOCK))[:, None]
        prob = jax.nn.softmax(jnp.where(mask, s, -jnp.inf), axis=-1)
        pdiff = prob[:, :, 0] - lam * prob[:, :, 1]
        outs.append(jnp.einsum("bhqk,bkhe->bqhe", pdiff.astype(v.dtype), v[:, :end]))
    o = jnp.concatenate(outs, axis=1)
    o = rms_norm(o, subln_w) * (1.0 - lambda_init)
    return o.reshape(bsz, t, wd)


def setup_inputs(seed: int = 0) -> dict:
    key = jax.random.key(seed)
    ks = iter(jax.random.split(key, 48))
    nrm = lambda shape, scale: scale * jax.random.normal(next(ks), shape, jnp.float32)
    uni = lambda shape, lo, hi: jax.random.uniform(next(ks), shape, jnp.float32, lo, hi)
    L, Lv, D = DEPTH, DEPTH - 1, D_MODEL
    dt = jnp.exp(uni((L, N_HEADS_GDN), math.log(1e-3), math.log(1e-1)))
    return {
        "x": nrm((BATCH, SEQ, D), 1.0),
        "attn_norm_w": 1.0 + nrm((L, D), 0.02),
        "w_in": nrm((L, D, N_IN), D ** -0.5),
        "w_vres_a": nrm((Lv, D, LORA_VRES), D ** -0.5),
        "mu_rwkv": uni((L, RWKV_COLS), 0.0, 1.0),
        "mu_vres": uni((Lv, LORA_VRES), 0.0, 1.0),
        "rwkv_w0": uni((L, W_RWKV), -5.0, 1.0),
        "rwkv_w_lora_b": nrm((L, LORA_DECAY, W_RWKV), 0.5 * LORA_DECAY ** -0.5),
        "rwkv_a0": nrm((L, W_RWKV), 0.5),
        "rwkv_a_lora_b": nrm((L, LORA_ICLR, W_RWKV), 0.5 * LORA_ICLR ** -0.5),
        "rwkv_g_lora_b": nrm((L, LORA_GATE, W_RWKV), LORA_GATE ** -0.5),
        "rwkv_v0": nrm((Lv, W_RWKV), 0.5),
        "rwkv_v_lora_b": nrm((Lv, LORA_VRES, W_RWKV), 0.5 * LORA_VRES ** -0.5),
        "rwkv_k_k": 0.85 + nrm((L, W_RWKV), 0.02),
        "rwkv_k_a": 1.0 + nrm((L, W_RWKV), 0.02),
        "rwkv_r_k": nrm((L, N_HEADS_RWKV, HEAD_DIM_RWKV), 0.1),
        "rwkv_ln_w": 1.0 + nrm((L, W_RWKV), 0.02),
        "rwkv_ln_b": nrm((L, W_RWKV), 0.02),
        "gdn_conv_w": nrm((L, CONV_WIDTH, 3 * W_GDN), CONV_WIDTH ** -0.5),
        "gdn_A_log": jnp.log(uni((L, N_HEADS_GDN), 1.0, 16.0)),
        "gdn_dt_bias": dt + jnp.log(-jnp.expm1(-dt)),
        "gdn_norm_w": 1.0 + nrm((L, HEAD_DIM_GDN), 0.02),
        "diff_q_norm_w": 1.0 + nrm((L, HEAD_DIM_DIFF), 0.02),
        "diff_k_norm_w": 1.0 + nrm((L, HEAD_DIM_DIFF), 0.02),
        "diff_lambda_q1": nrm((L, HEAD_DIM_DIFF), 0.1),
        "diff_lambda_k1": nrm((L, HEAD_DIM_DIFF), 0.1),
        "diff_lambda_q2": nrm((L, HEAD_DIM_DIFF), 0.1),
        "diff_lambda_k2": nrm((L, HEAD_DIM_DIFF), 0.1),
        "diff_subln_w": 1.0 + nrm((L, 2 * HEAD_DIM_DIFF), 0.02),
        "w_out": nrm((L, MIX_WIDTH, D), MIX_WIDTH ** -0.5),
        "ffn_norm_w": 1.0 + nrm((L, D), 0.02),
        "w_ffn_in": nrm((L, D, 2 * FFN_HIDDEN), D ** -0.5),
        "w_ffn_out": nrm((L, FFN_HIDDEN, D), FFN_HIDDEN ** -0.5),
    }


def reference(x, attn_norm_w, w_in, w_vres_a, mu_rwkv, mu_vres, rwkv_w0, rwkv_w_lora_b,
              rwkv_a0, rwkv_a_lora_b, rwkv_g_lora_b, rwkv_v0, rwkv_v_lora_b, rwkv_k_k,
              rwkv_k_a, rwkv_r_k, rwkv_ln_w, rwkv_ln_b, gdn_conv_w, gdn_A_log, gdn_dt_bias,
              gdn_norm_w, diff_q_norm_w, diff_k_norm_w, diff_lambda_q1, diff_lambda_k1,
              diff_lambda_q2, diff_lambda_k2, diff_subln_w, w_out, ffn_norm_w, w_ffn_in,
              w_ffn_out):
    v_first = None
    for l in range(DEPTH):
        h = rms_norm(x, attn_norm_w[l])
        if l == 0:
            w_proj, mu = w_in[l], mu_rwkv[l]
        else:
            w_proj = jnp.concatenate([w_in[l], w_vres_a[l - 1]], axis=1)
            mu = jnp.concatenate([mu_rwkv[l], mu_vres[l - 1]])
        p = h @ w_proj
        p_rw = p[..., :RWKV_COLS] if l == 0 else jnp.concatenate([p[..., :RWKV_COLS], p[..., N_IN:]], axis=-1)
        p_rw = p_rw + (token_shift(p_rw) - p_rw) * mu
        p_gdn = p[..., RWKV_COLS:RWKV_COLS + GDN_COLS]
        p_diff = p[..., RWKV_COLS + GDN_COLS:N_IN]
        if l == 0:
            y_rw, v_first = rwkv7_time_mix(
                p_rw, None, rwkv_w0[l], rwkv_w_lora_b[l], rwkv_a0[l], rwkv_a_lora_b[l],
                rwkv_g_lora_b[l], rwkv_k_k[l], rwkv_k_a[l], rwkv_r_k[l], rwkv_ln_w[l],
                rwkv_ln_b[l], None, None)
        else:
            y_rw, _ = rwkv7_time_mix(
                p_rw, v_first, rwkv_w0[l], rwkv_w_lora_b[l], rwkv_a0[l], rwkv_a_lora_b[l],
                rwkv_g_lora_b[l], rwkv_k_k[l], rwkv_k_a[l], rwkv_r_k[l], rwkv_ln_w[l],
                rwkv_ln_b[l], rwkv_v0[l - 1], rwkv_v_lora_b[l - 1])
        y_gdn = gated_deltanet(p_gdn, gdn_conv_w[l], gdn_A_log[l], gdn_dt_bias[l], gdn_norm_w[l])
        lambda_init = 0.8 - 0.6 * math.exp(-0.3 * l)
        y_diff = differential_attention(
            p_diff, diff_q_norm_w[l], diff_k_norm_w[l], diff_lambda_q1[l], diff_lambda_k1[l],
            diff_lambda_q2[l], diff_lambda_k2[l], diff_subln_w[l], lambda_init)
        mixed = jnp.concatenate([y_rw, y_gdn, y_diff], axis=-1)
        x = x + mixed @ w_out[l]
        h = rms_norm(x, ffn_norm_w[l])
        gate, up = jnp.split(h @ w_ffn_in[l], 2, axis=-1)
        x = x + (jax.nn.silu(gate) * up) @ w_ffn_out[l]
    return x
```

```python
import numpy as np
from contextlib import ExitStack
import concourse.bass as bass
import concourse.mybir as mybir

F32 = mybir.dt.float32
BF16 = mybir.dt.bfloat16
ALU = mybir.AluOpType
AF = mybir.ActivationFunctionType
AX = mybir.AxisListType

COMPUTE = ("pe", "dve", "act", "pool")
DMA_RING = 8


class View:
    __slots__ = ("tile", "ap", "key")

    def __init__(self, tile, ap, key=None):
        self.tile = tile
        self.ap = ap
        self.key = key

    def __getitem__(self, idx):
        return View(self.tile, self.ap[idx], self.key)

    def k(self, key):
        return View(self.tile, self.ap, key)

    def re(self, s, **kw):
        return View(self.tile, self.ap.rearrange(s, **kw), self.key)

    def bc(self, shape):
        return View(self.tile, self.ap.to_broadcast(shape), self.key)

    def bitcast(self, dt):
        return View(self.tile, self.ap.bitcast(dt), self.key)

    @property
    def shape(self):
        return self.ap.shape

    def v(self, key=None):
        return self if key is None else View(self.tile, self.ap, key)


class Buf:
    def __init__(self, name, ap):
        self.name = name
        self.ap = ap
        self.state = {None: [None, []]}

    def __getitem__(self, idx):
        return View(self, self.ap[idx], None)

    def v(self, key=None):
        return View(self, self.ap, key)

    def reset(self):
        self.state = {None: [None, []]}


class Op:
    __slots__ = ("eng", "emit", "deps", "sig", "sigval", "is_dma", "dsem", "dval", "dprev", "idx", "label")

    def __init__(self, eng, emit, is_dma=False, label=""):
        self.eng = eng
        self.emit = emit
        self.deps = {}
        self.sig = False
        self.sigval = 0
        self.is_dma = is_dma
        self.dsem = None
        self.dval = 0
        self.dprev = 0
        self.label = label


class FW:
    def __init__(self, nc, same_engine_sync=True):
        self.nc = nc
        self.es = ExitStack()
        self.engs = {"pe": nc.tensor, "dve": nc.vector, "act": nc.scalar, "pool": nc.gpsimd, "sp": nc.sync}
        self.sems = {}
        for e in self.engs:
            self.sems[e] = self.es.enter_context(nc.semaphore("s_" + e))
        self.dma_rings = {}
        self.dma_count = {}
        for q in ("sp", "pool", "act"):
            self.dma_rings[q] = [self.es.enter_context(nc.semaphore("d_%s%d" % (q, i))) for i in range(DMA_RING)]
            self.dma_count[q] = 0
        self.ops = []
        self.tiles = []
        self.stage_es = None
        self.same_engine_sync = same_engine_sync
        self.last_op = {e: None for e in self.engs}
        self.pending_dma = []
        self.n_emitted = 0

    def begin_stage(self):
        self.stage_es = ExitStack()

    def sb(self, name, shape, dtype=F32):
        self.uid = getattr(self, "uid", 0) + 1
        name = "%s_u%d" % (name, self.uid)
        t = self.stage_es.enter_context(self.nc.sbuf_tensor(name, list(shape), dtype))
        tl = Buf(name, t.ap() if hasattr(t, "ap") and callable(getattr(t, "ap")) else t[:])
        self.tiles.append(tl)
        return tl

    def ps(self, name, shape, dtype=F32):
        self.uid = getattr(self, "uid", 0) + 1
        name = "%s_u%d" % (name, self.uid)
        t = self.stage_es.enter_context(self.nc.psum_tensor(name, list(shape), dtype))
        tl = Buf(name, t.ap() if hasattr(t, "ap") and callable(getattr(t, "ap")) else t[:])
        self.tiles.append(tl)
        return tl

    def dram(self, name, shape, dtype=F32, kind="Internal"):
        t = self.nc.dram_tensor(name, list(shape), dtype, kind=kind)
        tl = Buf(name, t.ap())
        self.tiles.append(tl)
        return tl

    def _keys(self, v):
        st = v.tile.state
        if v.key is None:
            return list(st.keys())
        if v.key not in st:
            st[v.key] = [None, []]
        return [v.key, None]

    def _read(self, op, v):
        st = v.tile.state
        for k in self._keys(v):
            w = st[k][0]
            if w is not None and w is not op:
                op.deps[w] = True
        st[v.key][1].append(op)

    def _write(self, op, v):
        st = v.tile.state
        for k in self._keys(v):
            w, rs = st[k]
            if w is not None and w is not op:
                op.deps.setdefault(w, False)
            for r in rs:
                if r is not op:
                    op.deps.setdefault(r, False)
        if v.key is None:
            for k in list(st.keys()):
                st[k] = [op, []]
        else:
            st[v.key] = [op, []]

    def add(self, eng, emit, reads, writes, is_dma=False, label=""):
        op = Op(eng, emit, is_dma, label)
        for v in reads:
            self._read(op, v)
        for v in writes:
            self._write(op, v)
        if is_dma:
            j = self.dma_count[eng]
            self.dma_count[eng] = j + 1
            op.dsem = self.dma_rings[eng][j % DMA_RING]
            op.dval = 16 * (j // DMA_RING + 1)
            op.dprev = 16 * (j // DMA_RING)
            self.pending_dma.append(op)
        self.ops.append(op)
        self.last_op[eng] = op
        return op

    def barrier(self):
        lasts = [o for o in self.last_op.values() if o is not None and not o.is_dma]
        comp_last = {}
        for o in self.ops:
            if not o.is_dma and o.emit is not None:
                comp_last[o.eng] = o
        for e in self.engs:
            b = Op(e, None, False, "barrier")
            for e2, o in comp_last.items():
                if e2 != e:
                    b.deps[o] = True
            for d in self.pending_dma:
                b.deps[d] = True
            self.ops.append(b)
        self.pending_dma = []
        for t in self.tiles:
            t.reset()

    def end_stage(self):
        self.barrier()
        self.flush()
        self.stage_es.close()
        self.stage_es = None
        self.tiles = [t for t in self.tiles if t.name.startswith("D_")]

    def flush(self):
        ops = self.ops
        if not hasattr(self, "known"):
            self.known = {e: {} for e in self.engs}
            self.sigcount = {e: 0 for e in self.engs}
        for op in ops:
            for d, raw in op.deps.items():
                if d.is_dma:
                    continue
                if d.eng != op.eng:
                    d.sig = True
                elif self.same_engine_sync and raw and d.eng in ("dve", "act", "pool") and not op.is_dma:
                    d.sig = True
                elif op.is_dma and d.eng == op.eng:
                    d.sig = True
        for op in ops:
            e = op.eng
            eng = self.engs[e]
            kn = self.known[e]
            waits = {}
            for d, raw in op.deps.items():
                if d.is_dma:
                    s, val = d.dsem, d.dval
                elif d.sig and (d.eng != e or raw or op.is_dma):
                    s, val = self.sems[d.eng], d.sigval
                    assert val > 0, (d.label, op.label)
                else:
                    continue
                if kn.get(s, 0) < val:
                    waits[s] = max(waits.get(s, 0), val)
            if op.is_dma and op.dprev > 0:
                if kn.get(op.dsem, 0) < op.dprev:
                    waits[op.dsem] = max(waits.get(op.dsem, 0), op.dprev)
            for s, val in waits.items():
                eng.wait_ge(s, val)
                kn[s] = val
            if op.emit is None:
                continue
            ins = op.emit(eng)
            self.n_emitted += 1
            if op.is_dma:
                ins.then_inc(op.dsem, 16)
            elif op.sig:
                self.sigcount[e] += 1
                op.sigval = self.sigcount[e]
                ins.then_inc(self.sems[e], 1)
        self.ops = []

    def dma(self, out, in_, q="sp", **kw):
        o, i = out.ap, in_.ap
        return self.add(q, lambda eng: eng.dma_start(out=o, in_=i, **kw), [in_], [out], is_dma=True, label="dma")

    def mm(self, out, lhsT, rhs, start=True, stop=True, **kw):
        o, l, r = out.ap, lhsT.ap, rhs.ap
        return self.add("pe", lambda eng: eng.matmul(o, l, r, start=start, stop=stop, **kw), [lhsT, rhs], [out], label="mm")

    def transpose(self, out, in_, ident):
        o, i, d = out.ap, in_.ap, ident.ap
        return self.add("pe", lambda eng: eng.transpose(o, i, d), [in_, ident], [out], label="tr")

    def act(self, out, in_, func, bias=None, scale=None, accum=None, eng="act"):
        o, i = out.ap, in_.ap
        reads = [in_]
        kw = {}
        if bias is not None:
            if isinstance(bias, View):
                reads.append(bias)
                kw["bias"] = bias.ap
            else:
                kw["bias"] = float(bias)
        if scale is not None:
            if isinstance(scale, View):
                reads.append(scale)
                kw["scale"] = scale.ap
            else:
                kw["scale"] = float(scale)
        writes = [out]
        if accum is not None:
            writes.append(accum)
            kw["accum_out"] = accum.ap
        return self.add(eng, lambda en: en.activation(o, i, func, **kw), reads, writes, label="act")

    def tt(self, out, a, b, op, eng="dve"):
        o, x, y = out.ap, a.ap, b.ap
        return self.add(eng, lambda en: en.tensor_tensor(o, x, y, op), [a, b], [out], label="tt")

    def ts(self, out, a, s1, op0, s2=None, op1=None, accum=None, eng="dve"):
        o, x = out.ap, a.ap
        reads = [a]
        if isinstance(s1, View):
            reads.append(s1)
            s1v = s1.ap
        else:
            s1v = float(s1)
        if isinstance(s2, View):
            reads.append(s2)
            s2v = s2.ap
        else:
            s2v = None if s2 is None else float(s2)
        writes = [out]
        kw = {}
        if op1 is not None:
            kw["op1"] = op1
        if accum is not None:
            writes.append(accum)
            kw["accum_out"] = accum.ap
        return self.add(eng, lambda en: en.tensor_scalar(o, x, s1v, s2v, op0, **kw), reads, writes, label="ts")

    def stt(self, out, a, s, b, op0, op1, accum=None, eng="dve"):
        o, x, y = out.ap, a.ap, b.ap
        reads = [a, b]
        if isinstance(s, View):
            reads.append(s)
            sv = s.ap
        else:
            sv = float(s)
        writes = [out]
        kw = {}
        if accum is not None:
            writes.append(accum)
            kw["accum_out"] = accum.ap
        return self.add(eng, lambda en: en.scalar_tensor_tensor(o, x, sv, y, op0, op1, **kw), reads, writes, label="stt")

    def copy(self, out, in_, eng="dve"):
        o, i = out.ap, in_.ap
        if eng == "act":
            return self.add("act", lambda en: en.copy(o, i), [in_], [out], label="copy")
        return self.add(eng, lambda en: en.tensor_copy(o, i), [in_], [out], label="copy")

    def memset(self, out, val, eng="dve"):
        o = out.ap
        return self.add(eng, lambda en: en.memset(o, val), [], [out], label="memset")

    def reduce(self, out, in_, op=ALU.add, axis=AX.X, eng="dve"):
        o, i = out.ap, in_.ap
        return self.add(eng, lambda en: en.tensor_reduce(o, i, axis, op), [in_], [out], label="reduce")

    def recip(self, out, in_):
        o, i = out.ap, in_.ap
        return self.add("dve", lambda en: en.reciprocal(o, i), [in_], [out], label="recip")

    def finish(self):
        self.barrier()
        self.flush()
        if self.stage_es is not None:
            self.stage_es.close()
        self.es.close()


import math
import os
F32R = mybir.dt.float32r

D = 2048
NIN = 6920
NP = 6952
FH = 5632
EPS = 1e-6
GDN0 = 1792
DIFF0 = 3848
C_ID = 0
C_LE = 128
C_LT = 192
C_GT = 256
C_GE = 320
C_ONE = 384
C_NLT = 512
C_NGT = 576
NCONST = 640


def make_consts():
    c = np.zeros((128, NCONST), np.float32)
    c[:, C_ID:C_ID + 128] = np.eye(128)
    i = np.arange(64)
    le = (i[:, None] <= i[None, :]).astype(np.float32)
    lt = (i[:, None] < i[None, :]).astype(np.float32)
    c[:64, C_LE:C_LE + 64] = le
    c[:64, C_LT:C_LT + 64] = lt
    c[:64, C_GT:C_GT + 64] = lt.T
    c[:64, C_GE:C_GE + 64] = le.T
    c[:, C_ONE:C_ONE + 128] = 1.0
    c[:64, C_NLT:C_NLT + 64] = -lt
    c[:64, C_NGT:C_NGT + 64] = -lt.T
    cm = np.zeros((128, 4, 512), np.float32)
    kp = np.arange(128)[:, None]
    q = np.arange(512)[None, :]
    for r in range(4):
        cm[:, r, :] = np.where(q >= r * 128 + kp, 0.0, -30000.0)
    return c, cm.reshape(128, 2048)


def run_interleaved(gens):
    alive = list(gens)
    while alive:
        for g in list(alive):
            try:
                next(g)
            except StopIteration:
                alive.remove(g)


def bcv(view, shape):
    return View(view.tile, view.ap.to_broadcast(list(shape)), view.key)


def rowbc(dt, row, c0, c1, parts):
    return View(dt, dt.ap[row:row + 1, c0:c1].to_broadcast([parts, c1 - c0]))


class Prog:
    def __init__(self, T, L=2, debug=False):
        self.T = T
        self.L = L
        self.debug = debug
        nc = bass.Bass("TRN2", target_bir_lowering=False)
        self.nc = nc
        fw = FW(nc)
        self.fw = fw
        I = lambda n, s: fw.dram(n, s, F32, kind="ExternalInput")
        self.x = I("x", [T, D])
        self.attn_norm_w = I("attn_norm_w", [L, D])
        self.w_in = I("w_in", [L * D, NIN])
        self.w_vres_a = I("w_vres_a", [D, 32])
        self.mu_rwkv = I("mu_rwkv", [L, 1792])
        self.mu_vres = I("mu_vres", [1, 32])
        self.rwkv_w0 = I("rwkv_w0", [L, 512])
        self.rwkv_w_lora_b = I("rwkv_w_lora_b", [L * 64, 512])
        self.rwkv_a0 = I("rwkv_a0", [L, 512])
        self.rwkv_a_lora_b = I("rwkv_a_lora_b", [L * 64, 512])
        self.rwkv_g_lora_b = I("rwkv_g_lora_b", [L * 128, 512])
        self.rwkv_v0 = I("rwkv_v0", [1, 512])
        self.rwkv_v_lora_b = I("rwkv_v_lora_b", [32, 512])
        self.rwkv_k_k = I("rwkv_k_k", [L, 512])
        self.rwkv_k_a = I("rwkv_k_a", [L, 512])
        self.rwkv_r_k = I("rwkv_r_k", [L, 512])
        self.rwkv_ln_w = I("rwkv_ln_w", [L, 512])
        self.rwkv_ln_b = I("rwkv_ln_b", [L, 512])
        self.gdn_conv_w = I("gdn_conv_w", [L * 4, 1536])
        self.gdn_A_log = I("gdn_A_log", [L, 4])
        self.gdn_dt_bias = I("gdn_dt_bias", [L, 4])
        self.gdn_norm_w = I("gdn_norm_w", [L, 128])
        self.diff_q_norm_w = I("diff_q_norm_w", [L, 64])
        self.diff_k_norm_w = I("diff_k_norm_w", [L, 64])
        self.diff_lambda_q1 = I("diff_lambda_q1", [L, 64])
        self.diff_lambda_k1 = I("diff_lambda_k1", [L, 64])
        self.diff_lambda_q2 = I("diff_lambda_q2", [L, 64])
        self.diff_lambda_k2 = I("diff_lambda_k2", [L, 64])
        self.diff_subln_w = I("diff_subln_w", [L, 128])
        self.w_out = I("w_out", [L * D, D])
        self.ffn_norm_w = I("ffn_norm_w", [L, D])
        self.w_ffn_in = I("w_ffn_in", [L * D, 2 * FH])
        self.w_ffn_out = I("w_ffn_out", [L * FH, D])
        self.consts = I("consts", [128, NCONST])
        self.cmask = I("cmask", [128, 2048])
        self.y = fw.dram("y", [T, D], F32, kind="ExternalOutput")
        okind = "ExternalOutput" if debug else "Internal"
        self.P = fw.dram("D_P", [T, NP], F32, kind=okind)
        self.MIX = fw.dram("D_MIX", [T, D], F32, kind=okind)
        self.VF = fw.dram("D_VF", [T, 512], F32, kind="Internal")
        self.rr = 0
        import os
        self.cut = int(os.environ.get('GDN_CUT', '0'))

    def ev(self):
        self.rr += 1
        return "dve" if self.rr % 2 else "act"

    def load_consts(self, bf=False):
        fw = self.fw
        c = fw.sb("consts_sb", [128, NCONST], F32)
        fw.dma(c.v(), self.consts.v(), q="sp")
        self.c = c
        if bf:
            idb = fw.sb("idb", [128, 128], BF16)
            fw.copy(idb.v(), c[:, C_ID:C_ID + 128])
            self.idb = idb

    def nt_alloc(self, normw_row, single=False):
        fw = self.fw
        a = {}
        nb = 1 if single else 2
        a["xts"] = [fw.sb("nt_x%d" % i, [128, D], F32) for i in range(nb)] * (3 - nb)
        a["hs"] = [fw.sb("nt_h%d" % i, [128, D], BF16) for i in range(nb)] * (3 - nb)
        a["ss"] = [fw.sb("nt_ss%d" % i, [128, 1], F32) for i in range(2)]
        a["pst"] = [fw.ps("nt_ps%d" % i, [128, 4, 128], BF16) for i in range(2)]
        a["wbc"] = None
        if normw_row is not None:
            a["wbc"] = fw.sb("nt_w", [128, D], F32)
            fw.dma(a["wbc"].v(), rowbc(normw_row[0], normw_row[1], 0, D, 128), q="sp")
        return a

    def norm_transpose(self, a, src, r0, nrows_tiles, hT, tcol0, xkeyf=None):
        fw = self.fw
        xts, hs, ss, pst, wbc = a["xts"], a["hs"], a["ss"], a["pst"], a["wbc"]
        for t in range(nrows_tiles):
            xt = xts[t % 2]
            h = hs[t % 2]
            s_ = ss[t % 2]
            rows = slice(r0 + t * 128, r0 + (t + 1) * 128)
            sv = src[rows, :]
            if xkeyf is not None:
                sv = sv.k(xkeyf(r0 // 128 + t))
            fw.dma(xt.v(), sv, q="sp")
            if wbc is not None:
                fw.memset(s_.v(), 0.0)
                fw.act(h.v(), xt.v(), AF.Square, accum=s_.v())
                fw.act(s_.v(), s_.v(), AF.Sqrt, bias=EPS, scale=1.0 / D)
                fw.recip(s_.v(), s_.v())
                fw.stt(h.v(), xt.v(), s_.v(), wbc.v(), ALU.mult, ALU.mult)
            else:
                fw.copy(h.v(), xt.v(), eng="dve")
            for g in range(4):
                pt = pst[g % 2]
                for j in range(4):
                    cc = g * 4 + j
                    fw.transpose(pt[:, j, :], h[:, cc * 128:(cc + 1) * 128], self.idb.v())
                fw.copy(hT[:, g * 4:(g + 1) * 4, tcol0 + t * 128: tcol0 + (t + 1) * 128].k(("t", tcol0 // 128 + t)), pt.v(), eng=self.ev())

    def proj_stage(self, src, normw_row, blocks, resid=None):
        fw = self.fw
        T = self.T
        fw.begin_stage()
        self.load_consts(bf=True)
        hT = fw.sb("hT", [128, 16, T], BF16)
        psm = [fw.ps("psm%d" % i, [128, 512], F32) for i in range(3)]
        nta = self.nt_alloc(normw_row)
        self.norm_transpose(nta, src, 0, T // 128, hT, 0)
        wts = [fw.sb("wt%d" % i, [128, 16, 512], BF16) for i in range(2)]
        obs = [fw.sb("ob%d" % i, [128, 512], F32) for i in range(3)]
        rbs = [fw.sb("rb%d" % i, [128, 512], F32) for i in range(3)] if resid is not None else None
        n = 0
        for bi, (wd, row0, col0, ncols, dst, dc0) in enumerate(blocks):
            wt = wts[bi % 2]
            wv = View(wd, wd.ap[row0:row0 + D, col0:col0 + ncols].rearrange("(c p) n -> p c n", p=128))
            fw.dma(wt[:, :, 0:ncols], wv, q="pool")
            for t in range(T // 128):
                pm = psm[n % 3]
                ob = obs[n % 3]
                for c in range(16):
                    fw.mm(pm[:, 0:ncols], hT[:, c, t * 128:(t + 1) * 128].k(("t", t)), wt[:, c, 0:ncols], start=(c == 0), stop=(c == 15))
                rows = slice(t * 128, (t + 1) * 128)
                if resid is not None:
                    rb = rbs[n % 3]
                    fw.dma(rb[:, 0:ncols], resid[rows, dc0:dc0 + ncols].k(t), q="sp")
                    fw.tt(ob[:, 0:ncols], pm[:, 0:ncols], rb[:, 0:ncols], ALU.add)
                else:
                    fw.copy(ob[:, 0:ncols], pm[:, 0:ncols], eng=self.ev())
                fw.dma(dst[rows, dc0:dc0 + ncols].k(t), ob[:, 0:ncols], q="sp")
                n += 1
        fw.end_stage()

    def in_proj(self, l):
        src = self.x if l == 0 else self.y
        blocks = []
        c = 0
        while c < NIN:
            n = min(512, NIN - c)
            blocks.append((self.w_in, l * D, c, n, self.P, c))
            c += n
        if l > 0:
            blocks.append((self.w_vres_a, 0, 0, 32, self.P, NIN))
        self.proj_stage(src, (self.attn_norm_w, l), blocks)

    def out_proj(self, l):
        src = self.x if l == 0 else self.y
        blocks = [(self.w_out, l * D, cb * 512, 512, self.y, cb * 512) for cb in range(4)]
        self.proj_stage(self.MIX, None, blocks, resid=src)

    def ffn(self, l):
        fw = self.fw
        T = self.T
        TB = min(T, 1024)
        NH = TB // 512 if TB >= 512 else 1
        HW = min(TB, 512)
        pes = ExitStack()
        self.uid_p = getattr(self, "uid_p", 0) + 1
        t_ = pes.enter_context(self.nc.sbuf_tensor("D_actT_%d" % self.uid_p, [128, 44, TB], BF16))
        actT = Buf("D_actT", t_[:])
        fw.tiles.append(actT)
        nblk = T // TB
        for tb in range(nblk):
            fw.begin_stage()
            self.load_consts(bf=True)
            hT = fw.sb("f_hT", [128, 16, TB], BF16)
            wg = [fw.sb("f_wg%d" % i, [128, 16, 128], BF16) for i in range(2)]
            wu = [fw.sb("f_wu%d" % i, [128, 16, 128], BF16) for i in range(2)]
            psgu = [(fw.ps("f_psg%d" % i, [128, 512], F32), fw.ps("f_psu%d" % i, [128, 512], F32)) for i in range(3)]
            sg = [fw.sb("f_sg%d" % i, [128, 512], F32) for i in range(2)]
            nta = self.nt_alloc((self.ffn_norm_w, l), single=True)
            self.norm_transpose(nta, self.y, tb * TB, TB // 128, hT, 0, xkeyf=lambda i: i)
            n = 0
            r0 = l * D
            for j in range(44):
                g_ = wg[j % 2]
                u_ = wu[j % 2]
                fw.dma(g_.v(), View(self.w_ffn_in, self.w_ffn_in.ap[r0:r0 + D, j * 128:(j + 1) * 128].rearrange("(c p) n -> p c n", p=128)), q="pool")
                fw.dma(u_.v(), View(self.w_ffn_in, self.w_ffn_in.ap[r0:r0 + D, FH + j * 128:FH + (j + 1) * 128].rearrange("(c p) n -> p c n", p=128)), q="pool")
                for hf in range(NH):
                    pg, pu = psgu[n % 3]
                    s_ = sg[n % 2]
                    cols = slice(hf * HW, (hf + 1) * HW)
                    for c in range(16):
                        fw.mm(pg[:, 0:HW], g_[:, c, :], hT[:, c, cols], start=(c == 0), stop=(c == 15))
                    for c in range(16):
                        fw.mm(pu[:, 0:HW], u_[:, c, :], hT[:, c, cols], start=(c == 0), stop=(c == 15))
                    fw.act(s_[:, 0:HW], pg[:, 0:HW], AF.Silu)
                    fw.tt(actT[:, j, cols], s_[:, 0:HW], pu[:, 0:HW], ALU.mult)
                    n += 1
            fw.end_stage()
            fw.begin_stage()
            NTT = TB // 128
            wo = [fw.sb("f_wo%d" % i, [128, 11, 512], BF16) for i in range(4)]
            pso = [fw.ps("f_pso%d" % i, [128, 512], F32) for i in range(NTT)]
            ob = [fw.sb("f_ob%d" % i, [128, 512], F32) for i in range(2)]
            rb = [fw.sb("f_rb%d" % i, [128, 512], F32) for i in range(2)]
            nw_ = 0
            m = 0
            r0 = l * FH
            for cb in range(4):
                for kq in range(4):
                    w_ = wo[nw_ % 4]
                    nw_ += 1
                    fw.dma(w_.v(), View(self.w_ffn_out, self.w_ffn_out.ap[r0 + kq * 1408:r0 + (kq + 1) * 1408, cb * 512:(cb + 1) * 512].rearrange("(c p) n -> p c n", p=128)), q="pool")
                    for tt in range(NTT):
                        for jj in range(11):
                            j = kq * 11 + jj
                            fw.mm(pso[tt].v(), actT[:, j, tt * 128:(tt + 1) * 128], w_[:, jj, :], start=(j == 0), stop=(j == 43))
                for tt in range(NTT):
                    gt = tb * NTT + tt
                    rows = slice(gt * 128, (gt + 1) * 128)
                    fw.dma(rb[m % 2].v(), self.y[rows, cb * 512:(cb + 1) * 512].k(gt), q="sp")
                    fw.tt(ob[m % 2].v(), pso[tt].v(), rb[m % 2].v(), ALU.add)
                    fw.dma(self.y[rows, cb * 512:(cb + 1) * 512].k(gt), ob[m % 2].v(), q="sp")
                    m += 1
            fw.end_stage()
        fw.tiles.remove(actT)
        pes.close()

    def inv_chain(self, nh, X, XT, TT, Xb, XTb, psA, psB, psC, gen=False):
        g = self._inv_chain_gen(nh, X, XT, TT, Xb, XTb, psA, psB, psC)
        if gen:
            return g
        for _ in g:
            pass

    def _inv_chain_gen(self, nh, X, XT, TT, Xb, XTb, psA, psB, psC):
        fw = self.fw
        cur, curT, nxt, nxtT = X, XT, Xb, XTb
        for step in range(5):
            last = step == 4
            for h in range(nh):
                fw.mm(psA[:, h, :], curT[:, h, :], cur[:, h, :])
            if not last:
                for h in range(nh):
                    fw.mm(psB[:, h, :], cur[:, h, :], curT[:, h, :])
            yield
            fw.copy(nxt.v(), psA, eng="act")
            if not last:
                fw.copy(nxtT.v(), psB, eng="dve")
            yield
            for h in range(nh):
                fw.mm(psC[:, h, :], nxt[:, h, :], TT[:, h, :])
            yield
            fw.tt(TT.v(), TT.v(), psC, ALU.add)
            yield
            cur, curT, nxt, nxtT = nxt, nxtT, cur, curT

    def rwkv(self, l):
        fw = self.fw
        T = self.T
        NW = 1824
        fw.begin_stage()
        self.load_consts()
        c = self.c
        ident64 = c[0:64, C_ID:C_ID + 64]
        ones64 = c[0:64, C_ONE:C_ONE + 64]
        m3 = lambda col: bcv(View(c, c.ap[0:64, col:col + 64].unsqueeze(1)), [64, 8, 64])
        mu = fw.sb("r_mu", [64, NW], F32)
        fw.dma(mu[:, 0:1792], rowbc(self.mu_rwkv, l, 0, 1792, 64))
        if l > 0:
            fw.dma(mu[:, 1792:1824], rowbc(self.mu_vres, 0, 0, 32, 64))
        else:
            fw.memset(mu[:, 1792:1824], 0.0)
        wbw = fw.sb("r_wbw", [65, 512], F32)
        fw.dma(wbw[0:64, :], self.rwkv_w_lora_b[l * 64:(l + 1) * 64, :])
        fw.dma(wbw[64:65, :], self.rwkv_w0[l:l + 1, :])
        wba = fw.sb("r_wba", [65, 512], F32)
        fw.dma(wba[0:64, :], self.rwkv_a_lora_b[l * 64:(l + 1) * 64, :])
        fw.dma(wba[64:65, :], self.rwkv_a0[l:l + 1, :])
        wbg = fw.sb("r_wbg", [128, 512], F32)
        fw.dma(wbg.v(), self.rwkv_g_lora_b[l * 128:(l + 1) * 128, :])
        wbv = fw.sb("r_wbv", [33, 512], F32)
        if l > 0:
            fw.dma(wbv[0:32, :], self.rwkv_v_lora_b[0:32, :])
            fw.dma(wbv[32:33, :], self.rwkv_v0[0:1, :])
        kkb = fw.sb("r_kkb", [64, 512], F32)
        fw.dma(kkb.v(), rowbc(self.rwkv_k_k, l, 0, 512, 64))
        kab = fw.sb("r_kab", [64, 512], F32)
        fw.dma(kab.v(), rowbc(self.rwkv_k_a, l, 0, 512, 64))
        omka = fw.sb("r_omka", [64, 512], F32)
        fw.ts(omka.v(), kab.v(), -1.0, ALU.mult, 1.0, ALU.add)
        rkb = fw.sb("r_rkb", [64, 512], F32)
        fw.dma(rkb.v(), rowbc(self.rwkv_r_k, l, 0, 512, 64))
        lnw = fw.sb("r_lnw", [64, 512], F32)
        fw.dma(lnw.v(), rowbc(self.rwkv_ln_w, l, 0, 512, 64))
        lnb = fw.sb("r_lnb", [64, 512], F32)
        fw.dma(lnb.v(), rowbc(self.rwkv_ln_b, l, 0, 512, 64))
        H = fw.sb("r_H", [64, 8, 64], F32R)
        fw.ts(H.v().re("p h n -> p (h n)"), c[0:64, 0:512], 0.0, ALU.mult)
        mk2 = lambda nm, shp, dt=F32: [fw.sb("%s%d" % (nm, i), shp, dt) for i in range(2)]
        mk1 = lambda nm, shp, dt=F32: [fw.sb("%s%d" % (nm, 0), shp, dt)] * 2
        R32 = ("vp", "Bb", "Kb", "X", "U")
        pcs, pps, prws = mk1("r_pc", [64, NW]), mk1("r_pp", [64, NW]), mk2("r_prw", [64, NW])
        LTs = mk1("r_LT", [128, 4, 64])
        fw.memset(LTs[0][64:65, 0:2, :], 1.0)
        fw.memset(LTs[0][32:33, 3, :], 1.0)
        dbl = ["vp", "Bb", "Kb", "kp", "g"]
        sgl = ["lw", "a", "kk", "b", "cws", "Ep", "Em", "Ex", "EL", "At", "Bt", "Kt", "Rt", "t1", "t2", "t3", "t4", "X", "U", "yo"]
        S = {nm: mk2("r_" + nm, [64, 512], F32R if nm in R32 else F32) for nm in dbl}
        S.update({nm: mk1("r_" + nm, [64, 512], F32R if nm in R32 else F32) for nm in sgl})
        M = {nm: mk2("r_" + nm, [64, 8, 64], F32R) for nm in ["AtT", "BtT", "KtT", "RtT"]}
        M.update({nm: mk1("r_" + nm, [64, 8, 64], F32R) for nm in ["X1", "XT1", "X2", "XT2", "TT", "LrbT", "AakT", "LrkT"]})
        small = {nm: mk1("r_" + nm, [64, 8]) for nm in ["ss", "mean", "var", "bon"]}
        small["pct"] = mk2("r_pct", [64, 8])
        vfs = mk1("r_vf", [64, 512])
        ps = [fw.ps("r_ps%d" % i, [128, 512], F32) for i in range(8)]
        P8 = lambda i: View(ps[i], ps[i].ap[0:64, :].rearrange("p (h n) -> p h n", h=8))
        h3 = lambda v: v.re("p (h n) -> p h n", h=8)
        col8 = lambda t: bcv(View(t.tile, t.ap.unsqueeze(2)), [64, 8, 64])
        nch = T // 64

        def phaseA(ch):
            i = ch % 2
            t0 = ch * 64
            pc, pp, prw, LT = pcs[i], pps[i], prws[i], LTs[i]
            s = {k_: v_[i] for k_, v_ in S.items()}
            mm_ = {k_: v_[i] for k_, v_ in M.items()}
            sm = {k_: v_[i] for k_, v_ in small.items()}
            fw.dma(pc[:, 0:1792], self.P[t0:t0 + 64, 0:1792])
            if l > 0:
                fw.dma(pc[:, 1792:1824], self.P[t0:t0 + 64, NIN:NP])
            elif ch == 0:
                fw.memset(pc[:, 1792:1824], 0.0)
            if ch == 0:
                fw.memset(pp.v(), 0.0)
                fw.dma(pp[1:64, 0:1792], self.P[0:63, 0:1792])
                if l > 0:
                    fw.dma(pp[1:64, 1792:1824], self.P[0:63, NIN:NP])
            else:
                fw.dma(pp[:, 0:1792], self.P[t0 - 1:t0 + 63, 0:1792])
                if l > 0:
                    fw.dma(pp[:, 1792:1824], self.P[t0 - 1:t0 + 63, NIN:NP])
            yield
            fw.tt(pp.v(), pp.v(), pc.v(), ALU.subtract)
            yield
            fw.tt(pp.v(), pp.v(), mu.v(), ALU.mult)
            yield
            fw.tt(prw.v(), pc.v(), pp.v(), ALU.add)
            yield
            r, k_, v = prw[:, 0:512], prw[:, 512:1024], prw[:, 1024:1536]
            fw.act(prw[:, 1536:1600], prw[:, 1536:1600], AF.Tanh)
            fw.act(prw[:, 1664:1792], prw[:, 1664:1792], AF.Sigmoid)
            pl = View(ps[0], ps[0].ap[:, 0:256].rearrange("p (a t) -> p a t", a=4))
            fw.transpose(pl[0:64, 0, :], prw[:, 1536:1600], ident64)
            fw.transpose(pl[0:64, 1, :], prw[:, 1600:1664], ident64)
            fw.transpose(pl[0:128, 2, :], prw[:, 1664:1792], ident64)
            if l > 0:
                fw.transpose(pl[0:32, 3, :], prw[:, 1792:1824], ident64)
            yield
            fw.copy(LT[0:64, 0:2, :], pl[0:64, 0:2, :], eng="dve")
            fw.copy(LT[0:128, 2, :], pl[0:128, 2, :], eng="act")
            if l > 0:
                fw.copy(LT[0:32, 3, :], pl[0:32, 3, :], eng="dve")
            yield
            fw.mm(ps[1][0:64, :], LT[0:65, 0, :], wbw.v())
            fw.mm(ps[2][0:64, :], LT[0:65, 1, :], wba.v())
            yield
            fw.act(s["lw"].v(), ps[1][0:64, :], AF.Sigmoid)
            fw.act(s["a"].v(), ps[2][0:64, :], AF.Sigmoid)
            fw.mm(ps[1][0:64, :], LT[0:128, 2, :], wbg.v())
            if l > 0:
                fw.mm(ps[2][0:64, :], LT[0:33, 3, :], wbv.v())
            yield
            fw.ts(s["lw"].v(), s["lw"].v(), -math.exp(-0.5), ALU.mult)
            yield
            fw.copy(s["g"].v(), ps[1][0:64, :], eng="dve")
            yield
            if l > 0:
                vf = vfs[i]
                fw.dma(vf.v(), self.VF[t0:t0 + 64, :])
                fw.act(s["t1"].v(), ps[2][0:64, :], AF.Sigmoid)
                fw.tt(vf.v(), vf.v(), v, ALU.subtract)
                yield
                fw.tt(vf.v(), vf.v(), s["t1"].v(), ALU.mult)
                yield
                fw.tt(s["vp"].v(), v, vf.v(), ALU.add)
            else:
                fw.copy(s["vp"].v(), v, eng="dve")
                fw.dma(self.VF[t0:t0 + 64, :], v)
            yield
            fw.tt(s["kk"].v(), k_, kkb.v(), ALU.mult)
            yield
            fw.tt(s["t1"].v(), s["kk"].v(), s["kk"].v(), ALU.mult)
            yield
            fw.reduce(sm["ss"].v(), h3(s["t1"].v()))
            fw.act(sm["ss"].v(), sm["ss"].v(), AF.Sqrt, bias=EPS)
            fw.recip(sm["ss"].v(), sm["ss"].v())
            yield
            fw.tt(h3(s["kk"].v()), h3(s["kk"].v()), col8(sm["ss"].v()), ALU.mult)
            yield
            fw.tt(s["t1"].v(), s["a"].v(), kab.v(), ALU.mult)
            yield
            fw.tt(s["t1"].v(), s["t1"].v(), omka.v(), ALU.add)
            yield
            fw.tt(s["kp"].v(), k_, s["t1"].v(), ALU.mult)
            yield
            fw.tt(s["b"].v(), s["kk"].v(), s["a"].v(), ALU.mult)
            yield
            fw.mm(ps[1][0:64, :], c[0:64, C_LE:C_LE + 64], s["lw"].v())
            fw.mm(ps[2][0:64, :], ones64, s["lw"].v())
            pct_ps = View(ps[0], ps[0].ap[0:64, 256:264])
            for h in range(8):
                fw.mm(pct_ps[:, h:h + 1], s["lw"][:, h * 64:(h + 1) * 64], c[0:64, C_ONE:C_ONE + 1])
            yield
            fw.act(sm["pct"].v(), pct_ps, AF.Exp)
            fw.copy(s["cws"].v(), ps[1][0:64, :], eng="dve")
            fw.act(s["Ep"].v(), ps[1][0:64, :], AF.Exp)
            fw.act(s["Em"].v(), ps[1][0:64, :], AF.Exp, scale=-1.0)
            yield
            fw.tt(s["t1"].v(), s["cws"].v(), s["lw"].v(), ALU.subtract)
            fw.act(s["Ex"].v(), s["t1"].v(), AF.Exp)
            yield
            fw.tt(s["t2"].v(), ps[2][0:64, :], s["cws"].v(), ALU.subtract)
            fw.act(s["EL"].v(), s["t2"].v(), AF.Exp)
            yield
            fw.tt(s["Bt"].v(), s["b"].v(), s["Em"].v(), ALU.mult)
            yield
            fw.tt(s["Kt"].v(), s["kp"].v(), s["Em"].v(), ALU.mult)
            yield
            fw.tt(s["Rt"].v(), r, s["Ep"].v(), ALU.mult)
            yield
            fw.stt(s["At"].v(), s["kk"].v(), -1.0, s["Ex"].v(), ALU.mult, ALU.mult)
            yield
            fw.tt(s["Bb"].v(), s["b"].v(), s["EL"].v(), ALU.mult)
            yield
            fw.tt(s["Kb"].v(), s["kp"].v(), s["EL"].v(), ALU.mult)
            yield
            for j, (src_, dst_) in enumerate([("Bt", "BtT"), ("Kt", "KtT"), ("Rt", "RtT"), ("At", "AtT")]):
                pt = P8((1 + j % 2) if not os.environ.get('TRB') else (4 + j))
                for h in range(8):
                    fw.transpose(pt[:, h, :], s[src_][:, h * 64:(h + 1) * 64], ident64)
                yield
                fw.copy(mm_[dst_].v(), pt, eng="act" if j % 2 else "dve")
                yield

        def phaseBC(ch):
            i = ch % 2
            t0 = ch * 64
            prw = prws[i]
            s = {k_: v_[i] for k_, v_ in S.items()}
            mm_ = {k_: v_[i] for k_, v_ in M.items()}
            sm = {k_: v_[i] for k_, v_ in small.items()}
            r = prw[:, 0:512]
            vp = s["vp"]
            AtT, BtT, KtT, RtT = mm_["AtT"], mm_["BtT"], mm_["KtT"], mm_["RtT"]
            for h in range(8):
                fw.mm(P8(3)[:, h, :], BtT[:, h, :], AtT[:, h, :])
            for h in range(8):
                fw.mm(P8(4)[:, h, :], AtT[:, h, :], BtT[:, h, :])
            yield
            fw.tt(mm_["XT1"].v(), P8(3), m3(C_LT), ALU.mult)
            fw.tt(mm_["X1"].v(), P8(4), m3(C_GT), ALU.mult)
            for h in range(8):
                fw.mm(P8(5)[:, h, :], BtT[:, h, :], RtT[:, h, :])
            for h in range(8):
                fw.mm(P8(6)[:, h, :], KtT[:, h, :], AtT[:, h, :])
            for h in range(8):
                fw.mm(P8(7)[:, h, :], KtT[:, h, :], RtT[:, h, :])
            yield
            fw.tt(mm_["TT"].v(), mm_["XT1"].v(), m3(C_ID), ALU.add)
            yield
            fw.tt(mm_["LrbT"].v(), P8(5), m3(C_LE), ALU.mult)
            yield
            fw.tt(mm_["AakT"].v(), P8(6), m3(C_LT), ALU.mult)
            yield
            fw.tt(mm_["LrkT"].v(), P8(7), m3(C_LE), ALU.mult)
            yield
            yield from self.inv_chain(8, mm_["X1"], mm_["XT1"], mm_["TT"], mm_["X2"], mm_["XT2"], P8(3), P8(4), P8(5), gen=True)
            TT = mm_["TT"]
            for h in range(8):
                hs_ = slice(h * 64, (h + 1) * 64)
                fw.mm(P8(6)[:, h, :], AtT[:, h, :], H[:, h, :], start=True, stop=False)
                fw.mm(P8(6)[:, h, :], mm_["AakT"][:, h, :], vp[:, hs_], start=False, stop=True)
            yield
            fw.copy(s["X"].v(), ps[6][0:64, :], eng="dve")
            yield
            for h in range(8):
                hs_ = slice(h * 64, (h + 1) * 64)
                fw.mm(P8(7)[:, h, :], TT[:, h, :], s["X"][:, hs_])
            yield
            fw.copy(s["U"].v(), ps[7][0:64, :], eng="act")
            yield
            for h in range(8):
                hs_ = slice(h * 64, (h + 1) * 64)
                fw.mm(P8(4)[:, h, :], s["Bb"][:, hs_], s["U"][:, hs_], start=True, stop=False)
                fw.mm(P8(4)[:, h, :], s["Kb"][:, hs_], vp[:, hs_], start=False, stop=True)
            for h in range(8):
                hs_ = slice(h * 64, (h + 1) * 64)
                fw.mm(P8(3)[:, h, :], RtT[:, h, :], H[:, h, :], start=True, stop=False)
                fw.mm(P8(3)[:, h, :], mm_["LrbT"][:, h, :], s["U"][:, hs_], start=False, stop=False)
                fw.mm(P8(3)[:, h, :], mm_["LrkT"][:, h, :], vp[:, hs_], start=False, stop=True)
            yield
            fw.tt(H.v(), H.v(), col8(sm["pct"].v()), ALU.mult)
            yield
            fw.tt(H.v(), H.v(), P8(4), ALU.add)
            yield
            yo = s["yo"]
            fw.reduce(sm["mean"].v(), P8(3))
            fw.ts(sm["mean"].v(), sm["mean"].v(), 1.0 / 64, ALU.mult)
            yield
            fw.tt(h3(yo.v()), P8(3), col8(sm["mean"].v()), ALU.subtract)
            yield
            fw.tt(s["t3"].v(), yo.v(), yo.v(), ALU.mult)
            yield
            fw.reduce(sm["var"].v(), h3(s["t3"].v()))
            fw.act(sm["var"].v(), sm["var"].v(), AF.Sqrt, bias=64e-5, scale=1.0 / 64)
            fw.recip(sm["var"].v(), sm["var"].v())
            yield
            fw.tt(h3(yo.v()), h3(yo.v()), col8(sm["var"].v()), ALU.mult)
            yield
            fw.tt(yo.v(), yo.v(), lnw.v(), ALU.mult)
            yield
            fw.tt(yo.v(), yo.v(), lnb.v(), ALU.add)
            yield
            fw.tt(s["t3"].v(), r, s["kp"].v(), ALU.mult)
            yield
            fw.tt(s["t3"].v(), s["t3"].v(), rkb.v(), ALU.mult)
            yield
            fw.reduce(sm["bon"].v(), h3(s["t3"].v()))
            yield
            fw.tt(h3(s["t4"].v()), h3(vp.v()), col8(sm["bon"].v()), ALU.mult)
            yield
            fw.tt(yo.v(), yo.v(), s["t4"].v(), ALU.add)
            yield
            fw.tt(yo.v(), yo.v(), s["g"].v(), ALU.mult)
            fw.dma(self.MIX[t0:t0 + 64, 0:512].k(("r", ch)), yo.v())
            yield

        for _ in phaseA(0):
            pass
        for ch in range(nch):
            gens = [phaseBC(ch)]
            if ch + 1 < nch:
                gens.append(phaseA(ch + 1))
            if os.environ.get('SEQ'):
                for g_ in gens:
                    for _ in g_:
                        pass
            else:
                run_interleaved(gens)
        fw.end_stage()

    def gdn(self, l):
        fw = self.fw
        T = self.T
        fw.begin_stage()
        self.load_consts()
        c = self.c
        ident64 = c[0:64, C_ID:C_ID + 64]
        m4 = lambda col: bcv(View(c, c.ap[0:64, col:col + 64].unsqueeze(1)), [64, 4, 64])
        cw = fw.sb("g_cw", [64, 4, 1536], F32)
        for i in range(4):
            fw.dma(cw[:, i, :], rowbc(self.gdn_conv_w, l * 4 + i, 0, 1536, 64))
        nA = fw.sb("g_nA", [64, 4], F32)
        fw.dma(nA.v(), rowbc(self.gdn_A_log, l, 0, 4, 64))
        fw.act(nA.v(), nA.v(), AF.Exp)
        fw.ts(nA.v(), nA.v(), -1.0, ALU.mult)
        dtb = fw.sb("g_dtb", [64, 4], F32)
        fw.dma(dtb.v(), rowbc(self.gdn_dt_bias, l, 0, 4, 64))
        nw = fw.sb("g_nw", [64, 128], F32)
        fw.dma(nw.v(), rowbc(self.gdn_norm_w, l, 0, 128, 64))
        S = fw.sb("g_S", [128, 4, 128], F32R)
        fw.ts(S.v().re("p h n -> p (h n)"), c[:, 0:512], 0.0, ALU.mult)
        xs = [fw.sb("g_x%d" % i, [64, 1536], F32) for i in range(4)]
        zab = [fw.sb("g_zab%d" % b, [64, 520], F32) for b in range(2)]
        R32s = ("vb", "kbg", "kdec", "vn")
        dbl_s = ("vb", "kbg", "kdec")
        names = ["kb", "vb", "kbg", "kdec", "u", "vn", "o", "t1"]
        s2 = {nm: [fw.sb("g_%s%d" % (nm, i), [64, 512], F32R if nm in R32s else F32) for i in range(2 if nm in dbl_s else 1)] for nm in names}
        qkv = fw.sb("g_qkv", [64, 1536], F32)
        sq = fw.sb("g_sq", [64, 1024], F32)
        sm = {nm: fw.sb("g_" + nm, [64, 8], F32) for nm in ["ss", "g", "beta", "gc", "eg", "egl", "sp", "oss"]}
        egl128s = [fw.sb("g_egl128_%d" % i, [128, 4], F32) for i in range(2)]
        R32m = ("X1", "XT1", "X2", "XT2", "TT", "attnT")
        dbl_m = ("Egt", "ETlt", "ETle")
        M2 = {nm: [fw.sb("g_%s%d" % (nm, i), [64, 4, 64], F32R if nm in R32m else F32) for i in range(2 if nm in dbl_m else 1)]
              for nm in ["dg", "tmp", "E", "ET", "Egt", "ETlt", "ETle", "X1", "XT1", "X2", "XT2", "TT", "attnT"]}
        dbl_f = ("KT", "QT", "KBT", "QgT")
        F2 = {nm: [fw.sb("g_%s%d" % (nm, i), [128, 4, 64], F32 if nm == "EGR" else F32R) for i in range(2 if nm in dbl_f else 1)]
              for nm in ["KT", "QT", "KBT", "QgT", "wT", "EGR"]}
        ps = [fw.ps("g_ps%d" % i, [128, 512], F32) for i in range(8)]
        P4 = lambda i: View(ps[i], ps[i].ap[0:64, 0:256].rearrange("p (h n) -> p h n", h=4))
        F4 = lambda i, hf=0: View(ps[i], ps[i].ap[:, hf * 256:(hf + 1) * 256].rearrange("p (h n) -> p h n", h=4))
        T4 = lambda i: View(ps[i], ps[i].ap[0:64, :].rearrange("p (h n) -> p h n", h=4))
        h4 = lambda v: v.re("p (h n) -> p h n", h=4)
        col = lambda t, n: bcv(View(t.tile, t.ap.unsqueeze(2)), [64, t.ap.shape[1], n])
        sel = lambda d, i: {k_: v_[i % len(v_)] for k_, v_ in d.items()}
        nch = T // 64

        def phaseA(ch):
            b = ch % 2
            t0 = ch * 64
            X = xs
            s = sel(s2, b)
            M = sel(M2, b)
            F = sel(F2, b)
            egl128 = egl128s[b]
            for i in range(4):
                sh = 3 - i
                if t0 - sh < 0:
                    fw.memset(X[i].v(), 0.0)
                    fw.dma(X[i][sh:64, :], self.P[0:64 - sh, GDN0:GDN0 + 1536])
                else:
                    fw.dma(X[i].v(), self.P[t0 - sh:t0 - sh + 64, GDN0:GDN0 + 1536])
            fw.dma(zab[b].v(), self.P[t0:t0 + 64, GDN0 + 1536:GDN0 + 2056])
            a_ = zab[b][:, 512:516]
            b_ = zab[b][:, 516:520]
            yield
            acc = qkv
            fw.tt(acc.v(), X[0].v(), cw[:, 0, :], ALU.mult)
            yield
            for i in range(1, 4):
                fw.tt(X[i].v(), X[i].v(), cw[:, i, :], ALU.mult)
                yield
                fw.tt(acc.v(), acc.v(), X[i].v(), ALU.add)
                yield
            fw.act(qkv.v(), acc.v(), AF.Silu)
            yield
            fw.tt(sq.v(), qkv[:, 0:1024], qkv[:, 0:1024], ALU.mult)
            yield
            fw.reduce(sm["ss"].v(), sq.v().re("p (h n) -> p h n", h=8))
            fw.act(sm["ss"].v(), sm["ss"].v(), AF.Sqrt, bias=EPS)
            fw.recip(sm["ss"].v(), sm["ss"].v())
            fw.ts(sm["ss"][:, 0:4], sm["ss"][:, 0:4], 128.0 ** -0.5, ALU.mult)
            yield
            qk3 = qkv[:, 0:1024].re("p (h n) -> p h n", h=8)
            fw.tt(qk3, qk3, col(sm["ss"].v(), 128), ALU.mult)
            yield
            qn, kn, vv = qkv[:, 0:512], qkv[:, 512:1024], qkv[:, 1024:1536]
            fw.act(sm["beta"][:, 0:4], b_, AF.Sigmoid)
            fw.tt(sm["sp"][:, 0:4], a_, dtb.v(), ALU.add)
            fw.act(sm["sp"][:, 0:4], sm["sp"][:, 0:4], AF.Exp)
            fw.act(sm["sp"][:, 0:4], sm["sp"][:, 0:4], AF.Ln, bias=1.0)
            fw.tt(sm["g"][:, 0:4], sm["sp"][:, 0:4], nA.v(), ALU.mult)
            yield
            g4 = sm["g"][:, 0:4]
            beta = sm["beta"][:, 0:4]
            gcp = View(ps[0], ps[0].ap[0:64, 256:260])
            glp = View(ps[0], ps[0].ap[:, 260:264])
            fw.mm(gcp, c[0:64, C_LE:C_LE + 64], g4)
            fw.mm(glp, c[0:64, C_ONE:C_ONE + 128], g4)
            yield
            gc = sm["gc"][:, 0:4]
            fw.copy(gc, gcp, eng="dve")
            fw.act(sm["eg"][:, 0:4], gcp, AF.Exp)
            fw.tt(sm["egl"][:, 0:4], glp[0:64, :], gc, ALU.subtract)
            fw.act(sm["egl"][:, 0:4], sm["egl"][:, 0:4], AF.Exp)
            fw.act(egl128.v(), glp, AF.Exp)
            yield
            for h in range(4):
                fw.ts(M["dg"][:, h, :], ident64, sm["gc"][:, h:h + 1], ALU.mult)
            yield
            fw.mm(ps[1][:, 0:256], c[0:64, C_ONE:C_ONE + 128], M["dg"].v().re("p h n -> p (h n)"))
            fw.ts(M["tmp"].v(), col(gc, 64), -1.0, ALU.mult)
            yield
            fw.act(F["EGR"].v(), F4(1), AF.Exp)
            tps = View(ps[1], ps[1].ap[0:64, 256:512])
            fw.mm(tps, c[0:64, C_ONE:C_ONE + 64], M["dg"].v().re("p h n -> p (h n)"), start=True, stop=False)
            fw.mm(tps, ident64, M["tmp"].v().re("p h n -> p (h n)"), start=False, stop=True)
            tps3 = tps.re("p (h n) -> p h n", h=4)
            yield
            fw.ts(M["E"].v(), tps3, 0.0, ALU.max)
            fw.act(M["E"].v(), M["E"].v(), AF.Exp, scale=-1.0)
            yield
            fw.ts(M["ET"].v(), tps3, 0.0, ALU.min)
            fw.act(M["ET"].v(), M["ET"].v(), AF.Exp)
            yield
            fw.tt(M["Egt"].v(), M["E"].v(), m4(C_NGT), ALU.mult)
            yield
            fw.tt(M["ETlt"].v(), M["ET"].v(), m4(C_NLT), ALU.mult)
            yield
            fw.tt(M["ETle"].v(), M["ET"].v(), m4(C_LE), ALU.mult)
            yield
            fw.tt(h4(s["kb"].v()), h4(kn), col(beta, 128), ALU.mult)
            yield
            fw.tt(h4(s["vb"].v()), h4(vv), col(beta, 128), ALU.mult)
            yield
            fw.tt(h4(s["kbg"].v()), h4(s["kb"].v()), col(sm["eg"][:, 0:4], 128), ALU.mult)
            yield
            fw.tt(h4(s["kdec"].v()), h4(kn), col(sm["egl"][:, 0:4], 128), ALU.mult)
            yield
            for h in range(4):
                hs_ = slice(h * 128, (h + 1) * 128)
                fw.transpose(F4(2, 0)[:, h, :], kn[:, hs_], ident64)
                fw.transpose(F4(2, 1)[:, h, :], qn[:, hs_], ident64)
            yield
            fw.copy(F["KT"].v(), F4(2, 0), eng="dve")
            fw.copy(F["QT"].v(), F4(2, 1), eng="act")
            yield
            fw.tt(F["QgT"].v(), F4(2, 1), F["EGR"].v(), ALU.mult)
            for h in range(4):
                hs_ = slice(h * 128, (h + 1) * 128)
                fw.transpose(F4(0, 0)[:, h, :], s["kb"][:, hs_], ident64)
            yield
            fw.copy(F["KBT"].v(), F4(0, 0), eng="act")
            yield

        def phaseBC(ch):
            b = ch % 2
            t0 = ch * 64
            s = sel(s2, b)
            M = sel(M2, b)
            F = sel(F2, b)
            egl128 = egl128s[b]
            z = zab[b][:, 0:512]
            for h in range(4):
                fw.mm(P4(3)[:, h, :], F["KBT"][:, h, :], F["KT"][:, h, :])
                fw.mm(P4(4)[:, h, :], F["KT"][:, h, :], F["KBT"][:, h, :])
                fw.mm(P4(5)[:, h, :], F["KT"][:, h, :], F["QT"][:, h, :])
            yield
            fw.tt(M["X1"].v(), P4(3), M["Egt"].v(), ALU.mult)
            yield
            fw.tt(M["XT1"].v(), P4(4), M["ETlt"].v(), ALU.mult)
            yield
            fw.tt(M["attnT"].v(), P4(5), M["ETle"].v(), ALU.mult)
            yield
            fw.tt(M["TT"].v(), M["XT1"].v(), m4(C_ID), ALU.add)
            yield
            yield from self.inv_chain(4, M["X1"], M["XT1"], M["TT"], M["X2"], M["XT2"], P4(3), P4(4), P4(5), gen=True)
            TT = M["TT"]
            for h in range(4):
                hs_ = slice(h * 128, (h + 1) * 128)
                fw.mm(T4(6)[:, h, :], TT[:, h, :], s["vb"][:, hs_])
                fw.mm(F4(7)[:, h, :], s["kbg"][:, hs_], TT[:, h, :])
            yield
            fw.copy(s["u"].v(), ps[6][0:64, :], eng="dve")
            fw.copy(F["wT"].v(), F4(7), eng="act")
            yield
            for h in range(4):
                fw.mm(T4(3)[:, h, :], F["wT"][:, h, :], S[:, h, :])
            yield
            fw.tt(s["vn"].v(), s["u"].v(), ps[3][0:64, :], ALU.subtract)
            yield
            for h in range(4):
                hs_ = slice(h * 128, (h + 1) * 128)
                fw.mm(ps[5][:, hs_], s["kdec"][:, hs_], s["vn"][:, hs_])
            for h in range(4):
                hs_ = slice(h * 128, (h + 1) * 128)
                fw.mm(T4(4)[:, h, :], F["QgT"][:, h, :], S[:, h, :], start=True, stop=False)
                fw.mm(T4(4)[:, h, :], M["attnT"][:, h, :], s["vn"][:, hs_], start=False, stop=True)
            yield
            for h in range(4):
                hs_ = slice(h * 128, (h + 1) * 128)
                fw.stt(S[:, h, :], S[:, h, :], egl128[:, h:h + 1], ps[5][:, hs_], ALU.mult, ALU.add)
                yield
            o = s["o"]
            fw.copy(o.v(), ps[4][0:64, :], eng="act")
            yield
            fw.tt(s["t1"].v(), o.v(), o.v(), ALU.mult)
            yield
            fw.reduce(sm["oss"][:, 0:4], h4(s["t1"].v()))
            fw.act(sm["oss"][:, 0:4], sm["oss"][:, 0:4], AF.Sqrt, bias=EPS, scale=1.0 / 128)
            fw.recip(sm["oss"][:, 0:4], sm["oss"][:, 0:4])
            yield
            fw.tt(h4(o.v()), h4(o.v()), col(sm["oss"][:, 0:4], 128), ALU.mult)
            yield
            fw.tt(h4(o.v()), h4(o.v()), bcv(View(nw, nw.ap.unsqueeze(1)), [64, 4, 128]), ALU.mult)
            fw.act(s["t1"].v(), z, AF.Silu)
            yield
            fw.tt(o.v(), o.v(), s["t1"].v(), ALU.mult)
            fw.dma(self.MIX[t0:t0 + 64, 512:1024].k(("g", ch)), o.v())
            yield

        for _ in phaseA(0):
            pass
        for ch in range(nch):
            gens = [phaseBC(ch)]
            if ch + 1 < nch:
                gens.append(phaseA(ch + 1))
            iln = int(os.environ.get('GDN_ILN', '0'))
            if len(gens) == 2 and iln > 0:
                gb, ga = gens
                na = 0
                a_alive = b_alive = True
                while a_alive and na < iln:
                    if b_alive:
                        try:
                            next(gb)
                        except StopIteration:
                            b_alive = False
                    try:
                        next(ga)
                        na += 1
                    except StopIteration:
                        a_alive = False
                for _ in gb:
                    pass
                for _ in ga:
                    pass
            else:
                for g_ in gens:
                    for _ in g_:
                        pass
        fw.end_stage()

    def attn(self, l):
        fw = self.fw
        T = self.T
        NT = T // 128
        QG = min(T, 512)
        NQB = QG // 128
        lam_init = 0.8 - 0.6 * math.exp(-0.3 * l)
        fw.begin_stage()
        self.load_consts(bf=True)
        c = self.c
        cm = fw.sb("a_cm", [128, 4, 512], BF16)
        fw.dma(cm.v().re("p r n -> p (r n)"), self.cmask.v(), q="pool")
        onesb = fw.sb("a_ones", [128, 1], BF16)
        fw.memset(onesb.v(), 1.0)
        qnw = fw.sb("a_qnw", [128, 64], F32)
        fw.dma(qnw.v(), rowbc(self.diff_q_norm_w, l, 0, 64, 128))
        fw.ts(qnw.v(), qnw.v(), 0.125, ALU.mult)
        knw = fw.sb("a_knw", [128, 64], F32)
        fw.dma(knw.v(), rowbc(self.diff_k_norm_w, l, 0, 64, 128))
        subw = fw.sb("a_subw", [128, 128], F32)
        fw.dma(subw.v(), rowbc(self.diff_subln_w, l, 0, 128, 128))
        fw.ts(subw.v(), subw.v(), 1.0 - lam_init, ALU.mult)
        lv = fw.sb("a_lv", [128, 4, 64], F32)
        for i, dt_ in enumerate([self.diff_lambda_q1, self.diff_lambda_k1, self.diff_lambda_q2, self.diff_lambda_k2]):
            fw.dma(lv[:, i, :], rowbc(dt_, l, 0, 64, 128))
        ls = fw.sb("a_ls", [128, 4], F32)
        fw.tt(lv[:, 0, :], lv[:, 0, :], lv[:, 1, :], ALU.mult)
        fw.tt(lv[:, 2, :], lv[:, 2, :], lv[:, 3, :], ALU.mult)
        fw.reduce(ls[:, 0:1], lv[:, 0, :])
        fw.reduce(ls[:, 1:2], lv[:, 2, :])
        fw.act(ls[:, 0:2], ls[:, 0:2], AF.Exp)
        fw.tt(ls[:, 2:3], ls[:, 1:2], ls[:, 0:1], ALU.subtract)
        fw.ts(ls[:, 3:4], ls[:, 2:3], -lam_init, ALU.add)
        nlam = ls[:, 3:4]
        qT = fw.sb("a_qT", [64, 16, T], BF16)
        kT = fw.sb("a_kT", [64, 16, T], BF16)
        vx = fw.sb("a_vx", [128, NT, 8, 132], BF16)
        fw.memset(vx.v(), 1.0)
        xin = [fw.sb("a_xin%d" % i, [128, 1024], F32) for i in range(2)]
        xsq = fw.sb("a_xsq", [128, 1024], F32)
        xn = [fw.sb("a_xn%d" % i, [128, 1024], BF16) for i in range(2)]
        ss = fw.sb("a_ss", [128, 16], F32)
        ps = [fw.ps("a_ps%d" % i, [128, 512], F32) for i in range(8)]
        col16 = lambda t: bcv(View(t.tile, t.ap.unsqueeze(2)), [128, 16, 64])
        nrm = 0
        for t in range(NT):
            rows = slice(t * 128, (t + 1) * 128)
            for which, (c0, wt_, dstT) in enumerate([(DIFF0, qnw, qT), (DIFF0 + 1024, knw, kT)]):
                xi = xin[nrm % 2]
                xo = xn[nrm % 2]
                nrm += 1
                fw.dma(xi.v(), self.P[rows, c0:c0 + 1024])
                fw.tt(xsq.v(), xi.v(), xi.v(), ALU.mult)
                fw.reduce(ss.v(), xsq.v().re("p (g n) -> p g n", g=16))
                fw.act(ss.v(), ss.v(), AF.Sqrt, bias=EPS, scale=1.0 / 64)
                fw.recip(ss.v(), ss.v())
                x3 = xi.v().re("p (g n) -> p g n", g=16)
                fw.tt(x3, x3, col16(ss.v()), ALU.mult)
                fw.tt(xo.v().re("p (g n) -> p g n", g=16), x3, bcv(View(wt_, wt_.ap.unsqueeze(1)), [128, 16, 64]), ALU.mult)
                for half in range(2):
                    pt = View(ps[half], ps[half].ap[0:64, :].bitcast(BF16)[:, 0:1024].rearrange("p (g n) -> p g n", g=8))
                    for g in range(8):
                        gg = half * 8 + g
                        fw.transpose(pt[:, g, :], xo[:, gg * 64:(gg + 1) * 64], self.idb.v())
                    fw.copy(dstT[:, half * 8:(half + 1) * 8, t * 128:(t + 1) * 128], pt, eng=self.ev())
            xi = xin[nrm % 2]
            nrm += 1
            fw.dma(xi.v(), self.P[rows, DIFF0 + 2048:DIFF0 + 3072])
            fw.copy(vx[:, t, :, 0:128], xi.v().re("p (h e) -> p h e", h=8), eng="act")
        pex = [fw.sb("a_pex%d" % i, [128, 512], BF16) for i in range(4)]
        NBK = max(1, NQB // 2)
        osb = [fw.sb("a_osb%d" % i, [128, 2, NBK, 264], F32) for i in range(2)]
        eo = [fw.sb("a_eo%d" % i, [128, NQB, 128], F32) for i in range(2)]
        et = fw.sb("a_et", [128, NQB, 128], F32)
        esm = fw.sb("a_esm", [128, 4, NQB], F32)
        npx = 0
        nst = 0
        nq = 0
        for h in range(8):
            for qg in range(T // QG):
                nkb = (qg + 1) * NQB
                ob_sb = osb[nq % 2]
                o_ = eo[nq % 2]
                nq += 1
                items = [(m, kb) for m in range(2) for kb in range(nkb)]
                sts = {}

                def emit_st(i):
                    nonlocal nst
                    m, kb = items[i]
                    hm = h * 2 + m
                    st = ps[(0, 1, 6, 7)[nst % 4]]
                    nst += 1
                    r = kb - qg * NQB
                    fw.mm(st[:, 0:QG], kT[:, hm, kb * 128:(kb + 1) * 128], qT[:, hm, qg * QG:(qg + 1) * QG], start=True, stop=(r < 0))
                    if r >= 0:
                        fw.mm(st[:, 0:QG], self.idb.v(), cm[:, r, 0:QG], start=False, stop=True)
                    sts[i] = st

                PD = 3
                for i0 in range(min(PD, len(items))):
                    emit_st(i0)
                for i, (m, kb) in enumerate(items):
                    if i + PD < len(items):
                        emit_st(i + PD)
                    st = sts.pop(i)
                    px = pex[npx % 4]
                    npx += 1
                    fw.act(px[:, 0:QG], st[:, 0:QG], AF.Exp)
                    for qb in range(NQB):
                        if kb <= qg * NQB + qb:
                            ob_ = ps[2 + m * 2 + qb // 2]
                            oc = (qb % 2) * 132
                            fw.mm(ob_[:, oc:oc + 129], px[:, qb * 128:(qb + 1) * 128], vx[:, kb, h, 0:129],
                                  start=(kb == 0 and qb % 2 == 0), stop=(kb == qg * NQB + qb), skip_group_check=True)
                    if kb == nkb - 1:
                        for bk in range(NBK):
                            fw.copy(ob_sb[:, m, bk, :], ps[2 + m * 2 + bk][:, 0:264], eng=("dve" if bk % 2 == 0 else "act"))
                O = [ob_sb[:, m].re("p b (t c) -> p (b t) c", t=2) for m in range(2)]
                O = [o[:, 0:NQB, :] for o in O]
                rs = [o[:, :, 128:129].re("p q c -> p (q c)") for o in O]
                fw.recip(esm[:, 0, :], rs[0])
                fw.recip(esm[:, 1, :], rs[1])
                fw.ts(esm[:, 1, :], esm[:, 1, :], nlam, ALU.mult)
                bq = lambda v: bcv(View(v.tile, v.ap.unsqueeze(2)), [128, NQB, 128])
                fw.tt(o_.v(), O[0][:, :, 0:128], bq(esm[:, 0, :]), ALU.mult)
                fw.tt(et.v(), O[1][:, :, 0:128], bq(esm[:, 1, :]), ALU.mult)
                fw.tt(o_.v(), o_.v(), et.v(), ALU.add)
                fw.tt(et.v(), o_.v(), o_.v(), ALU.mult)
                fw.reduce(esm[:, 2, :], et.v())
                fw.act(esm[:, 2, :], esm[:, 2, :], AF.Ln, bias=EPS, scale=1.0 / 128)
                fw.act(esm[:, 2, :], esm[:, 2, :], AF.Exp, scale=-0.5)
                fw.tt(o_.v(), o_.v(), bq(esm[:, 2, :]), ALU.mult)
                fw.tt(o_.v(), o_.v(), bcv(View(subw, subw.ap.unsqueeze(1)), [128, NQB, 128]), ALU.mult)
                q0 = qg * QG
                dst = self.MIX[q0:q0 + QG, 1024 + h * 128:1024 + (h + 1) * 128].re("(q p) e -> p q e", p=128)
                fw.dma(dst.k(("a", h, qg)), o_.v())
        fw.end_stage()

    def build(self, stages=None):
        for l in range(self.L):
            for nm in ["in_proj", "rwkv", "gdn", "attn", "out_proj", "ffn"]:
                if stages is None or (l, nm) in stages:
                    getattr(self, nm)(l)
        self.fw.finish()
        return self.nc


PARAM_SHAPES2D = None


def make_inputs(inputs, b, T):
    L = 2
    r = lambda a, shp: np.ascontiguousarray(np.asarray(a, dtype=np.float32).reshape(shp))
    d = {}
    d["x"] = r(inputs["x"][b, :T], (T, D))
    d["attn_norm_w"] = r(inputs["attn_norm_w"], (L, D))
    d["w_in"] = r(inputs["w_in"], (L * D, NIN))
    d["w_vres_a"] = r(inputs["w_vres_a"], (D, 32))
    d["mu_rwkv"] = r(inputs["mu_rwkv"], (L, 1792))
    d["mu_vres"] = r(inputs["mu_vres"], (1, 32))
    d["rwkv_w0"] = r(inputs["rwkv_w0"], (L, 512))
    d["rwkv_w_lora_b"] = r(inputs["rwkv_w_lora_b"], (L * 64, 512))
    d["rwkv_a0"] = r(inputs["rwkv_a0"], (L, 512))
    d["rwkv_a_lora_b"] = r(inputs["rwkv_a_lora_b"], (L * 64, 512))
    d["rwkv_g_lora_b"] = r(inputs["rwkv_g_lora_b"], (L * 128, 512))
    d["rwkv_v0"] = r(inputs["rwkv_v0"], (1, 512))
    d["rwkv_v_lora_b"] = r(inputs["rwkv_v_lora_b"], (32, 512))
    for nm in ["rwkv_k_k", "rwkv_k_a", "rwkv_r_k", "rwkv_ln_w", "rwkv_ln_b"]:
        d[nm] = r(inputs[nm], (L, 512))
    d["gdn_conv_w"] = r(inputs["gdn_conv_w"], (L * 4, 1536))
    d["gdn_A_log"] = r(inputs["gdn_A_log"], (L, 4))
    d["gdn_dt_bias"] = r(inputs["gdn_dt_bias"], (L, 4))
    d["gdn_norm_w"] = r(inputs["gdn_norm_w"], (L, 128))
    for nm in ["diff_q_norm_w", "diff_k_norm_w", "diff_lambda_q1", "diff_lambda_k1", "diff_lambda_q2", "diff_lambda_k2"]:
        d[nm] = r(inputs[nm], (L, 64))
    d["diff_subln_w"] = r(inputs["diff_subln_w"], (L, 128))
    d["w_out"] = r(inputs["w_out"], (L * D, D))
    d["ffn_norm_w"] = r(inputs["ffn_norm_w"], (L, D))
    d["w_ffn_in"] = r(inputs["w_ffn_in"], (L * D, 2 * FH))
    d["w_ffn_out"] = r(inputs["w_ffn_out"], (L * FH, D))
    c, cm = make_consts()
    d["consts"] = c
    d["cmask"] = cm
    return d


_CACHE = {}


def kernel(**inputs):
    from concourse.bass_utils import run_bass_kernel_spmd
    T = 2048
    n = 8
    if "nc" not in _CACHE:
        _CACHE["nc"] = Prog(T, debug=False).build()
    nc = _CACHE["nc"]
    in_maps = [make_inputs(inputs, b, T) for b in range(n)]
    res = run_bass_kernel_spmd(nc, in_maps, core_ids=list(range(n)))
    out = np.stack([np.asarray(res.results[b]["y"], dtype=np.float32) for b in range(n)], axis=0)
    return out
```

```python
import numpy as np
from contextlib import ExitStack
import concourse.bass as bass
import concourse.mybir as mybir

F32 = mybir.dt.float32
BF16 = mybir.dt.bfloat16
ALU = mybir.AluOpType
AF = mybir.ActivationFunctionType
AX = mybir.AxisListType

COMPUTE = ("pe", "dve", "act", "pool")
DMA_RING = 8


class View:
    __slots__ = ("tile", "ap", "key")

    def __init__(self, tile, ap, key=None):
        self.tile = tile
        self.ap = ap
        self.key = key

    def __getitem__(self, idx):
        return View(self.tile, self.ap[idx], self.key)

    def k(self, key):
        return View(self.tile, self.ap, key)

    def re(self, s, **kw):
        return View(self.tile, self.ap.rearrange(s, **kw), self.key)

    def bc(self, shape):
        return View(self.tile, self.ap.to_broadcast(shape), self.key)

    def bitcast(self, dt):
        return View(self.tile, self.ap.bitcast(dt), self.key)

    @property
    def shape(self):
        return self.ap.shape

    def v(self, key=None):
        return self if key is None else View(self.tile, self.ap, key)


class Buf:
    def __init__(self, name, ap):
        self.name = name
        self.ap = ap
        self.state = {None: [None, []]}

    def __getitem__(self, idx):
        return View(self, self.ap[idx], None)

    def v(self, key=None):
        return View(self, self.ap, key)

    def reset(self):
        self.state = {None: [None, []]}


class Op:
    __slots__ = ("eng", "emit", "deps", "sig", "sigval", "is_dma", "dsem", "dval", "dprev", "idx", "label")

    def __init__(self, eng, emit, is_dma=False, label=""):
        self.eng = eng
        self.emit = emit
        self.deps = {}
        self.sig = False
        self.sigval = 0
        self.is_dma = is_dma
        self.dsem = None
        self.dval = 0
        self.dprev = 0
        self.label = label


class FW:
    def __init__(self, nc, same_engine_sync=True):
        self.nc = nc
        self.es = ExitStack()
        self.engs = {"pe": nc.tensor, "dve": nc.vector, "act": nc.scalar, "pool": nc.gpsimd, "sp": nc.sync}
        self.sems = {}
        for e in self.engs:
            self.sems[e] = self.es.enter_context(nc.semaphore("s_" + e))
        self.dma_rings = {}
        self.dma_count = {}
        for q in ("sp", "pool", "act"):
            self.dma_rings[q] = [self.es.enter_context(nc.semaphore("d_%s%d" % (q, i))) for i in range(DMA_RING)]
            self.dma_count[q] = 0
        self.ops = []
        self.tiles = []
        self.stage_es = None
        self.same_engine_sync = same_engine_sync
        self.last_op = {e: None for e in self.engs}
        self.pending_dma = []
        self.n_emitted = 0

    def begin_stage(self):
        self.stage_es = ExitStack()

    def sb(self, name, shape, dtype=F32):
        self.uid = getattr(self, "uid", 0) + 1
        name = "%s_u%d" % (name, self.uid)
        t = self.stage_es.enter_context(self.nc.sbuf_tensor(name, list(shape), dtype))
        tl = Buf(name, t.ap() if hasattr(t, "ap") and callable(getattr(t, "ap")) else t[:])
        self.tiles.append(tl)
        return tl

    def ps(self, name, shape, dtype=F32):
        self.uid = getattr(self, "uid", 0) + 1
        name = "%s_u%d" % (name, self.uid)
        t = self.stage_es.enter_context(self.nc.psum_tensor(name, list(shape), dtype))
        tl = Buf(name, t.ap() if hasattr(t, "ap") and callable(getattr(t, "ap")) else t[:])
        self.tiles.append(tl)
        return tl

    def dram(self, name, shape, dtype=F32, kind="Internal"):
        t = self.nc.dram_tensor(name, list(shape), dtype, kind=kind)
        tl = Buf(name, t.ap())
        self.tiles.append(tl)
        return tl

    def _keys(self, v):
        st = v.tile.state
        if v.key is None:
            return list(st.keys())
        if v.key not in st:
            st[v.key] = [None, []]
        return [v.key, None]

    def _read(self, op, v):
        st = v.tile.state
        for k in self._keys(v):
            w = st[k][0]
            if w is not None and w is not op:
                op.deps[w] = True
        st[v.key][1].append(op)

    def _write(self, op, v):
        st = v.tile.state
        for k in self._keys(v):
            w, rs = st[k]
            if w is not None and w is not op:
                op.deps.setdefault(w, False)
            for r in rs:
                if r is not op:
                    op.deps.setdefault(r, False)
        if v.key is None:
            for k in list(st.keys()):
                st[k] = [op, []]
        else:
            st[v.key] = [op, []]

    def add(self, eng, emit, reads, writes, is_dma=False, label=""):
        op = Op(eng, emit, is_dma, label)
        for v in reads:
            self._read(op, v)
        for v in writes:
            self._write(op, v)
        if is_dma:
            j = self.dma_count[eng]
            self.dma_count[eng] = j + 1
            op.dsem = self.dma_rings[eng][j % DMA_RING]
            op.dval = 16 * (j // DMA_RING + 1)
            op.dprev = 16 * (j // DMA_RING)
            self.pending_dma.append(op)
        self.ops.append(op)
        self.last_op[eng] = op
        return op

    def barrier(self):
        lasts = [o for o in self.last_op.values() if o is not None and not o.is_dma]
        comp_last = {}
        for o in self.ops:
            if not o.is_dma and o.emit is not None:
                comp_last[o.eng] = o
        for e in self.engs:
            b = Op(e, None, False, "barrier")
            for e2, o in comp_last.items():
                if e2 != e:
                    b.deps[o] = True
            for d in self.pending_dma:
                b.deps[d] = True
            self.ops.append(b)
        self.pending_dma = []
        for t in self.tiles:
            t.reset()

    def end_stage(self):
        self.barrier()
        self.flush()
        self.stage_es.close()
        self.stage_es = None
        self.tiles = [t for t in self.tiles if t.name.startswith("D_")]

    def flush(self):
        ops = self.ops
        if not hasattr(self, "known"):
            self.known = {e: {} for e in self.engs}
            self.sigcount = {e: 0 for e in self.engs}
        for op in ops:
            for d, raw in op.deps.items():
                if d.is_dma:
                    continue
                if d.eng != op.eng:
                    d.sig = True
                elif self.same_engine_sync and raw and d.eng in ("dve", "act", "pool") and not op.is_dma:
                    d.sig = True
                elif op.is_dma and d.eng == op.eng:
                    d.sig = True
        for op in ops:
            e = op.eng
            eng = self.engs[e]
            kn = self.known[e]
            waits = {}
            for d, raw in op.deps.items():
                if d.is_dma:
                    s, val = d.dsem, d.dval
                elif d.sig and (d.eng != e or raw or op.is_dma):
                    s, val = self.sems[d.eng], d.sigval
                    assert val > 0, (d.label, op.label)
                else:
                    continue
                if kn.get(s, 0) < val:
                    waits[s] = max(waits.get(s, 0), val)
            if op.is_dma and op.dprev > 0:
                if kn.get(op.dsem, 0) < op.dprev:
                    waits[op.dsem] = max(waits.get(op.dsem, 0), op.dprev)
            for s, val in waits.items():
                eng.wait_ge(s, val)
                kn[s] = val
            if op.emit is None:
                continue
            ins = op.emit(eng)
            self.n_emitted += 1
            if op.is_dma:
                ins.then_inc(op.dsem, 16)
            elif op.sig:
                self.sigcount[e] += 1
                op.sigval = self.sigcount[e]
                ins.then_inc(self.sems[e], 1)
        self.ops = []

    def dma(self, out, in_, q="sp", **kw):
        o, i = out.ap, in_.ap
        return self.add(q, lambda eng: eng.dma_start(out=o, in_=i, **kw), [in_], [out], is_dma=True, label="dma")

    def mm(self, out, lhsT, rhs, start=True, stop=True, **kw):
        o, l, r = out.ap, lhsT.ap, rhs.ap
        return self.add("pe", lambda eng: eng.matmul(o, l, r, start=start, stop=stop, **kw), [lhsT, rhs], [out], label="mm")

    def transpose(self, out, in_, ident):
        o, i, d = out.ap, in_.ap, ident.ap
        return self.add("pe", lambda eng: eng.transpose(o, i, d), [in_, ident], [out], label="tr")

    def act(self, out, in_, func, bias=None, scale=None, accum=None, eng="act"):
        o, i = out.ap, in_.ap
        reads = [in_]
        kw = {}
        if bias is not None:
            if isinstance(bias, View):
                reads.append(bias)
                kw["bias"] = bias.ap
            else:
                kw["bias"] = float(bias)
        if scale is not None:
            if isinstance(scale, View):
                reads.append(scale)
                kw["scale"] = scale.ap
            else:
                kw["scale"] = float(scale)
        writes = [out]
        if accum is not None:
            writes.append(accum)
            kw["accum_out"] = accum.ap
        return self.add(eng, lambda en: en.activation(o, i, func, **kw), reads, writes, label="act")

    def tt(self, out, a, b, op, eng="dve"):
        o, x, y = out.ap, a.ap, b.ap
        return self.add(eng, lambda en: en.tensor_tensor(o, x, y, op), [a, b], [out], label="tt")

    def ts(self, out, a, s1, op0, s2=None, op1=None, accum=None, eng="dve"):
        o, x = out.ap, a.ap
        reads = [a]
        if isinstance(s1, View):
            reads.append(s1)
            s1v = s1.ap
        else:
            s1v = float(s1)
        if isinstance(s2, View):
            reads.append(s2)
            s2v = s2.ap
        else:
            s2v = None if s2 is None else float(s2)
        writes = [out]
        kw = {}
        if op1 is not None:
            kw["op1"] = op1
        if accum is not None:
            writes.append(accum)
            kw["accum_out"] = accum.ap
        return self.add(eng, lambda en: en.tensor_scalar(o, x, s1v, s2v, op0, **kw), reads, writes, label="ts")

    def stt(self, out, a, s, b, op0, op1, accum=None, eng="dve"):
        o, x, y = out.ap, a.ap, b.ap
        reads = [a, b]
        if isinstance(s, View):
            reads.append(s)
            sv = s.ap
        else:
            sv = float(s)
        writes = [out]
        kw = {}
        if accum is not None:
            writes.append(accum)
            kw["accum_out"] = accum.ap
        return self.add(eng, lambda en: en.scalar_tensor_tensor(o, x, sv, y, op0, op1, **kw), reads, writes, label="stt")

    def copy(self, out, in_, eng="dve"):
        o, i = out.ap, in_.ap
        if eng == "act":
            return self.add("act", lambda en: en.copy(o, i), [in_], [out], label="copy")
        return self.add(eng, lambda en: en.tensor_copy(o, i), [in_], [out], label="copy")

    def memset(self, out, val, eng="dve"):
        o = out.ap
        return self.add(eng, lambda en: en.memset(o, val), [], [out], label="memset")

    def reduce(self, out, in_, op=ALU.add, axis=AX.X, eng="dve"):
        o, i = out.ap, in_.ap
        return self.add(eng, lambda en: en.tensor_reduce(o, i, axis, op), [in_], [out], label="reduce")

    def recip(self, out, in_):
        o, i = out.ap, in_.ap
        return self.add("dve", lambda en: en.reciprocal(o, i), [in_], [out], label="recip")

    def finish(self):
        self.barrier()
        self.flush()
        if self.stage_es is not None:
            self.stage_es.close()
        self.es.close()


import math
import os
F32R = mybir.dt.float32r

D = 2048
NIN = 6920
NP = 6952
FH = 5632
EPS = 1e-6
GDN0 = 1792
DIFF0 = 3848
C_ID = 0
C_LE = 128
C_LT = 192
C_GT = 256
C_GE = 320
C_ONE = 384
C_NLT = 512
C_NGT = 576
NCONST = 640


def make_consts():
    c = np.zeros((128, NCONST), np.float32)
    c[:, C_ID:C_ID + 128] = np.eye(128)
    i = np.arange(64)
    le = (i[:, None] <= i[None, :]).astype(np.float32)
    lt = (i[:, None] < i[None, :]).astype(np.float32)
    c[:64, C_LE:C_LE + 64] = le
    c[:64, C_LT:C_LT + 64] = lt
    c[:64, C_GT:C_GT + 64] = lt.T
    c[:64, C_GE:C_GE + 64] = le.T
    c[:, C_ONE:C_ONE + 128] = 1.0
    c[:64, C_NLT:C_NLT + 64] = -lt
    c[:64, C_NGT:C_NGT + 64] = -lt.T
    cm = np.zeros((128, 4, 512), np.float32)
    kp = np.arange(128)[:, None]
    q = np.arange(512)[None, :]
    for r in range(4):
        cm[:, r, :] = np.where(q >= r * 128 + kp, 0.0, -30000.0)
    return c, cm.reshape(128, 2048)


def run_interleaved(gens):
    alive = list(gens)
    while alive:
        for g in list(alive):
            try:
                next(g)
            except StopIteration:
                alive.remove(g)


def bcv(view, shape):
    return View(view.tile, view.ap.to_broadcast(list(shape)), view.key)


def rowbc(dt, row, c0, c1, parts):
    return View(dt, dt.ap[row:row + 1, c0:c1].to_broadcast([parts, c1 - c0]))


class Prog:
    def __init__(self, T, L=2, debug=False):
        self.T = T
        self.L = L
        self.debug = debug
        nc = bass.Bass("TRN2", target_bir_lowering=False)
        self.nc = nc
        fw = FW(nc)
        self.fw = fw
        I = lambda n, s: fw.dram(n, s, F32, kind="ExternalInput")
        self.x = I("x", [T, D])
        self.attn_norm_w = I("attn_norm_w", [L, D])
        self.w_in = I("w_in", [L * D, NIN])
        self.w_vres_a = I("w_vres_a", [D, 32])
        self.mu_rwkv = I("mu_rwkv", [L, 1792])
        self.mu_vres = I("mu_vres", [1, 32])
        self.rwkv_w0 = I("rwkv_w0", [L, 512])
        self.rwkv_w_lora_b = I("rwkv_w_lora_b", [L * 64, 512])
        self.rwkv_a0 = I("rwkv_a0", [L, 512])
        self.rwkv_a_lora_b = I("rwkv_a_lora_b", [L * 64, 512])
        self.rwkv_g_lora_b = I("rwkv_g_lora_b", [L * 128, 512])
        self.rwkv_v0 = I("rwkv_v0", [1, 512])
        self.rwkv_v_lora_b = I("rwkv_v_lora_b", [32, 512])
        self.rwkv_k_k = I("rwkv_k_k", [L, 512])
        self.rwkv_k_a = I("rwkv_k_a", [L, 512])
        self.rwkv_r_k = I("rwkv_r_k", [L, 512])
        self.rwkv_ln_w = I("rwkv_ln_w", [L, 512])
        self.rwkv_ln_b = I("rwkv_ln_b", [L, 512])
        self.gdn_conv_w = I("gdn_conv_w", [L * 4, 1536])
        self.gdn_A_log = I("gdn_A_log", [L, 4])
        self.gdn_dt_bias = I("gdn_dt_bias", [L, 4])
        self.gdn_norm_w = I("gdn_norm_w", [L, 128])
        self.diff_q_norm_w = I("diff_q_norm_w", [L, 64])
        self.diff_k_norm_w = I("diff_k_norm_w", [L, 64])
        self.diff_lambda_q1 = I("diff_lambda_q1", [L, 64])
        self.diff_lambda_k1 = I("diff_lambda_k1", [L, 64])
        self.diff_lambda_q2 = I("diff_lambda_q2", [L, 64])
        self.diff_lambda_k2 = I("diff_lambda_k2", [L, 64])
        self.diff_subln_w = I("diff_subln_w", [L, 128])
        self.w_out = I("w_out", [L * D, D])
        self.ffn_norm_w = I("ffn_norm_w", [L, D])
        self.w_ffn_in = I("w_ffn_in", [L * D, 2 * FH])
        self.w_ffn_out = I("w_ffn_out", [L * FH, D])
        self.consts = I("consts", [128, NCONST])
        self.cmask = I("cmask", [128, 2048])
        self.y = fw.dram("y", [T, D], F32, kind="ExternalOutput")
        okind = "ExternalOutput" if debug else "Internal"
        self.P = fw.dram("D_P", [T, NP], F32, kind=okind)
        self.MIX = fw.dram("D_MIX", [T, D], F32, kind=okind)
        self.VF = fw.dram("D_VF", [T, 512], F32, kind="Internal")
        self.rr = 0
        import os
        self.cut = int(os.environ.get('GDN_CUT', '0'))

    def ev(self):
        self.rr += 1
        return "dve" if self.rr % 2 else "act"

    def load_consts(self, bf=False):
        fw = self.fw
        c = fw.sb("consts_sb", [128, NCONST], F32)
        fw.dma(c.v(), self.consts.v(), q="sp")
        self.c = c
        if bf:
            idb = fw.sb("idb", [128, 128], BF16)
            fw.copy(idb.v(), c[:, C_ID:C_ID + 128])
            self.idb = idb

    def nt_alloc(self, normw_row, single=False):
        fw = self.fw
        a = {}
        nb = 1 if single else 2
        a["xts"] = [fw.sb("nt_x%d" % i, [128, D], F32) for i in range(nb)] * (3 - nb)
        a["hs"] = [fw.sb("nt_h%d" % i, [128, D], BF16) for i in range(nb)] * (3 - nb)
        a["ss"] = [fw.sb("nt_ss%d" % i, [128, 1], F32) for i in range(2)]
        a["pst"] = [fw.ps("nt_ps%d" % i, [128, 4, 128], BF16) for i in range(2)]
        a["wbc"] = None
        if normw_row is not None:
            a["wbc"] = fw.sb("nt_w", [128, D], F32)
            fw.dma(a["wbc"].v(), rowbc(normw_row[0], normw_row[1], 0, D, 128), q="sp")
        return a

    def norm_transpose(self, a, src, r0, nrows_tiles, hT, tcol0, xkeyf=None):
        fw = self.fw
        xts, hs, ss, pst, wbc = a["xts"], a["hs"], a["ss"], a["pst"], a["wbc"]
        for t in range(nrows_tiles):
            xt = xts[t % 2]
            h = hs[t % 2]
            s_ = ss[t % 2]
            rows = slice(r0 + t * 128, r0 + (t + 1) * 128)
            sv = src[rows, :]
            if xkeyf is not None:
                sv = sv.k(xkeyf(r0 // 128 + t))
            fw.dma(xt.v(), sv, q="sp")
            if wbc is not None:
                fw.memset(s_.v(), 0.0)
                fw.act(h.v(), xt.v(), AF.Square, accum=s_.v())
                fw.act(s_.v(), s_.v(), AF.Sqrt, bias=EPS, scale=1.0 / D)
                fw.recip(s_.v(), s_.v())
                fw.stt(h.v(), xt.v(), s_.v(), wbc.v(), ALU.mult, ALU.mult)
            else:
                fw.copy(h.v(), xt.v(), eng="dve")
            for g in range(4):
                pt = pst[g % 2]
                for j in range(4):
                    cc = g * 4 + j
                    fw.transpose(pt[:, j, :], h[:, cc * 128:(cc + 1) * 128], self.idb.v())
                fw.copy(hT[:, g * 4:(g + 1) * 4, tcol0 + t * 128: tcol0 + (t + 1) * 128].k(("t", tcol0 // 128 + t)), pt.v(), eng=self.ev())

    def proj_stage(self, src, normw_row, blocks, resid=None):
        fw = self.fw
        T = self.T
        fw.begin_stage()
        self.load_consts(bf=True)
        hT = fw.sb("hT", [128, 16, T], BF16)
        psm = [fw.ps("psm%d" % i, [128, 512], F32) for i in range(3)]
        nta = self.nt_alloc(normw_row)
        self.norm_transpose(nta, src, 0, T // 128, hT, 0)
        wts = [fw.sb("wt%d" % i, [128, 16, 512], BF16) for i in range(2)]
        obs = [fw.sb("ob%d" % i, [128, 512], F32) for i in range(3)]
        rbs = [fw.sb("rb%d" % i, [128, 512], F32) for i in range(3)] if resid is not None else None
        n = 0
        for bi, (wd, row0, col0, ncols, dst, dc0) in enumerate(blocks):
            wt = wts[bi % 2]
            wv = View(wd, wd.ap[row0:row0 + D, col0:col0 + ncols].rearrange("(c p) n -> p c n", p=128))
            fw.dma(wt[:, :, 0:ncols], wv, q="pool")
            for t in range(T // 128):
                pm = psm[n % 3]
                ob = obs[n % 3]
                for c in range(16):
                    fw.mm(pm[:, 0:ncols], hT[:, c, t * 128:(t + 1) * 128].k(("t", t)), wt[:, c, 0:ncols], start=(c == 0), stop=(c == 15))
                rows = slice(t * 128, (t + 1) * 128)
                if resid is not None:
                    rb = rbs[n % 3]
                    fw.dma(rb[:, 0:ncols], resid[rows, dc0:dc0 + ncols].k(t), q="sp")
                    fw.tt(ob[:, 0:ncols], pm[:, 0:ncols], rb[:, 0:ncols], ALU.add)
                else:
                    fw.copy(ob[:, 0:ncols], pm[:, 0:ncols], eng=self.ev())
                fw.dma(dst[rows, dc0:dc0 + ncols].k(t), ob[:, 0:ncols], q="sp")
                n += 1
        fw.end_stage()

    def in_proj(self, l):
        src = self.x if l == 0 else self.y
        blocks = []
        c = 0
        while c < NIN:
            n = min(512, NIN - c)
            blocks.append((self.w_in, l * D, c, n, self.P, c))
            c += n
        if l > 0:
            blocks.append((self.w_vres_a, 0, 0, 32, self.P, NIN))
        self.proj_stage(src, (self.attn_norm_w, l), blocks)

    def out_proj(self, l):
        src = self.x if l == 0 else self.y
        blocks = [(self.w_out, l * D, cb * 512, 512, self.y, cb * 512) for cb in range(4)]
        self.proj_stage(self.MIX, None, blocks, resid=src)

    def ffn(self, l):
        fw = self.fw
        T = self.T
        TB = min(T, 1024)
        NH = TB // 512 if TB >= 512 else 1
        HW = min(TB, 512)
        pes = ExitStack()
        self.uid_p = getattr(self, "uid_p", 0) + 1
        t_ = pes.enter_context(self.nc.sbuf_tensor("D_actT_%d" % self.uid_p, [128, 44, TB], BF16))
        actT = Buf("D_actT", t_[:])
        fw.tiles.append(actT)
        nblk = T // TB
        for tb in range(nblk):
            fw.begin_stage()
            self.load_consts(bf=True)
            hT = fw.sb("f_hT", [128, 16, TB], BF16)
            wg = [fw.sb("f_wg%d" % i, [128, 16, 128], BF16) for i in range(2)]
            wu = [fw.sb("f_wu%d" % i, [128, 16, 128], BF16) for i in range(2)]
            psgu = [(fw.ps("f_psg%d" % i, [128, 512], F32), fw.ps("f_psu%d" % i, [128, 512], F32)) for i in range(3)]
            sg = [fw.sb("f_sg%d" % i, [128, 512], F32) for i in range(2)]
            nta = self.nt_alloc((self.ffn_norm_w, l), single=True)
            self.norm_transpose(nta, self.y, tb * TB, TB // 128, hT, 0, xkeyf=lambda i: i)
            n = 0
            r0 = l * D
            for j in range(44):
                g_ = wg[j % 2]
                u_ = wu[j % 2]
                fw.dma(g_.v(), View(self.w_ffn_in, self.w_ffn_in.ap[r0:r0 + D, j * 128:(j + 1) * 128].rearrange("(c p) n -> p c n", p=128)), q="pool")
                fw.dma(u_.v(), View(self.w_ffn_in, self.w_ffn_in.ap[r0:r0 + D, FH + j * 128:FH + (j + 1) * 128].rearrange("(c p) n -> p c n", p=128)), q="pool")
                for hf in range(NH):
                    pg, pu = psgu[n % 3]
                    s_ = sg[n % 2]
                    cols = slice(hf * HW, (hf + 1) * HW)
                    for c in range(16):
                        fw.mm(pg[:, 0:HW], g_[:, c, :], hT[:, c, cols], start=(c == 0), stop=(c == 15))
                    for c in range(16):
                        fw.mm(pu[:, 0:HW], u_[:, c, :], hT[:, c, cols], start=(c == 0), stop=(c == 15))
                    fw.act(s_[:, 0:HW], pg[:, 0:HW], AF.Silu)
                    fw.tt(actT[:, j, cols], s_[:, 0:HW], pu[:, 0:HW], ALU.mult)
                    n += 1
            fw.end_stage()
            fw.begin_stage()
            NTT = TB // 128
            wo = [fw.sb("f_wo%d" % i, [128, 11, 512], BF16) for i in range(4)]
            pso = [fw.ps("f_pso%d" % i, [128, 512], F32) for i in range(NTT)]
            ob = [fw.sb("f_ob%d" % i, [128, 512], F32) for i in range(2)]
            rb = [fw.sb("f_rb%d" % i, [128, 512], F32) for i in range(2)]
            nw_ = 0
            m = 0
            r0 = l * FH
            for cb in range(4):
                for kq in range(4):
                    w_ = wo[nw_ % 4]
                    nw_ += 1
                    fw.dma(w_.v(), View(self.w_ffn_out, self.w_ffn_out.ap[r0 + kq * 1408:r0 + (kq + 1) * 1408, cb * 512:(cb + 1) * 512].rearrange("(c p) n -> p c n", p=128)), q="pool")
                    for tt in range(NTT):
                        for jj in range(11):
                            j = kq * 11 + jj
                            fw.mm(pso[tt].v(), actT[:, j, tt * 128:(tt + 1) * 128], w_[:, jj, :], start=(j == 0), stop=(j == 43))
                for tt in range(NTT):
                    gt = tb * NTT + tt
                    rows = slice(gt * 128, (gt + 1) * 128)
                    fw.dma(rb[m % 2].v(), self.y[rows, cb * 512:(cb + 1) * 512].k(gt), q="sp")
                    fw.tt(ob[m % 2].v(), pso[tt].v(), rb[m % 2].v(), ALU.add)
                    fw.dma(self.y[rows, cb * 512:(cb + 1) * 512].k(gt), ob[m % 2].v(), q="sp")
                    m += 1
            fw.end_stage()
        fw.tiles.remove(actT)
        pes.close()

    def inv_chain(self, nh, X, XT, TT, Xb, XTb, psA, psB, psC, gen=False):
        g = self._inv_chain_gen(nh, X, XT, TT, Xb, XTb, psA, psB, psC)
        if gen:
            return g
        for _ in g:
            pass

    def _inv_chain_gen(self, nh, X, XT, TT, Xb, XTb, psA, psB, psC):
        fw = self.fw
        cur, curT, nxt, nxtT = X, XT, Xb, XTb
        for step in range(5):
            last = step == 4
            for h in range(nh):
                fw.mm(psA[:, h, :], curT[:, h, :], cur[:, h, :])
            if not last:
                for h in range(nh):
                    fw.mm(psB[:, h, :], cur[:, h, :], curT[:, h, :])
            yield
            fw.copy(nxt.v(), psA, eng="act")
            if not last:
                fw.copy(nxtT.v(), psB, eng="dve")
            yield
            for h in range(nh):
                fw.mm(psC[:, h, :], nxt[:, h, :], TT[:, h, :])
            yield
            fw.tt(TT.v(), TT.v(), psC, ALU.add)
            yield
            cur, curT, nxt, nxtT = nxt, nxtT, cur, curT

    def rwkv(self, l):
        fw = self.fw
        T = self.T
        NW = 1824
        fw.begin_stage()
        self.load_consts()
        c = self.c
        ident64 = c[0:64, C_ID:C_ID + 64]
        ones64 = c[0:64, C_ONE:C_ONE + 64]
        m3 = lambda col: bcv(View(c, c.ap[0:64, col:col + 64].unsqueeze(1)), [64, 8, 64])
        mu = fw.sb("r_mu", [64, NW], F32)
        fw.dma(mu[:, 0:1792], rowbc(self.mu_rwkv, l, 0, 1792, 64))
        if l > 0:
            fw.dma(mu[:, 1792:1824], rowbc(self.mu_vres, 0, 0, 32, 64))
        else:
            fw.memset(mu[:, 1792:1824], 0.0)
        wbw = fw.sb("r_wbw", [65, 512], F32)
        fw.dma(wbw[0:64, :], self.rwkv_w_lora_b[l * 64:(l + 1) * 64, :])
        fw.dma(wbw[64:65, :], self.rwkv_w0[l:l + 1, :])
        wba = fw.sb("r_wba", [65, 512], F32)
        fw.dma(wba[0:64, :], self.rwkv_a_lora_b[l * 64:(l + 1) * 64, :])
        fw.dma(wba[64:65, :], self.rwkv_a0[l:l + 1, :])
        wbg = fw.sb("r_wbg", [128, 512], F32)
        fw.dma(wbg.v(), self.rwkv_g_lora_b[l * 128:(l + 1) * 128, :])
        wbv = fw.sb("r_wbv", [33, 512], F32)
        if l > 0:
            fw.dma(wbv[0:32, :], self.rwkv_v_lora_b[0:32, :])
            fw.dma(wbv[32:33, :], self.rwkv_v0[0:1, :])
        kkb = fw.sb("r_kkb", [64, 512], F32)
        fw.dma(kkb.v(), rowbc(self.rwkv_k_k, l, 0, 512, 64))
        kab = fw.sb("r_kab", [64, 512], F32)
        fw.dma(kab.v(), rowbc(self.rwkv_k_a, l, 0, 512, 64))
        omka = fw.sb("r_omka", [64, 512], F32)
        fw.ts(omka.v(), kab.v(), -1.0, ALU.mult, 1.0, ALU.add)
        rkb = fw.sb("r_rkb", [64, 512], F32)
        fw.dma(rkb.v(), rowbc(self.rwkv_r_k, l, 0, 512, 64))
        lnw = fw.sb("r_lnw", [64, 512], F32)
        fw.dma(lnw.v(), rowbc(self.rwkv_ln_w, l, 0, 512, 64))
        lnb = fw.sb("r_lnb", [64, 512], F32)
        fw.dma(lnb.v(), rowbc(self.rwkv_ln_b, l, 0, 512, 64))
        H = fw.sb("r_H", [64, 8, 64], F32R)
        fw.ts(H.v().re("p h n -> p (h n)"), c[0:64, 0:512], 0.0, ALU.mult)
        mk2 = lambda nm, shp, dt=F32: [fw.sb("%s%d" % (nm, i), shp, dt) for i in range(2)]
        mk1 = lambda nm, shp, dt=F32: [fw.sb("%s%d" % (nm, 0), shp, dt)] * 2
        R32 = ("vp", "Bb", "Kb", "X", "U")
        pcs, pps, prws = mk1("r_pc", [64, NW]), mk1("r_pp", [64, NW]), mk2("r_prw", [64, NW])
        LTs = mk1("r_LT", [128, 4, 64])
        fw.memset(LTs[0][64:65, 0:2, :], 1.0)
        fw.memset(LTs[0][32:33, 3, :], 1.0)
        dbl = ["vp", "Bb", "Kb", "kp", "g"]
        sgl = ["lw", "a", "kk", "b", "cws", "Ep", "Em", "Ex", "EL", "At", "Bt", "Kt", "Rt", "t1", "t2", "t3", "t4", "X", "U", "yo"]
        S = {nm: mk2("r_" + nm, [64, 512], F32R if nm in R32 else F32) for nm in dbl}
        S.update({nm: mk1("r_" + nm, [64, 512], F32R if nm in R32 else F32) for nm in sgl})
        M = {nm: mk2("r_" + nm, [64, 8, 64], F32R) for nm in ["AtT", "BtT", "KtT", "RtT"]}
        M.update({nm: mk1("r_" + nm, [64, 8, 64], F32R) for nm in ["X1", "XT1", "X2", "XT2", "TT", "LrbT", "AakT", "LrkT"]})
        small = {nm: mk1("r_" + nm, [64, 8]) for nm in ["ss", "mean", "var", "bon"]}
        small["pct"] = mk2("r_pct", [64, 8])
        vfs = mk1("r_vf", [64, 512])
        ps = [fw.ps("r_ps%d" % i, [128, 512], F32) for i in range(8)]
        P8 = lambda i: View(ps[i], ps[i].ap[0:64, :].rearrange("p (h n) -> p h n", h=8))
        h3 = lambda v: v.re("p (h n) -> p h n", h=8)
        col8 = lambda t: bcv(View(t.tile, t.ap.unsqueeze(2)), [64, 8, 64])
        nch = T // 64

        def phaseA(ch):
            i = ch % 2
            t0 = ch * 64
            pc, pp, prw, LT = pcs[i], pps[i], prws[i], LTs[i]
            s = {k_: v_[i] for k_, v_ in S.items()}
            mm_ = {k_: v_[i] for k_, v_ in M.items()}
            sm = {k_: v_[i] for k_, v_ in small.items()}
            fw.dma(pc[:, 0:1792], self.P[t0:t0 + 64, 0:1792])
            if l > 0:
                fw.dma(pc[:, 1792:1824], self.P[t0:t0 + 64, NIN:NP])
            elif ch == 0:
                fw.memset(pc[:, 1792:1824], 0.0)
            if ch == 0:
                fw.memset(pp.v(), 0.0)
                fw.dma(pp[1:64, 0:1792], self.P[0:63, 0:1792])
                if l > 0:
                    fw.dma(pp[1:64, 1792:1824], self.P[0:63, NIN:NP])
            else:
                fw.dma(pp[:, 0:1792], self.P[t0 - 1:t0 + 63, 0:1792])
                if l > 0:
                    fw.dma(pp[:, 1792:1824], self.P[t0 - 1:t0 + 63, NIN:NP])
            yield
            fw.tt(pp.v(), pp.v(), pc.v(), ALU.subtract)
            yield
            fw.tt(pp.v(), pp.v(), mu.v(), ALU.mult)
            yield
            fw.tt(prw.v(), pc.v(), pp.v(), ALU.add)
            yield
            r, k_, v = prw[:, 0:512], prw[:, 512:1024], prw[:, 1024:1536]
            fw.act(prw[:, 1536:1600], prw[:, 1536:1600], AF.Tanh)
            fw.act(prw[:, 1664:1792], prw[:, 1664:1792], AF.Sigmoid)
            pl = View(ps[0], ps[0].ap[:, 0:256].rearrange("p (a t) -> p a t", a=4))
            fw.transpose(pl[0:64, 0, :], prw[:, 1536:1600], ident64)
            fw.transpose(pl[0:64, 1, :], prw[:, 1600:1664], ident64)
            fw.transpose(pl[0:128, 2, :], prw[:, 1664:1792], ident64)
            if l > 0:
                fw.transpose(pl[0:32, 3, :], prw[:, 1792:1824], ident64)
            yield
            fw.copy(LT[0:64, 0:2, :], pl[0:64, 0:2, :], eng="dve")
            fw.copy(LT[0:128, 2, :], pl[0:128, 2, :], eng="act")
            if l > 0:
                fw.copy(LT[0:32, 3, :], pl[0:32, 3, :], eng="dve")
            yield
            fw.mm(ps[1][0:64, :], LT[0:65, 0, :], wbw.v())
            fw.mm(ps[2][0:64, :], LT[0:65, 1, :], wba.v())
            yield
            fw.act(s["lw"].v(), ps[1][0:64, :], AF.Sigmoid)
            fw.act(s["a"].v(), ps[2][0:64, :], AF.Sigmoid)
            fw.mm(ps[1][0:64, :], LT[0:128, 2, :], wbg.v())
            if l > 0:
                fw.mm(ps[2][0:64, :], LT[0:33, 3, :], wbv.v())
            yield
            fw.ts(s["lw"].v(), s["lw"].v(), -math.exp(-0.5), ALU.mult)
            yield
            fw.copy(s["g"].v(), ps[1][0:64, :], eng="dve")
            yield
            if l > 0:
                vf = vfs[i]
                fw.dma(vf.v(), self.VF[t0:t0 + 64, :])
                fw.act(s["t1"].v(), ps[2][0:64, :], AF.Sigmoid)
                fw.tt(vf.v(), vf.v(), v, ALU.subtract)
                yield
                fw.tt(vf.v(), vf.v(), s["t1"].v(), ALU.mult)
                yield
                fw.tt(s["vp"].v(), v, vf.v(), ALU.add)
            else:
                fw.copy(s["vp"].v(), v, eng="dve")
                fw.dma(self.VF[t0:t0 + 64, :], v)
            yield
            fw.tt(s["kk"].v(), k_, kkb.v(), ALU.mult)
            yield
            fw.tt(s["t1"].v(), s["kk"].v(), s["kk"].v(), ALU.mult)
            yield
            fw.reduce(sm["ss"].v(), h3(s["t1"].v()))
            fw.act(sm["ss"].v(), sm["ss"].v(), AF.Ln, bias=EPS)
            fw.act(sm["ss"].v(), sm["ss"].v(), AF.Exp, scale=-0.5)
            yield
            fw.tt(h3(s["kk"].v()), h3(s["kk"].v()), col8(sm["ss"].v()), ALU.mult)
            yield
            fw.tt(s["t1"].v(), s["a"].v(), kab.v(), ALU.mult)
            yield
            fw.tt(s["t1"].v(), s["t1"].v(), omka.v(), ALU.add)
            yield
            fw.tt(s["kp"].v(), k_, s["t1"].v(), ALU.mult)
            yield
            fw.tt(s["b"].v(), s["kk"].v(), s["a"].v(), ALU.mult)
            yield
            fw.mm(ps[1][0:64, :], c[0:64, C_LE:C_LE + 64], s["lw"].v())
            fw.mm(ps[2][0:64, :], ones64, s["lw"].v())
            pct_ps = View(ps[0], ps[0].ap[0:64, 256:264])
            for h in range(8):
                fw.mm(pct_ps[:, h:h + 1], s["lw"][:, h * 64:(h + 1) * 64], c[0:64, C_ONE:C_ONE + 1])
            yield
            fw.act(sm["pct"].v(), pct_ps, AF.Exp)
            fw.copy(s["cws"].v(), ps[1][0:64, :], eng="dve")
            fw.act(s["Ep"].v(), ps[1][0:64, :], AF.Exp)
            fw.act(s["Em"].v(), ps[1][0:64, :], AF.Exp, scale=-1.0)
            yield
            fw.tt(s["t1"].v(), s["cws"].v(), s["lw"].v(), ALU.subtract)
            fw.act(s["Ex"].v(), s["t1"].v(), AF.Exp)
            yield
            fw.tt(s["t2"].v(), ps[2][0:64, :], s["cws"].v(), ALU.subtract)
            fw.act(s["EL"].v(), s["t2"].v(), AF.Exp)
            yield
            fw.tt(s["Bt"].v(), s["b"].v(), s["Em"].v(), ALU.mult)
            yield
            fw.tt(s["Kt"].v(), s["kp"].v(), s["Em"].v(), ALU.mult)
            yield
            fw.tt(s["Rt"].v(), r, s["Ep"].v(), ALU.mult)
            yield
            fw.stt(s["At"].v(), s["kk"].v(), -1.0, s["Ex"].v(), ALU.mult, ALU.mult)
            yield
            fw.tt(s["Bb"].v(), s["b"].v(), s["EL"].v(), ALU.mult)
            yield
            fw.tt(s["Kb"].v(), s["kp"].v(), s["EL"].v(), ALU.mult)
            yield
            for j, (src_, dst_) in enumerate([("Bt", "BtT"), ("Kt", "KtT"), ("Rt", "RtT"), ("At", "AtT")]):
                pt = P8((1 + j % 2) if not os.environ.get('TRB') else (4 + j))
                for h in range(8):
                    fw.transpose(pt[:, h, :], s[src_][:, h * 64:(h + 1) * 64], ident64)
                yield
                fw.copy(mm_[dst_].v(), pt, eng="act" if j % 2 else "dve")
                yield

        def phaseBC(ch):
            i = ch % 2
            t0 = ch * 64
            prw = prws[i]
            s = {k_: v_[i] for k_, v_ in S.items()}
            mm_ = {k_: v_[i] for k_, v_ in M.items()}
            sm = {k_: v_[i] for k_, v_ in small.items()}
            r = prw[:, 0:512]
            vp = s["vp"]
            AtT, BtT, KtT, RtT = mm_["AtT"], mm_["BtT"], mm_["KtT"], mm_["RtT"]
            for h in range(8):
                fw.mm(P8(3)[:, h, :], BtT[:, h, :], AtT[:, h, :])
            for h in range(8):
                fw.mm(P8(4)[:, h, :], AtT[:, h, :], BtT[:, h, :])
            yield
            fw.tt(mm_["XT1"].v(), P8(3), m3(C_LT), ALU.mult)
            fw.tt(mm_["X1"].v(), P8(4), m3(C_GT), ALU.mult)
            for h in range(8):
                fw.mm(P8(5)[:, h, :], BtT[:, h, :], RtT[:, h, :])
            for h in range(8):
                fw.mm(P8(6)[:, h, :], KtT[:, h, :], AtT[:, h, :])
            for h in range(8):
                fw.mm(P8(7)[:, h, :], KtT[:, h, :], RtT[:, h, :])
            yield
            fw.tt(mm_["TT"].v(), mm_["XT1"].v(), m3(C_ID), ALU.add)
            yield
            fw.tt(mm_["LrbT"].v(), P8(5), m3(C_LE), ALU.mult)
            yield
            fw.tt(mm_["AakT"].v(), P8(6), m3(C_LT), ALU.mult)
            yield
            fw.tt(mm_["LrkT"].v(), P8(7), m3(C_LE), ALU.mult)
            yield
            yield from self.inv_chain(8, mm_["X1"], mm_["XT1"], mm_["TT"], mm_["X2"], mm_["XT2"], P8(3), P8(4), P8(5), gen=True)
            TT = mm_["TT"]
            for h in range(8):
                hs_ = slice(h * 64, (h + 1) * 64)
                fw.mm(P8(6)[:, h, :], AtT[:, h, :], H[:, h, :], start=True, stop=False)
                fw.mm(P8(6)[:, h, :], mm_["AakT"][:, h, :], vp[:, hs_], start=False, stop=True)
            yield
            fw.copy(s["X"].v(), ps[6][0:64, :], eng="dve")
            yield
            for h in range(8):
                hs_ = slice(h * 64, (h + 1) * 64)
                fw.mm(P8(7)[:, h, :], TT[:, h, :], s["X"][:, hs_])
            yield
            fw.copy(s["U"].v(), ps[7][0:64, :], eng="act")
            yield
            for h in range(8):
                hs_ = slice(h * 64, (h + 1) * 64)
                fw.mm(P8(4)[:, h, :], s["Bb"][:, hs_], s["U"][:, hs_], start=True, stop=False)
                fw.mm(P8(4)[:, h, :], s["Kb"][:, hs_], vp[:, hs_], start=False, stop=True)
            for h in range(8):
                hs_ = slice(h * 64, (h + 1) * 64)
                fw.mm(P8(3)[:, h, :], RtT[:, h, :], H[:, h, :], start=True, stop=False)
                fw.mm(P8(3)[:, h, :], mm_["LrbT"][:, h, :], s["U"][:, hs_], start=False, stop=False)
                fw.mm(P8(3)[:, h, :], mm_["LrkT"][:, h, :], vp[:, hs_], start=False, stop=True)
            yield
            fw.tt(H.v(), H.v(), col8(sm["pct"].v()), ALU.mult)
            yield
            fw.tt(H.v(), H.v(), P8(4), ALU.add)
            yield
            yo = s["yo"]
            fw.reduce(sm["mean"].v(), P8(3))
            fw.ts(sm["mean"].v(), sm["mean"].v(), 1.0 / 64, ALU.mult)
            yield
            fw.tt(h3(yo.v()), P8(3), col8(sm["mean"].v()), ALU.subtract)
            yield
            fw.tt(s["t3"].v(), yo.v(), yo.v(), ALU.mult)
            yield
            fw.reduce(sm["var"].v(), h3(s["t3"].v()))
            fw.act(sm["var"].v(), sm["var"].v(), AF.Ln, bias=64e-5, scale=1.0 / 64)
            fw.act(sm["var"].v(), sm["var"].v(), AF.Exp, scale=-0.5)
            yield
            fw.tt(h3(yo.v()), h3(yo.v()), col8(sm["var"].v()), ALU.mult)
            yield
            fw.tt(yo.v(), yo.v(), lnw.v(), ALU.mult)
            yield
            fw.tt(yo.v(), yo.v(), lnb.v(), ALU.add)
            yield
            fw.tt(s["t3"].v(), r, s["kp"].v(), ALU.mult)
            yield
            fw.tt(s["t3"].v(), s["t3"].v(), rkb.v(), ALU.mult)
            yield
            fw.reduce(sm["bon"].v(), h3(s["t3"].v()))
            yield
            fw.tt(h3(s["t4"].v()), h3(vp.v()), col8(sm["bon"].v()), ALU.mult)
            yield
            fw.tt(yo.v(), yo.v(), s["t4"].v(), ALU.add)
            yield
            fw.tt(yo.v(), yo.v(), s["g"].v(), ALU.mult)
            fw.dma(self.MIX[t0:t0 + 64, 0:512].k(("r", ch)), yo.v())
            yield

        for _ in phaseA(0):
            pass
        for ch in range(nch):
            gens = [phaseBC(ch)]
            if ch + 1 < nch:
                gens.append(phaseA(ch + 1))
            if os.environ.get('SEQ'):
                for g_ in gens:
                    for _ in g_:
                        pass
            else:
                run_interleaved(gens)
        fw.end_stage()

    def gdn(self, l):
        fw = self.fw
        T = self.T
        fw.begin_stage()
        self.load_consts()
        c = self.c
        ident64 = c[0:64, C_ID:C_ID + 64]
        m4 = lambda col: bcv(View(c, c.ap[0:64, col:col + 64].unsqueeze(1)), [64, 4, 64])
        cw = fw.sb("g_cw", [64, 4, 1536], F32)
        for i in range(4):
            fw.dma(cw[:, i, :], rowbc(self.gdn_conv_w, l * 4 + i, 0, 1536, 64))
        nA = fw.sb("g_nA", [64, 4], F32)
        fw.dma(nA.v(), rowbc(self.gdn_A_log, l, 0, 4, 64))
        fw.act(nA.v(), nA.v(), AF.Exp)
        fw.ts(nA.v(), nA.v(), -1.0, ALU.mult)
        dtb = fw.sb("g_dtb", [64, 4], F32)
        fw.dma(dtb.v(), rowbc(self.gdn_dt_bias, l, 0, 4, 64))
        nw = fw.sb("g_nw", [64, 128], F32)
        fw.dma(nw.v(), rowbc(self.gdn_norm_w, l, 0, 128, 64))
        S = fw.sb("g_S", [128, 4, 128], F32R)
        fw.ts(S.v().re("p h n -> p (h n)"), c[:, 0:512], 0.0, ALU.mult)
        xs = [fw.sb("g_x%d" % i, [64, 1536], F32) for i in range(4)]
        zab = [fw.sb("g_zab%d" % b, [64, 520], F32) for b in range(2)]
        R32s = ("vb", "kbg", "kdec", "vn")
        dbl_s = ("vb", "kbg", "kdec")
        names = ["kb", "vb", "kbg", "kdec", "u", "vn", "o", "t1"]
        s2 = {nm: [fw.sb("g_%s%d" % (nm, i), [64, 512], F32R if nm in R32s else F32) for i in range(2 if nm in dbl_s else 1)] for nm in names}
        qkv = fw.sb("g_qkv", [64, 1536], F32)
        sq = fw.sb("g_sq", [64, 1024], F32)
        sm = {nm: fw.sb("g_" + nm, [64, 8], F32) for nm in ["ss", "g", "beta", "gc", "eg", "egl", "sp", "oss"]}
        egl128s = [fw.sb("g_egl128_%d" % i, [128, 4], F32) for i in range(2)]
        R32m = ("X1", "XT1", "X2", "XT2", "TT", "attnT")
        dbl_m = ("Egt", "ETlt", "ETle")
        M2 = {nm: [fw.sb("g_%s%d" % (nm, i), [64, 4, 64], F32R if nm in R32m else F32) for i in range(2 if nm in dbl_m else 1)]
              for nm in ["dg", "tmp", "E", "ET", "Egt", "ETlt", "ETle", "X1", "XT1", "X2", "XT2", "TT", "attnT"]}
        dbl_f = ("KT", "QT", "KBT", "QgT")
        F2 = {nm: [fw.sb("g_%s%d" % (nm, i), [128, 4, 64], F32 if nm == "EGR" else F32R) for i in range(2 if nm in dbl_f else 1)]
              for nm in ["KT", "QT", "KBT", "QgT", "wT", "EGR"]}
        ps = [fw.ps("g_ps%d" % i, [128, 512], F32) for i in range(8)]
        P4 = lambda i: View(ps[i], ps[i].ap[0:64, 0:256].rearrange("p (h n) -> p h n", h=4))
        F4 = lambda i, hf=0: View(ps[i], ps[i].ap[:, hf * 256:(hf + 1) * 256].rearrange("p (h n) -> p h n", h=4))
        T4 = lambda i: View(ps[i], ps[i].ap[0:64, :].rearrange("p (h n) -> p h n", h=4))
        h4 = lambda v: v.re("p (h n) -> p h n", h=4)
        col = lambda t, n: bcv(View(t.tile, t.ap.unsqueeze(2)), [64, t.ap.shape[1], n])
        sel = lambda d, i: {k_: v_[i % len(v_)] for k_, v_ in d.items()}
        nch = T // 64

        def phaseA(ch):
            b = ch % 2
            t0 = ch * 64
            X = xs
            s = sel(s2, b)
            M = sel(M2, b)
            F = sel(F2, b)
            egl128 = egl128s[b]
            for i in range(4):
                sh = 3 - i
                if t0 - sh < 0:
                    fw.memset(X[i].v(), 0.0)
                    fw.dma(X[i][sh:64, :], self.P[0:64 - sh, GDN0:GDN0 + 1536])
                else:
                    fw.dma(X[i].v(), self.P[t0 - sh:t0 - sh + 64, GDN0:GDN0 + 1536])
            fw.dma(zab[b].v(), self.P[t0:t0 + 64, GDN0 + 1536:GDN0 + 2056])
            a_ = zab[b][:, 512:516]
            b_ = zab[b][:, 516:520]
            yield
            acc = qkv
            fw.tt(acc.v(), X[0].v(), cw[:, 0, :], ALU.mult)
            yield
            for i in range(1, 4):
                fw.tt(X[i].v(), X[i].v(), cw[:, i, :], ALU.mult)
                yield
                fw.tt(acc.v(), acc.v(), X[i].v(), ALU.add)
                yield
            fw.act(qkv.v(), acc.v(), AF.Silu)
            yield
            fw.tt(sq.v(), qkv[:, 0:1024], qkv[:, 0:1024], ALU.mult)
            yield
            fw.reduce(sm["ss"].v(), sq.v().re("p (h n) -> p h n", h=8))
            fw.act(sm["ss"].v(), sm["ss"].v(), AF.Ln, bias=EPS)
            fw.act(sm["ss"].v(), sm["ss"].v(), AF.Exp, scale=-0.5)
            fw.ts(sm["ss"][:, 0:4], sm["ss"][:, 0:4], 128.0 ** -0.5, ALU.mult)
            yield
            qk3 = qkv[:, 0:1024].re("p (h n) -> p h n", h=8)
            fw.tt(qk3, qk3, col(sm["ss"].v(), 128), ALU.mult)
            yield
            qn, kn, vv = qkv[:, 0:512], qkv[:, 512:1024], qkv[:, 1024:1536]
            fw.act(sm["beta"][:, 0:4], b_, AF.Sigmoid)
            fw.tt(sm["sp"][:, 0:4], a_, dtb.v(), ALU.add)
            fw.act(sm["sp"][:, 0:4], sm["sp"][:, 0:4], AF.Exp)
            fw.act(sm["sp"][:, 0:4], sm["sp"][:, 0:4], AF.Ln, bias=1.0)
            fw.tt(sm["g"][:, 0:4], sm["sp"][:, 0:4], nA.v(), ALU.mult)
            yield
            g4 = sm["g"][:, 0:4]
            beta = sm["beta"][:, 0:4]
            gcp = View(ps[0], ps[0].ap[0:64, 256:260])
            glp = View(ps[0], ps[0].ap[:, 260:264])
            fw.mm(gcp, c[0:64, C_LE:C_LE + 64], g4)
            fw.mm(glp, c[0:64, C_ONE:C_ONE + 128], g4)
            yield
            gc = sm["gc"][:, 0:4]
            fw.copy(gc, gcp, eng="dve")
            fw.act(sm["eg"][:, 0:4], gcp, AF.Exp)
            fw.tt(sm["egl"][:, 0:4], glp[0:64, :], gc, ALU.subtract)
            fw.act(sm["egl"][:, 0:4], sm["egl"][:, 0:4], AF.Exp)
            fw.act(egl128.v(), glp, AF.Exp)
            yield
            for h in range(4):
                fw.ts(M["dg"][:, h, :], ident64, sm["gc"][:, h:h + 1], ALU.mult)
            yield
            fw.mm(ps[1][:, 0:256], c[0:64, C_ONE:C_ONE + 128], M["dg"].v().re("p h n -> p (h n)"))
            fw.ts(M["tmp"].v(), col(gc, 64), -1.0, ALU.mult)
            yield
            fw.act(F["EGR"].v(), F4(1), AF.Exp)
            tps = View(ps[1], ps[1].ap[0:64, 256:512])
            fw.mm(tps, c[0:64, C_ONE:C_ONE + 64], M["dg"].v().re("p h n -> p (h n)"), start=True, stop=False)
            fw.mm(tps, ident64, M["tmp"].v().re("p h n -> p (h n)"), start=False, stop=True)
            tps3 = tps.re("p (h n) -> p h n", h=4)
            yield
            fw.ts(M["E"].v(), tps3, 0.0, ALU.max)
            fw.act(M["E"].v(), M["E"].v(), AF.Exp, scale=-1.0)
            yield
            fw.ts(M["ET"].v(), tps3, 0.0, ALU.min)
            fw.act(M["ET"].v(), M["ET"].v(), AF.Exp)
            yield
            fw.tt(M["Egt"].v(), M["E"].v(), m4(C_NGT), ALU.mult)
            yield
            fw.tt(M["ETlt"].v(), M["ET"].v(), m4(C_NLT), ALU.mult)
            yield
            fw.tt(M["ETle"].v(), M["ET"].v(), m4(C_LE), ALU.mult)
            yield
            fw.tt(h4(s["kb"].v()), h4(kn), col(beta, 128), ALU.mult)
            yield
            fw.tt(h4(s["vb"].v()), h4(vv), col(beta, 128), ALU.mult)
            yield
            fw.tt(h4(s["kbg"].v()), h4(s["kb"].v()), col(sm["eg"][:, 0:4], 128), ALU.mult)
            yield
            fw.tt(h4(s["kdec"].v()), h4(kn), col(sm["egl"][:, 0:4], 128), ALU.mult)
            yield
            for h in range(4):
                hs_ = slice(h * 128, (h + 1) * 128)
                fw.transpose(F4(2, 0)[:, h, :], kn[:, hs_], ident64)
                fw.transpose(F4(2, 1)[:, h, :], qn[:, hs_], ident64)
            yield
            fw.copy(F["KT"].v(), F4(2, 0), eng="dve")
            fw.copy(F["QT"].v(), F4(2, 1), eng="act")
            yield
            fw.tt(F["QgT"].v(), F4(2, 1), F["EGR"].v(), ALU.mult)
            for h in range(4):
                hs_ = slice(h * 128, (h + 1) * 128)
                fw.transpose(F4(0, 0)[:, h, :], s["kb"][:, hs_], ident64)
            yield
            fw.copy(F["KBT"].v(), F4(0, 0), eng="act")
            yield

        def phaseBC(ch):
            b = ch % 2
            t0 = ch * 64
            s = sel(s2, b)
            M = sel(M2, b)
            F = sel(F2, b)
            egl128 = egl128s[b]
            z = zab[b][:, 0:512]
            for h in range(4):
                fw.mm(P4(3)[:, h, :], F["KBT"][:, h, :], F["KT"][:, h, :])
                fw.mm(P4(4)[:, h, :], F["KT"][:, h, :], F["KBT"][:, h, :])
                fw.mm(P4(5)[:, h, :], F["KT"][:, h, :], F["QT"][:, h, :])
            yield
            fw.tt(M["X1"].v(), P4(3), M["Egt"].v(), ALU.mult)
            yield
            fw.tt(M["XT1"].v(), P4(4), M["ETlt"].v(), ALU.mult)
            yield
            fw.tt(M["attnT"].v(), P4(5), M["ETle"].v(), ALU.mult)
            yield
            fw.tt(M["TT"].v(), M["XT1"].v(), m4(C_ID), ALU.add)
            yield
            yield from self.inv_chain(4, M["X1"], M["XT1"], M["TT"], M["X2"], M["XT2"], P4(3), P4(4), P4(5), gen=True)
            TT = M["TT"]
            for h in range(4):
                hs_ = slice(h * 128, (h + 1) * 128)
                fw.mm(T4(6)[:, h, :], TT[:, h, :], s["vb"][:, hs_])
                fw.mm(F4(7)[:, h, :], s["kbg"][:, hs_], TT[:, h, :])
            yield
            fw.copy(s["u"].v(), ps[6][0:64, :], eng="dve")
            fw.copy(F["wT"].v(), F4(7), eng="act")
            yield
            for h in range(4):
                fw.mm(T4(3)[:, h, :], F["wT"][:, h, :], S[:, h, :])
            yield
            fw.tt(s["vn"].v(), s["u"].v(), ps[3][0:64, :], ALU.subtract)
            yield
            for h in range(4):
                hs_ = slice(h * 128, (h + 1) * 128)
                fw.mm(ps[5][:, hs_], s["kdec"][:, hs_], s["vn"][:, hs_])
            for h in range(4):
                hs_ = slice(h * 128, (h + 1) * 128)
                fw.mm(T4(4)[:, h, :], F["QgT"][:, h, :], S[:, h, :], start=True, stop=False)
                fw.mm(T4(4)[:, h, :], M["attnT"][:, h, :], s["vn"][:, hs_], start=False, stop=True)
            yield
            for h in range(4):
                hs_ = slice(h * 128, (h + 1) * 128)
                fw.stt(S[:, h, :], S[:, h, :], egl128[:, h:h + 1], ps[5][:, hs_], ALU.mult, ALU.add)
                yield
            o = s["o"]
            fw.copy(o.v(), ps[4][0:64, :], eng="act")
            yield
            fw.tt(s["t1"].v(), o.v(), o.v(), ALU.mult)
            yield
            fw.reduce(sm["oss"][:, 0:4], h4(s["t1"].v()))
            fw.act(sm["oss"][:, 0:4], sm["oss"][:, 0:4], AF.Ln, bias=EPS, scale=1.0 / 128)
            fw.act(sm["oss"][:, 0:4], sm["oss"][:, 0:4], AF.Exp, scale=-0.5)
            yield
            fw.tt(h4(o.v()), h4(o.v()), col(sm["oss"][:, 0:4], 128), ALU.mult)
            yield
            fw.tt(h4(o.v()), h4(o.v()), bcv(View(nw, nw.ap.unsqueeze(1)), [64, 4, 128]), ALU.mult)
            fw.act(s["t1"].v(), z, AF.Silu)
            yield
            fw.tt(o.v(), o.v(), s["t1"].v(), ALU.mult)
            fw.dma(self.MIX[t0:t0 + 64, 512:1024].k(("g", ch)), o.v())
            yield

        for _ in phaseA(0):
            pass
        for ch in range(nch):
            gens = [phaseBC(ch)]
            if ch + 1 < nch:
                gens.append(phaseA(ch + 1))
            iln = int(os.environ.get('GDN_ILN', '0'))
            if len(gens) == 2 and iln > 0:
                gb, ga = gens
                na = 0
                a_alive = b_alive = True
                while a_alive and na < iln:
                    if b_alive:
                        try:
                            next(gb)
                        except StopIteration:
                            b_alive = False
                    try:
                        next(ga)
                        na += 1
                    except StopIteration:
                        a_alive = False
                for _ in gb:
                    pass
                for _ in ga:
                    pass
            else:
                for g_ in gens:
                    for _ in g_:
                        pass
        fw.end_stage()

    def attn(self, l):
        fw = self.fw
        T = self.T
        NT = T // 128
        QG = min(T, 512)
        NQB = QG // 128
        lam_init = 0.8 - 0.6 * math.exp(-0.3 * l)
        fw.begin_stage()
        self.load_consts(bf=True)
        c = self.c
        cm = fw.sb("a_cm", [128, 4, 512], BF16)
        fw.dma(cm.v().re("p r n -> p (r n)"), self.cmask.v(), q="pool")
        onesb = fw.sb("a_ones", [128, 1], BF16)
        fw.memset(onesb.v(), 1.0)
        qnw = fw.sb("a_qnw", [128, 64], F32)
        fw.dma(qnw.v(), rowbc(self.diff_q_norm_w, l, 0, 64, 128))
        fw.ts(qnw.v(), qnw.v(), 0.125, ALU.mult)
        knw = fw.sb("a_knw", [128, 64], F32)
        fw.dma(knw.v(), rowbc(self.diff_k_norm_w, l, 0, 64, 128))
        subw = fw.sb("a_subw", [128, 128], F32)
        fw.dma(subw.v(), rowbc(self.diff_subln_w, l, 0, 128, 128))
        fw.ts(subw.v(), subw.v(), 1.0 - lam_init, ALU.mult)
        lv = fw.sb("a_lv", [128, 4, 64], F32)
        for i, dt_ in enumerate([self.diff_lambda_q1, self.diff_lambda_k1, self.diff_lambda_q2, self.diff_lambda_k2]):
            fw.dma(lv[:, i, :], rowbc(dt_, l, 0, 64, 128))
        ls = fw.sb("a_ls", [128, 4], F32)
        fw.tt(lv[:, 0, :], lv[:, 0, :], lv[:, 1, :], ALU.mult)
        fw.tt(lv[:, 2, :], lv[:, 2, :], lv[:, 3, :], ALU.mult)
        fw.reduce(ls[:, 0:1], lv[:, 0, :])
        fw.reduce(ls[:, 1:2], lv[:, 2, :])
        fw.act(ls[:, 0:2], ls[:, 0:2], AF.Exp)
        fw.tt(ls[:, 2:3], ls[:, 1:2], ls[:, 0:1], ALU.subtract)
        fw.ts(ls[:, 3:4], ls[:, 2:3], -lam_init, ALU.add)
        nlam = ls[:, 3:4]
        qT = fw.sb("a_qT", [64, 16, T], BF16)
        kT = fw.sb("a_kT", [64, 16, T], BF16)
        vx = fw.sb("a_vx", [128, NT, 8, 132], BF16)
        fw.memset(vx.v(), 1.0)
        xin = [fw.sb("a_xin%d" % i, [128, 1024], F32) for i in range(2)]
        xsq = fw.sb("a_xsq", [128, 1024], F32)
        xn = [fw.sb("a_xn%d" % i, [128, 1024], BF16) for i in range(2)]
        ss = fw.sb("a_ss", [128, 16], F32)
        ps = [fw.ps("a_ps%d" % i, [128, 512], F32) for i in range(8)]
        col16 = lambda t: bcv(View(t.tile, t.ap.unsqueeze(2)), [128, 16, 64])
        nrm = 0
        for t in range(NT):
            rows = slice(t * 128, (t + 1) * 128)
            for which, (c0, wt_, dstT) in enumerate([(DIFF0, qnw, qT), (DIFF0 + 1024, knw, kT)]):
                xi = xin[nrm % 2]
                xo = xn[nrm % 2]
                nrm += 1
                fw.dma(xi.v(), self.P[rows, c0:c0 + 1024])
                fw.tt(xsq.v(), xi.v(), xi.v(), ALU.mult)
                fw.reduce(ss.v(), xsq.v().re("p (g n) -> p g n", g=16))
                fw.act(ss.v(), ss.v(), AF.Sqrt, bias=EPS, scale=1.0 / 64)
                fw.recip(ss.v(), ss.v())
                x3 = xi.v().re("p (g n) -> p g n", g=16)
                fw.tt(x3, x3, col16(ss.v()), ALU.mult)
                fw.tt(xo.v().re("p (g n) -> p g n", g=16), x3, bcv(View(wt_, wt_.ap.unsqueeze(1)), [128, 16, 64]), ALU.mult)
                for half in range(2):
                    pt = View(ps[half], ps[half].ap[0:64, :].bitcast(BF16)[:, 0:1024].rearrange("p (g n) -> p g n", g=8))
                    for g in range(8):
                        gg = half * 8 + g
                        fw.transpose(pt[:, g, :], xo[:, gg * 64:(gg + 1) * 64], self.idb.v())
                    fw.copy(dstT[:, half * 8:(half + 1) * 8, t * 128:(t + 1) * 128], pt, eng=self.ev())
            xi = xin[nrm % 2]
            nrm += 1
            fw.dma(xi.v(), self.P[rows, DIFF0 + 2048:DIFF0 + 3072])
            fw.copy(vx[:, t, :, 0:128], xi.v().re("p (h e) -> p h e", h=8), eng="act")
        pex = [fw.sb("a_pex%d" % i, [128, 512], BF16) for i in range(4)]
        NBK = max(1, NQB // 2)
        osb = [fw.sb("a_osb%d" % i, [128, 2, NBK, 264], F32) for i in range(2)]
        eo = [fw.sb("a_eo%d" % i, [128, NQB, 128], F32) for i in range(2)]
        et = fw.sb("a_et", [128, NQB, 128], F32)
        esm = fw.sb("a_esm", [128, 4, NQB], F32)
        npx = 0
        nst = 0
        nq = 0
        for h in range(8):
            for qg in range(T // QG):
                nkb = (qg + 1) * NQB
                ob_sb = osb[nq % 2]
                o_ = eo[nq % 2]
                nq += 1
                items = [(m, kb) for m in range(2) for kb in range(nkb)]
                sts = {}

                def emit_st(i):
                    nonlocal nst
                    m, kb = items[i]
                    hm = h * 2 + m
                    st = ps[(0, 1, 6, 7)[nst % 4]]
                    nst += 1
                    r = kb - qg * NQB
                    fw.mm(st[:, 0:QG], kT[:, hm, kb * 128:(kb + 1) * 128], qT[:, hm, qg * QG:(qg + 1) * QG], start=True, stop=(r < 0))
                    if r >= 0:
                        fw.mm(st[:, 0:QG], self.idb.v(), cm[:, r, 0:QG], start=False, stop=True)
                    sts[i] = st

                PD = 3
                for i0 in range(min(PD, len(items))):
                    emit_st(i0)
                for i, (m, kb) in enumerate(items):
                    if i + PD < len(items):
                        emit_st(i + PD)
                    st = sts.pop(i)
                    px = pex[npx % 4]
                    npx += 1
                    fw.act(px[:, 0:QG], st[:, 0:QG], AF.Exp)
                    for qb in range(NQB):
                        if kb <= qg * NQB + qb:
                            ob_ = ps[2 + m * 2 + qb // 2]
                            oc = (qb % 2) * 132
                            fw.mm(ob_[:, oc:oc + 129], px[:, qb * 128:(qb + 1) * 128], vx[:, kb, h, 0:129],
                                  start=(kb == 0 and qb % 2 == 0), stop=(kb == qg * NQB + qb), skip_group_check=True)
                    if kb == nkb - 1:
                        for bk in range(NBK):
                            fw.copy(ob_sb[:, m, bk, :], ps[2 + m * 2 + bk][:, 0:264], eng=("dve" if bk % 2 == 0 else "act"))
                O = [ob_sb[:, m].re("p b (t c) -> p (b t) c", t=2) for m in range(2)]
                O = [o[:, 0:NQB, :] for o in O]
                rs = [o[:, :, 128:129].re("p q c -> p (q c)") for o in O]
                fw.recip(esm[:, 0, :], rs[0])
                fw.recip(esm[:, 1, :], rs[1])
                fw.ts(esm[:, 1, :], esm[:, 1, :], nlam, ALU.mult)
                bq = lambda v: bcv(View(v.tile, v.ap.unsqueeze(2)), [128, NQB, 128])
                fw.tt(o_.v(), O[0][:, :, 0:128], bq(esm[:, 0, :]), ALU.mult)
                fw.tt(et.v(), O[1][:, :, 0:128], bq(esm[:, 1, :]), ALU.mult)
                fw.tt(o_.v(), o_.v(), et.v(), ALU.add)
                fw.tt(et.v(), o_.v(), o_.v(), ALU.mult)
                fw.reduce(esm[:, 2, :], et.v())
                fw.act(esm[:, 2, :], esm[:, 2, :], AF.Ln, bias=EPS, scale=1.0 / 128)
                fw.act(esm[:, 2, :], esm[:, 2, :], AF.Exp, scale=-0.5)
                fw.tt(o_.v(), o_.v(), bq(esm[:, 2, :]), ALU.mult)
                fw.tt(o_.v(), o_.v(), bcv(View(subw, subw.ap.unsqueeze(1)), [128, NQB, 128]), ALU.mult)
                q0 = qg * QG
                dst = self.MIX[q0:q0 + QG, 1024 + h * 128:1024 + (h + 1) * 128].re("(q p) e -> p q e", p=128)
                fw.dma(dst.k(("a", h, qg)), o_.v())
        fw.end_stage()

    def build(self, stages=None):
        for l in range(self.L):
            for nm in ["in_proj", "rwkv", "gdn", "attn", "out_proj", "ffn"]:
                if stages is None or (l, nm) in stages:
                    getattr(self, nm)(l)
        self.fw.finish()
        return self.nc


PARAM_SHAPES2D = None


def make_inputs(inputs, b, T):
    L = 2
    r = lambda a, shp: np.ascontiguousarray(np.asarray(a, dtype=np.float32).reshape(shp))
    d = {}
    d["x"] = r(inputs["x"][b, :T], (T, D))
    d["attn_norm_w"] = r(inputs["attn_norm_w"], (L, D))
    d["w_in"] = r(inputs["w_in"], (L * D, NIN))
    d["w_vres_a"] = r(inputs["w_vres_a"], (D, 32))
    d["mu_rwkv"] = r(inputs["mu_rwkv"], (L, 1792))
    d["mu_vres"] = r(inputs["mu_vres"], (1, 32))
    d["rwkv_w0"] = r(inputs["rwkv_w0"], (L, 512))
    d["rwkv_w_lora_b"] = r(inputs["rwkv_w_lora_b"], (L * 64, 512))
    d["rwkv_a0"] = r(inputs["rwkv_a0"], (L, 512))
    d["rwkv_a_lora_b"] = r(inputs["rwkv_a_lora_b"], (L * 64, 512))
    d["rwkv_g_lora_b"] = r(inputs["rwkv_g_lora_b"], (L * 128, 512))
    d["rwkv_v0"] = r(inputs["rwkv_v0"], (1, 512))
    d["rwkv_v_lora_b"] = r(inputs["rwkv_v_lora_b"], (32, 512))
    for nm in ["rwkv_k_k", "rwkv_k_a", "rwkv_r_k", "rwkv_ln_w", "rwkv_ln_b"]:
        d[nm] = r(inputs[nm], (L, 512))
    d["gdn_conv_w"] = r(inputs["gdn_conv_w"], (L * 4, 1536))
    d["gdn_A_log"] = r(inputs["gdn_A_log"], (L, 4))
    d["gdn_dt_bias"] = r(inputs["gdn_dt_bias"], (L, 4))
    d["gdn_norm_w"] = r(inputs["gdn_norm_w"], (L, 128))
    for nm in ["diff_q_norm_w", "diff_k_norm_w", "diff_lambda_q1", "diff_lambda_k1", "diff_lambda_q2", "diff_lambda_k2"]:
        d[nm] = r(inputs[nm], (L, 64))
    d["diff_subln_w"] = r(inputs["diff_subln_w"], (L, 128))
    d["w_out"] = r(inputs["w_out"], (L * D, D))
    d["ffn_norm_w"] = r(inputs["ffn_norm_w"], (L, D))
    d["w_ffn_in"] = r(inputs["w_ffn_in"], (L * D, 2 * FH))
    d["w_ffn_out"] = r(inputs["w_ffn_out"], (L * FH, D))
    c, cm = make_consts()
    d["consts"] = c
    d["cmask"] = cm
    return d


_CACHE = {}


def kernel(**inputs):
    from concourse.bass_utils import run_bass_kernel_spmd
    T = 2048
    n = 8
    if "nc" not in _CACHE:
        _CACHE["nc"] = Prog(T, debug=False).build()
    nc = _CACHE["nc"]
    in_maps = [make_inputs(inputs, b, T) for b in range(n)]
    res = run_bass_kernel_spmd(nc, in_maps, core_ids=list(range(n)))
    out = np.stack([np.asarray(res.results[b]["y"], dtype=np.float32) for b in range(n)], axis=0)
    return out
```

```python
import numpy as np
from contextlib import ExitStack
import concourse.bass as bass
import concourse.mybir as mybir

F32 = mybir.dt.float32
BF16 = mybir.dt.bfloat16
ALU = mybir.AluOpType
AF = mybir.ActivationFunctionType
AX = mybir.AxisListType

COMPUTE = ("pe", "dve", "act", "pool")
DMA_RING = 8


class View:
    __slots__ = ("tile", "ap", "key")

    def __init__(self, tile, ap, key=None):
        self.tile = tile
        self.ap = ap
        self.key = key

    def __getitem__(self, idx):
        return View(self.tile, self.ap[idx], self.key)

    def k(self, key):
        return View(self.tile, self.ap, key)

    def re(self, s, **kw):
        return View(self.tile, self.ap.rearrange(s, **kw), self.key)

    def bc(self, shape):
        return View(self.tile, self.ap.to_broadcast(shape), self.key)

    def bitcast(self, dt):
        return View(self.tile, self.ap.bitcast(dt), self.key)

    @property
    def shape(self):
        return self.ap.shape

    def v(self, key=None):
        return self if key is None else View(self.tile, self.ap, key)


class Buf:
    def __init__(self, name, ap):
        self.name = name
        self.ap = ap
        self.state = {None: [None, []]}

    def __getitem__(self, idx):
        return View(self, self.ap[idx], None)

    def v(self, key=None):
        return View(self, self.ap, key)

    def reset(self):
        self.state = {None: [None, []]}


class Op:
    __slots__ = ("eng", "emit", "deps", "sig", "sigval", "is_dma", "dsem", "dval", "dprev", "idx", "label")

    def __init__(self, eng, emit, is_dma=False, label=""):
        self.eng = eng
        self.emit = emit
        self.deps = {}
        self.sig = False
        self.sigval = 0
        self.is_dma = is_dma
        self.dsem = None
        self.dval = 0
        self.dprev = 0
        self.label = label


class FW:
    def __init__(self, nc, same_engine_sync=True):
        self.nc = nc
        self.es = ExitStack()
        self.engs = {"pe": nc.tensor, "dve": nc.vector, "act": nc.scalar, "pool": nc.gpsimd, "sp": nc.sync}
        self.sems = {}
        for e in self.engs:
            self.sems[e] = self.es.enter_context(nc.semaphore("s_" + e))
        self.dma_rings = {}
        self.dma_count = {}
        for q in ("sp", "pool", "act"):
            self.dma_rings[q] = [self.es.enter_context(nc.semaphore("d_%s%d" % (q, i))) for i in range(DMA_RING)]
            self.dma_count[q] = 0
        self.ops = []
        self.tiles = []
        self.stage_es = None
        self.same_engine_sync = same_engine_sync
        self.last_op = {e: None for e in self.engs}
        self.pending_dma = []
        self.n_emitted = 0

    def begin_stage(self):
        self.stage_es = ExitStack()

    def sb(self, name, shape, dtype=F32):
        self.uid = getattr(self, "uid", 0) + 1
        name = "%s_u%d" % (name, self.uid)
        t = self.stage_es.enter_context(self.nc.sbuf_tensor(name, list(shape), dtype))
        tl = Buf(name, t.ap() if hasattr(t, "ap") and callable(getattr(t, "ap")) else t[:])
        self.tiles.append(tl)
        return tl

    def ps(self, name, shape, dtype=F32):
        self.uid = getattr(self, "uid", 0) + 1
        name = "%s_u%d" % (name, self.uid)
        t = self.stage_es.enter_context(self.nc.psum_tensor(name, list(shape), dtype))
        tl = Buf(name, t.ap() if hasattr(t, "ap") and callable(getattr(t, "ap")) else t[:])
        self.tiles.append(tl)
        return tl

    def dram(self, name, shape, dtype=F32, kind="Internal"):
        t = self.nc.dram_tensor(name, list(shape), dtype, kind=kind)
        tl = Buf(name, t.ap())
        self.tiles.append(tl)
        return tl

    def _keys(self, v):
        st = v.tile.state
        if v.key is None:
            return list(st.keys())
        if v.key not in st:
            st[v.key] = [None, []]
        return [v.key, None]

    def _read(self, op, v):
        st = v.tile.state
        for k in self._keys(v):
            w = st[k][0]
            if w is not None and w is not op:
                op.deps[w] = True
        st[v.key][1].append(op)

    def _write(self, op, v):
        st = v.tile.state
        for k in self._keys(v):
            w, rs = st[k]
            if w is not None and w is not op:
                op.deps.setdefault(w, False)
            for r in rs:
                if r is not op:
                    op.deps.setdefault(r, False)
        if v.key is None:
            for k in list(st.keys()):
                st[k] = [op, []]
        else:
            st[v.key] = [op, []]

    def add(self, eng, emit, reads, writes, is_dma=False, label=""):
        op = Op(eng, emit, is_dma, label)
        for v in reads:
            self._read(op, v)
        for v in writes:
            self._write(op, v)
        if is_dma:
            j = self.dma_count[eng]
            self.dma_count[eng] = j + 1
            op.dsem = self.dma_rings[eng][j % DMA_RING]
            op.dval = 16 * (j // DMA_RING + 1)
            op.dprev = 16 * (j // DMA_RING)
            self.pending_dma.append(op)
        self.ops.append(op)
        self.last_op[eng] = op
        return op

    def barrier(self):
        lasts = [o for o in self.last_op.values() if o is not None and not o.is_dma]
        comp_last = {}
        for o in self.ops:
            if not o.is_dma and o.emit is not None:
                comp_last[o.eng] = o
        for e in self.engs:
            b = Op(e, None, False, "barrier")
            for e2, o in comp_last.items():
                if e2 != e:
                    b.deps[o] = True
            for d in self.pending_dma:
                b.deps[d] = True
            self.ops.append(b)
        self.pending_dma = []
        for t in self.tiles:
            t.reset()

    def end_stage(self):
        self.barrier()
        self.flush()
        self.stage_es.close()
        self.stage_es = None
        self.tiles = [t for t in self.tiles if t.name.startswith("D_")]

    def flush(self):
        ops = self.ops
        if not hasattr(self, "known"):
            self.known = {e: {} for e in self.engs}
            self.sigcount = {e: 0 for e in self.engs}
        for op in ops:
            for d, raw in op.deps.items():
                if d.is_dma:
                    continue
                if d.eng != op.eng:
                    d.sig = True
                elif self.same_engine_sync and raw and d.eng in ("dve", "act", "pool") and not op.is_dma:
                    d.sig = True
                elif op.is_dma and d.eng == op.eng:
                    d.sig = True
        for op in ops:
            e = op.eng
            eng = self.engs[e]
            kn = self.known[e]
            waits = {}
            for d, raw in op.deps.items():
                if d.is_dma:
                    s, val = d.dsem, d.dval
                elif d.sig and (d.eng != e or raw or op.is_dma):
                    s, val = self.sems[d.eng], d.sigval
                    assert val > 0, (d.label, op.label)
                else:
                    continue
                if kn.get(s, 0) < val:
                    waits[s] = max(waits.get(s, 0), val)
            if op.is_dma and op.dprev > 0:
                if kn.get(op.dsem, 0) < op.dprev:
                    waits[op.dsem] = max(waits.get(op.dsem, 0), op.dprev)
            for s, val in waits.items():
                eng.wait_ge(s, val)
                kn[s] = val
            if op.emit is None:
                continue
            ins = op.emit(eng)
            self.n_emitted += 1
            if op.is_dma:
                ins.then_inc(op.dsem, 16)
            elif op.sig:
                self.sigcount[e] += 1
                op.sigval = self.sigcount[e]
                ins.then_inc(self.sems[e], 1)
        self.ops = []

    def dma(self, out, in_, q="sp", **kw):
        o, i = out.ap, in_.ap
        return self.add(q, lambda eng: eng.dma_start(out=o, in_=i, **kw), [in_], [out], is_dma=True, label="dma")

    def mm(self, out, lhsT, rhs, start=True, stop=True, **kw):
        o, l, r = out.ap, lhsT.ap, rhs.ap
        return self.add("pe", lambda eng: eng.matmul(o, l, r, start=start, stop=stop, **kw), [lhsT, rhs], [out], label="mm")

    def transpose(self, out, in_, ident):
        o, i, d = out.ap, in_.ap, ident.ap
        return self.add("pe", lambda eng: eng.transpose(o, i, d), [in_, ident], [out], label="tr")

    def act(self, out, in_, func, bias=None, scale=None, accum=None, eng="act"):
        o, i = out.ap, in_.ap
        reads = [in_]
        kw = {}
        if bias is not None:
            if isinstance(bias, View):
                reads.append(bias)
                kw["bias"] = bias.ap
            else:
                kw["bias"] = float(bias)
        if scale is not None:
            if isinstance(scale, View):
                reads.append(scale)
                kw["scale"] = scale.ap
            else:
                kw["scale"] = float(scale)
        writes = [out]
        if accum is not None:
            writes.append(accum)
            kw["accum_out"] = accum.ap
        return self.add(eng, lambda en: en.activation(o, i, func, **kw), reads, writes, label="act")

    def tt(self, out, a, b, op, eng="dve"):
        o, x, y = out.ap, a.ap, b.ap
        return self.add(eng, lambda en: en.tensor_tensor(o, x, y, op), [a, b], [out], label="tt")

    def ts(self, out, a, s1, op0, s2=None, op1=None, accum=None, eng="dve"):
        o, x = out.ap, a.ap
        reads = [a]
        if isinstance(s1, View):
            reads.append(s1)
            s1v = s1.ap
        else:
            s1v = float(s1)
        if isinstance(s2, View):
            reads.append(s2)
            s2v = s2.ap
        else:
            s2v = None if s2 is None else float(s2)
        writes = [out]
        kw = {}
        if op1 is not None:
            kw["op1"] = op1
        if accum is not None:
            writes.append(accum)
            kw["accum_out"] = accum.ap
        return self.add(eng, lambda en: en.tensor_scalar(o, x, s1v, s2v, op0, **kw), reads, writes, label="ts")

    def stt(self, out, a, s, b, op0, op1, accum=None, eng="dve"):
        o, x, y = out.ap, a.ap, b.ap
        reads = [a, b]
        if isinstance(s, View):
            reads.append(s)
            sv = s.ap
        else:
            sv = float(s)
        writes = [out]
        kw = {}
        if accum is not None:
            writes.append(accum)
            kw["accum_out"] = accum.ap
        return self.add(eng, lambda en: en.scalar_tensor_tensor(o, x, sv, y, op0, op1, **kw), reads, writes, label="stt")

    def copy(self, out, in_, eng="dve"):
        o, i = out.ap, in_.ap
        if eng == "act":
            return self.add("act", lambda en: en.copy(o, i), [in_], [out], label="copy")
        return self.add(eng, lambda en: en.tensor_copy(o, i), [in_], [out], label="copy")

    def memset(self, out, val, eng="dve"):
        o = out.ap
        return self.add(eng, lambda en: en.memset(o, val), [], [out], label="memset")

    def reduce(self, out, in_, op=ALU.add, axis=AX.X, eng="dve"):
        o, i = out.ap, in_.ap
        return self.add(eng, lambda en: en.tensor_reduce(o, i, axis, op), [in_], [out], label="reduce")

    def recip(self, out, in_):
        o, i = out.ap, in_.ap
        return self.add("dve", lambda en: en.reciprocal(o, i), [in_], [out], label="recip")

    def finish(self):
        self.barrier()
        self.flush()
        if self.stage_es is not None:
            self.stage_es.close()
        self.es.close()


import math
import os
F32R = mybir.dt.float32r

D = 2048
NIN = 6920
NP = 6952
FH = 5632
EPS = 1e-6
GDN0 = 1792
DIFF0 = 3848
C_ID = 0
C_LE = 128
C_LT = 192
C_GT = 256
C_GE = 320
C_ONE = 384
C_NLT = 512
C_NGT = 576
NCONST = 640


def make_consts():
    c = np.zeros((128, NCONST), np.float32)
    c[:, C_ID:C_ID + 128] = np.eye(128)
    i = np.arange(64)
    le = (i[:, None] <= i[None, :]).astype(np.float32)
    lt = (i[:, None] < i[None, :]).astype(np.float32)
    c[:64, C_LE:C_LE + 64] = le
    c[:64, C_LT:C_LT + 64] = lt
    c[:64, C_GT:C_GT + 64] = lt.T
    c[:64, C_GE:C_GE + 64] = le.T
    c[:, C_ONE:C_ONE + 128] = 1.0
    c[:64, C_NLT:C_NLT + 64] = -lt
    c[:64, C_NGT:C_NGT + 64] = -lt.T
    cm = np.zeros((128, 4, 512), np.float32)
    kp = np.arange(128)[:, None]
    q = np.arange(512)[None, :]
    for r in range(4):
        cm[:, r, :] = np.where(q >= r * 128 + kp, 0.0, -30000.0)
    return c, cm.reshape(128, 2048)


def run_interleaved(gens):
    alive = list(gens)
    rat = [int(x) for x in os.environ.get('IL_RATIO', '2,1').split(',')]
    while alive:
        for gi, g in enumerate(list(alive)):
            for _ in range(rat[min(gens.index(g), len(rat) - 1)]):
                try:
                    next(g)
                except StopIteration:
                    alive.remove(g)
                    break


def bcv(view, shape):
    return View(view.tile, view.ap.to_broadcast(list(shape)), view.key)


def rowbc(dt, row, c0, c1, parts):
    return View(dt, dt.ap[row:row + 1, c0:c1].to_broadcast([parts, c1 - c0]))


class Prog:
    def __init__(self, T, L=2, debug=False):
        self.T = T
        self.L = L
        self.debug = debug
        nc = bass.Bass("TRN2", target_bir_lowering=False)
        self.nc = nc
        fw = FW(nc)
        self.fw = fw
        I = lambda n, s: fw.dram(n, s, F32, kind="ExternalInput")
        self.x = I("x", [T, D])
        self.attn_norm_w = I("attn_norm_w", [L, D])
        self.w_in = I("w_in", [L * D, NIN])
        self.w_vres_a = I("w_vres_a", [D, 32])
        self.mu_rwkv = I("mu_rwkv", [L, 1792])
        self.mu_vres = I("mu_vres", [1, 32])
        self.rwkv_w0 = I("rwkv_w0", [L, 512])
        self.rwkv_w_lora_b = I("rwkv_w_lora_b", [L * 64, 512])
        self.rwkv_a0 = I("rwkv_a0", [L, 512])
        self.rwkv_a_lora_b = I("rwkv_a_lora_b", [L * 64, 512])
        self.rwkv_g_lora_b = I("rwkv_g_lora_b", [L * 128, 512])
        self.rwkv_v0 = I("rwkv_v0", [1, 512])
        self.rwkv_v_lora_b = I("rwkv_v_lora_b", [32, 512])
        self.rwkv_k_k = I("rwkv_k_k", [L, 512])
        self.rwkv_k_a = I("rwkv_k_a", [L, 512])
        self.rwkv_r_k = I("rwkv_r_k", [L, 512])
        self.rwkv_ln_w = I("rwkv_ln_w", [L, 512])
        self.rwkv_ln_b = I("rwkv_ln_b", [L, 512])
        self.gdn_conv_w = I("gdn_conv_w", [L * 4, 1536])
        self.gdn_A_log = I("gdn_A_log", [L, 4])
        self.gdn_dt_bias = I("gdn_dt_bias", [L, 4])
        self.gdn_norm_w = I("gdn_norm_w", [L, 128])
        self.diff_q_norm_w = I("diff_q_norm_w", [L, 64])
        self.diff_k_norm_w = I("diff_k_norm_w", [L, 64])
        self.diff_lambda_q1 = I("diff_lambda_q1", [L, 64])
        self.diff_lambda_k1 = I("diff_lambda_k1", [L, 64])
        self.diff_lambda_q2 = I("diff_lambda_q2", [L, 64])
        self.diff_lambda_k2 = I("diff_lambda_k2", [L, 64])
        self.diff_subln_w = I("diff_subln_w", [L, 128])
        self.w_out = I("w_out", [L * D, D])
        self.ffn_norm_w = I("ffn_norm_w", [L, D])
        self.w_ffn_in = I("w_ffn_in", [L * D, 2 * FH])
        self.w_ffn_out = I("w_ffn_out", [L * FH, D])
        self.consts = I("consts", [128, NCONST])
        self.cmask = I("cmask", [128, 2048])
        self.y = fw.dram("y", [T, D], F32, kind="ExternalOutput")
        okind = "ExternalOutput" if debug else "Internal"
        self.P = fw.dram("D_P", [T, NP], F32, kind=okind)
        self.MIX = fw.dram("D_MIX", [T, D], F32, kind=okind)
        self.VF = fw.dram("D_VF", [T, 512], F32, kind="Internal")
        self.rr = 0
        import os
        self.cut = int(os.environ.get('GDN_CUT', '0'))

    def ev(self):
        self.rr += 1
        return "dve" if self.rr % 2 else "act"

    def load_consts(self, bf=False):
        fw = self.fw
        c = fw.sb("consts_sb", [128, NCONST], F32)
        fw.dma(c.v(), self.consts.v(), q="sp")
        self.c = c
        if bf:
            idb = fw.sb("idb", [128, 128], BF16)
            fw.copy(idb.v(), c[:, C_ID:C_ID + 128])
            self.idb = idb

    def nt_alloc(self, normw_row, single=False):
        fw = self.fw
        a = {}
        nb = 1 if single else 2
        a["xts"] = [fw.sb("nt_x%d" % i, [128, D], F32) for i in range(nb)] * (3 - nb)
        a["hs"] = [fw.sb("nt_h%d" % i, [128, D], BF16) for i in range(nb)] * (3 - nb)
        a["ss"] = [fw.sb("nt_ss%d" % i, [128, 1], F32) for i in range(2)]
        a["pst"] = [fw.ps("nt_ps%d" % i, [128, 4, 128], BF16) for i in range(2)]
        a["wbc"] = None
        if normw_row is not None:
            a["wbc"] = fw.sb("nt_w", [128, D], F32)
            fw.dma(a["wbc"].v(), rowbc(normw_row[0], normw_row[1], 0, D, 128), q="sp")
        return a

    def norm_transpose(self, a, src, r0, nrows_tiles, hT, tcol0, xkeyf=None):
        fw = self.fw
        xts, hs, ss, pst, wbc = a["xts"], a["hs"], a["ss"], a["pst"], a["wbc"]
        for t in range(nrows_tiles):
            xt = xts[t % 2]
            h = hs[t % 2]
            s_ = ss[t % 2]
            rows = slice(r0 + t * 128, r0 + (t + 1) * 128)
            sv = src[rows, :]
            if xkeyf is not None:
                sv = sv.k(xkeyf(r0 // 128 + t))
            fw.dma(xt.v(), sv, q="sp")
            if wbc is not None:
                fw.memset(s_.v(), 0.0)
                fw.act(h.v(), xt.v(), AF.Square, accum=s_.v())
                fw.act(s_.v(), s_.v(), AF.Sqrt, bias=EPS, scale=1.0 / D)
                fw.recip(s_.v(), s_.v())
                fw.stt(h.v(), xt.v(), s_.v(), wbc.v(), ALU.mult, ALU.mult)
            else:
                fw.copy(h.v(), xt.v(), eng="dve")
            for g in range(4):
                pt = pst[g % 2]
                for j in range(4):
                    cc = g * 4 + j
                    fw.transpose(pt[:, j, :], h[:, cc * 128:(cc + 1) * 128], self.idb.v())
                fw.copy(hT[:, g * 4:(g + 1) * 4, tcol0 + t * 128: tcol0 + (t + 1) * 128].k(("t", tcol0 // 128 + t)), pt.v(), eng=self.ev())

    def proj_stage(self, src, normw_row, blocks, resid=None):
        fw = self.fw
        T = self.T
        fw.begin_stage()
        self.load_consts(bf=True)
        hT = fw.sb("hT", [128, 16, T], BF16)
        psm = [fw.ps("psm%d" % i, [128, 512], F32) for i in range(3)]
        nta = self.nt_alloc(normw_row)
        self.norm_transpose(nta, src, 0, T // 128, hT, 0)
        wts = [fw.sb("wt%d" % i, [128, 16, 512], BF16) for i in range(2)]
        obs = [fw.sb("ob%d" % i, [128, 512], F32) for i in range(3)]
        rbs = [fw.sb("rb%d" % i, [128, 512], F32) for i in range(3)] if resid is not None else None
        n = 0
        for bi, (wd, row0, col0, ncols, dst, dc0) in enumerate(blocks):
            wt = wts[bi % 2]
            wv = View(wd, wd.ap[row0:row0 + D, col0:col0 + ncols].rearrange("(c p) n -> p c n", p=128))
            fw.dma(wt[:, :, 0:ncols], wv, q="pool")
            for t in range(T // 128):
                pm = psm[n % 3]
                ob = obs[n % 3]
                for c in range(16):
                    fw.mm(pm[:, 0:ncols], hT[:, c, t * 128:(t + 1) * 128].k(("t", t)), wt[:, c, 0:ncols], start=(c == 0), stop=(c == 15))
                rows = slice(t * 128, (t + 1) * 128)
                if resid is not None:
                    rb = rbs[n % 3]
                    fw.dma(rb[:, 0:ncols], resid[rows, dc0:dc0 + ncols].k(t), q="sp")
                    fw.tt(ob[:, 0:ncols], pm[:, 0:ncols], rb[:, 0:ncols], ALU.add)
                else:
                    fw.copy(ob[:, 0:ncols], pm[:, 0:ncols], eng=self.ev())
                fw.dma(dst[rows, dc0:dc0 + ncols].k(t), ob[:, 0:ncols], q="sp")
                n += 1
        fw.end_stage()

    def in_proj(self, l):
        src = self.x if l == 0 else self.y
        blocks = []
        c = 0
        while c < NIN:
            n = min(512, NIN - c)
            blocks.append((self.w_in, l * D, c, n, self.P, c))
            c += n
        if l > 0:
            blocks.append((self.w_vres_a, 0, 0, 32, self.P, NIN))
        self.proj_stage(src, (self.attn_norm_w, l), blocks)

    def out_proj(self, l):
        src = self.x if l == 0 else self.y
        blocks = [(self.w_out, l * D, cb * 512, 512, self.y, cb * 512) for cb in range(4)]
        self.proj_stage(self.MIX, None, blocks, resid=src)

    def ffn(self, l):
        fw = self.fw
        T = self.T
        TB = min(T, 1024)
        NH = TB // 512 if TB >= 512 else 1
        HW = min(TB, 512)
        pes = ExitStack()
        self.uid_p = getattr(self, "uid_p", 0) + 1
        t_ = pes.enter_context(self.nc.sbuf_tensor("D_actT_%d" % self.uid_p, [128, 44, TB], BF16))
        actT = Buf("D_actT", t_[:])
        fw.tiles.append(actT)
        nblk = T // TB
        for tb in range(nblk):
            fw.begin_stage()
            self.load_consts(bf=True)
            hT = fw.sb("f_hT", [128, 16, TB], BF16)
            wg = [fw.sb("f_wg%d" % i, [128, 16, 128], BF16) for i in range(2)]
            wu = [fw.sb("f_wu%d" % i, [128, 16, 128], BF16) for i in range(2)]
            psgu = [(fw.ps("f_psg%d" % i, [128, 512], F32), fw.ps("f_psu%d" % i, [128, 512], F32)) for i in range(3)]
            sg = [fw.sb("f_sg%d" % i, [128, 512], F32) for i in range(2)]
            nta = self.nt_alloc((self.ffn_norm_w, l), single=True)
            self.norm_transpose(nta, self.y, tb * TB, TB // 128, hT, 0, xkeyf=lambda i: i)
            n = 0
            r0 = l * D
            for j in range(44):
                g_ = wg[j % 2]
                u_ = wu[j % 2]
                fw.dma(g_.v(), View(self.w_ffn_in, self.w_ffn_in.ap[r0:r0 + D, j * 128:(j + 1) * 128].rearrange("(c p) n -> p c n", p=128)), q="pool")
                fw.dma(u_.v(), View(self.w_ffn_in, self.w_ffn_in.ap[r0:r0 + D, FH + j * 128:FH + (j + 1) * 128].rearrange("(c p) n -> p c n", p=128)), q="pool")
                for hf in range(NH):
                    pg, pu = psgu[n % 3]
                    s_ = sg[n % 2]
                    cols = slice(hf * HW, (hf + 1) * HW)
                    for c in range(16):
                        fw.mm(pg[:, 0:HW], g_[:, c, :], hT[:, c, cols], start=(c == 0), stop=(c == 15))
                    for c in range(16):
                        fw.mm(pu[:, 0:HW], u_[:, c, :], hT[:, c, cols], start=(c == 0), stop=(c == 15))
                    fw.act(s_[:, 0:HW], pg[:, 0:HW], AF.Silu)
                    fw.tt(actT[:, j, cols], s_[:, 0:HW], pu[:, 0:HW], ALU.mult)
                    n += 1
            fw.end_stage()
            fw.begin_stage()
            NTT = TB // 128
            wo = [fw.sb("f_wo%d" % i, [128, 11, 512], BF16) for i in range(4)]
            pso = [fw.ps("f_pso%d" % i, [128, 512], F32) for i in range(NTT)]
            ob = [fw.sb("f_ob%d" % i, [128, 512], F32) for i in range(2)]
            rb = [fw.sb("f_rb%d" % i, [128, 512], F32) for i in range(2)]
            nw_ = 0
            m = 0
            r0 = l * FH
            for cb in range(4):
                for kq in range(4):
                    w_ = wo[nw_ % 4]
                    nw_ += 1
                    fw.dma(w_.v(), View(self.w_ffn_out, self.w_ffn_out.ap[r0 + kq * 1408:r0 + (kq + 1) * 1408, cb * 512:(cb + 1) * 512].rearrange("(c p) n -> p c n", p=128)), q="pool")
                    for tt in range(NTT):
                        for jj in range(11):
                            j = kq * 11 + jj
                            fw.mm(pso[tt].v(), actT[:, j, tt * 128:(tt + 1) * 128], w_[:, jj, :], start=(j == 0), stop=(j == 43))
                for tt in range(NTT):
                    gt = tb * NTT + tt
                    rows = slice(gt * 128, (gt + 1) * 128)
                    fw.dma(rb[m % 2].v(), self.y[rows, cb * 512:(cb + 1) * 512].k(gt), q="sp")
                    fw.tt(ob[m % 2].v(), pso[tt].v(), rb[m % 2].v(), ALU.add)
                    fw.dma(self.y[rows, cb * 512:(cb + 1) * 512].k(gt), ob[m % 2].v(), q="sp")
                    m += 1
            fw.end_stage()
        fw.tiles.remove(actT)
        pes.close()

    def inv_chain(self, nh, X, XT, TT, Xb, XTb, psA, psB, psC, gen=False):
        g = self._inv_chain_gen(nh, X, XT, TT, Xb, XTb, psA, psB, psC)
        if gen:
            return g
        for _ in g:
            pass

    def _inv_chain_gen(self, nh, X, XT, TT, Xb, XTb, psA, psB, psC):
        fw = self.fw
        cur, curT, nxt, nxtT = X, XT, Xb, XTb
        for step in range(5):
            last = step == 4
            for h in range(nh):
                fw.mm(psA[:, h, :], curT[:, h, :], cur[:, h, :])
            if not last:
                for h in range(nh):
                    fw.mm(psB[:, h, :], cur[:, h, :], curT[:, h, :])
            yield
            fw.copy(nxt.v(), psA, eng="act")
            if not last:
                fw.copy(nxtT.v(), psB, eng="dve")
            yield
            for h in range(nh):
                fw.mm(psC[:, h, :], nxt[:, h, :], TT[:, h, :])
            yield
            fw.tt(TT.v(), TT.v(), psC, ALU.add)
            yield
            cur, curT, nxt, nxtT = nxt, nxtT, cur, curT

    def rwkv(self, l):
        fw = self.fw
        T = self.T
        NW = 1824
        fw.begin_stage()
        self.load_consts()
        c = self.c
        ident64 = c[0:64, C_ID:C_ID + 64]
        ones64 = c[0:64, C_ONE:C_ONE + 64]
        m3 = lambda col: bcv(View(c, c.ap[0:64, col:col + 64].unsqueeze(1)), [64, 8, 64])
        mu = fw.sb("r_mu", [64, NW], F32)
        fw.dma(mu[:, 0:1792], rowbc(self.mu_rwkv, l, 0, 1792, 64))
        if l > 0:
            fw.dma(mu[:, 1792:1824], rowbc(self.mu_vres, 0, 0, 32, 64))
        else:
            fw.memset(mu[:, 1792:1824], 0.0)
        wbw = fw.sb("r_wbw", [65, 512], F32)
        fw.dma(wbw[0:64, :], self.rwkv_w_lora_b[l * 64:(l + 1) * 64, :])
        fw.dma(wbw[64:65, :], self.rwkv_w0[l:l + 1, :])
        wba = fw.sb("r_wba", [65, 512], F32)
        fw.dma(wba[0:64, :], self.rwkv_a_lora_b[l * 64:(l + 1) * 64, :])
        fw.dma(wba[64:65, :], self.rwkv_a0[l:l + 1, :])
        wbg = fw.sb("r_wbg", [128, 512], F32)
        fw.dma(wbg.v(), self.rwkv_g_lora_b[l * 128:(l + 1) * 128, :])
        wbv = fw.sb("r_wbv", [33, 512], F32)
        if l > 0:
            fw.dma(wbv[0:32, :], self.rwkv_v_lora_b[0:32, :])
            fw.dma(wbv[32:33, :], self.rwkv_v0[0:1, :])
        kkb = fw.sb("r_kkb", [64, 512], F32)
        fw.dma(kkb.v(), rowbc(self.rwkv_k_k, l, 0, 512, 64))
        kab = fw.sb("r_kab", [64, 512], F32)
        fw.dma(kab.v(), rowbc(self.rwkv_k_a, l, 0, 512, 64))
        omka = fw.sb("r_omka", [64, 512], F32)
        fw.ts(omka.v(), kab.v(), -1.0, ALU.mult, 1.0, ALU.add)
        rkb = fw.sb("r_rkb", [64, 512], F32)
        fw.dma(rkb.v(), rowbc(self.rwkv_r_k, l, 0, 512, 64))
        lnw = fw.sb("r_lnw", [64, 512], F32)
        fw.dma(lnw.v(), rowbc(self.rwkv_ln_w, l, 0, 512, 64))
        lnb = fw.sb("r_lnb", [64, 512], F32)
        fw.dma(lnb.v(), rowbc(self.rwkv_ln_b, l, 0, 512, 64))
        H = fw.sb("r_H", [64, 8, 64], F32R)
        fw.ts(H.v().re("p h n -> p (h n)"), c[0:64, 0:512], 0.0, ALU.mult)
        mk2 = lambda nm, shp, dt=F32: [fw.sb("%s%d" % (nm, i), shp, dt) for i in range(2)]
        mk1 = lambda nm, shp, dt=F32: [fw.sb("%s%d" % (nm, 0), shp, dt)] * 2
        R32 = ("vp", "Bb", "Kb", "X", "U")
        pcs, pps, prws = mk1("r_pc", [64, NW]), mk1("r_pp", [64, NW]), mk2("r_prw", [64, NW])
        LTs = mk1("r_LT", [128, 4, 64])
        fw.memset(LTs[0][64:65, 0:2, :], 1.0)
        fw.memset(LTs[0][32:33, 3, :], 1.0)
        dbl = ["vp", "Bb", "Kb", "kp", "g"]
        sgl = ["lw", "a", "kk", "b", "cws", "Ep", "Em", "Ex", "EL", "At", "Bt", "Kt", "Rt", "t1", "t2", "t3", "t4", "X", "U", "yo"]
        S = {nm: mk2("r_" + nm, [64, 512], F32R if nm in R32 else F32) for nm in dbl}
        S.update({nm: mk1("r_" + nm, [64, 512], F32R if nm in R32 else F32) for nm in sgl})
        M = {nm: mk2("r_" + nm, [64, 8, 64], F32R) for nm in ["AtT", "BtT", "KtT", "RtT"]}
        M.update({nm: mk1("r_" + nm, [64, 8, 64], F32R) for nm in ["X1", "XT1", "X2", "XT2", "TT", "LrbT", "AakT", "LrkT"]})
        small = {nm: mk1("r_" + nm, [64, 8]) for nm in ["ss", "mean", "var", "bon"]}
        small["pct"] = mk2("r_pct", [64, 8])
        vfs = mk1("r_vf", [64, 512])
        ps = [fw.ps("r_ps%d" % i, [128, 512], F32) for i in range(8)]
        P8 = lambda i: View(ps[i], ps[i].ap[0:64, :].rearrange("p (h n) -> p h n", h=8))
        h3 = lambda v: v.re("p (h n) -> p h n", h=8)
        col8 = lambda t: bcv(View(t.tile, t.ap.unsqueeze(2)), [64, 8, 64])
        nch = T // 64

        def phaseA(ch):
            i = ch % 2
            t0 = ch * 64
            pc, pp, prw, LT = pcs[i], pps[i], prws[i], LTs[i]
            s = {k_: v_[i] for k_, v_ in S.items()}
            mm_ = {k_: v_[i] for k_, v_ in M.items()}
            sm = {k_: v_[i] for k_, v_ in small.items()}
            fw.dma(pc[:, 0:1792], self.P[t0:t0 + 64, 0:1792])
            if l > 0:
                fw.dma(pc[:, 1792:1824], self.P[t0:t0 + 64, NIN:NP])
            elif ch == 0:
                fw.memset(pc[:, 1792:1824], 0.0)
            if ch == 0:
                fw.memset(pp.v(), 0.0)
                fw.dma(pp[1:64, 0:1792], self.P[0:63, 0:1792])
                if l > 0:
                    fw.dma(pp[1:64, 1792:1824], self.P[0:63, NIN:NP])
            else:
                fw.dma(pp[:, 0:1792], self.P[t0 - 1:t0 + 63, 0:1792])
                if l > 0:
                    fw.dma(pp[:, 1792:1824], self.P[t0 - 1:t0 + 63, NIN:NP])
            yield
            fw.tt(pp.v(), pp.v(), pc.v(), ALU.subtract)
            yield
            fw.tt(pp.v(), pp.v(), mu.v(), ALU.mult)
            yield
            fw.tt(prw.v(), pc.v(), pp.v(), ALU.add)
            yield
            r, k_, v = prw[:, 0:512], prw[:, 512:1024], prw[:, 1024:1536]
            fw.act(prw[:, 1536:1600], prw[:, 1536:1600], AF.Tanh)
            fw.act(prw[:, 1664:1792], prw[:, 1664:1792], AF.Sigmoid)
            pl = View(ps[0], ps[0].ap[:, 0:256].rearrange("p (a t) -> p a t", a=4))
            fw.transpose(pl[0:64, 0, :], prw[:, 1536:1600], ident64)
            fw.transpose(pl[0:64, 1, :], prw[:, 1600:1664], ident64)
            fw.transpose(pl[0:128, 2, :], prw[:, 1664:1792], ident64)
            if l > 0:
                fw.transpose(pl[0:32, 3, :], prw[:, 1792:1824], ident64)
            yield
            fw.copy(LT[0:64, 0:2, :], pl[0:64, 0:2, :], eng="dve")
            fw.copy(LT[0:128, 2, :], pl[0:128, 2, :], eng="act")
            if l > 0:
                fw.copy(LT[0:32, 3, :], pl[0:32, 3, :], eng="dve")
            yield
            fw.mm(ps[1][0:64, :], LT[0:65, 0, :], wbw.v())
            fw.mm(ps[2][0:64, :], LT[0:65, 1, :], wba.v())
            yield
            fw.act(s["lw"].v(), ps[1][0:64, :], AF.Sigmoid)
            fw.act(s["a"].v(), ps[2][0:64, :], AF.Sigmoid)
            fw.mm(ps[1][0:64, :], LT[0:128, 2, :], wbg.v())
            if l > 0:
                fw.mm(ps[2][0:64, :], LT[0:33, 3, :], wbv.v())
            yield
            fw.ts(s["lw"].v(), s["lw"].v(), -math.exp(-0.5), ALU.mult)
            yield
            fw.copy(s["g"].v(), ps[1][0:64, :], eng="dve")
            yield
            if l > 0:
                vf = vfs[i]
                fw.dma(vf.v(), self.VF[t0:t0 + 64, :])
                fw.act(s["t1"].v(), ps[2][0:64, :], AF.Sigmoid)
                fw.tt(vf.v(), vf.v(), v, ALU.subtract)
                yield
                fw.tt(vf.v(), vf.v(), s["t1"].v(), ALU.mult)
                yield
                fw.tt(s["vp"].v(), v, vf.v(), ALU.add)
            else:
                fw.copy(s["vp"].v(), v, eng="dve")
                fw.dma(self.VF[t0:t0 + 64, :], v)
            yield
            fw.tt(s["kk"].v(), k_, kkb.v(), ALU.mult)
            yield
            fw.tt(s["t1"].v(), s["kk"].v(), s["kk"].v(), ALU.mult)
            yield
            fw.reduce(sm["ss"].v(), h3(s["t1"].v()))
            fw.act(sm["ss"].v(), sm["ss"].v(), AF.Ln, bias=EPS)
            fw.act(sm["ss"].v(), sm["ss"].v(), AF.Exp, scale=-0.5)
            yield
            fw.tt(h3(s["kk"].v()), h3(s["kk"].v()), col8(sm["ss"].v()), ALU.mult)
            yield
            fw.tt(s["t1"].v(), s["a"].v(), kab.v(), ALU.mult)
            yield
            fw.tt(s["t1"].v(), s["t1"].v(), omka.v(), ALU.add)
            yield
            fw.tt(s["kp"].v(), k_, s["t1"].v(), ALU.mult)
            yield
            fw.tt(s["b"].v(), s["kk"].v(), s["a"].v(), ALU.mult)
            yield
            fw.mm(ps[1][0:64, :], c[0:64, C_LE:C_LE + 64], s["lw"].v())
            fw.mm(ps[2][0:64, :], ones64, s["lw"].v())
            pct_ps = View(ps[0], ps[0].ap[0:64, 256:264])
            for h in range(8):
                fw.mm(pct_ps[:, h:h + 1], s["lw"][:, h * 64:(h + 1) * 64], c[0:64, C_ONE:C_ONE + 1])
            yield
            fw.act(sm["pct"].v(), pct_ps, AF.Exp)
            fw.copy(s["cws"].v(), ps[1][0:64, :], eng="dve")
            fw.act(s["Ep"].v(), ps[1][0:64, :], AF.Exp)
            fw.act(s["Em"].v(), ps[1][0:64, :], AF.Exp, scale=-1.0)
            yield
            fw.tt(s["t1"].v(), s["cws"].v(), s["lw"].v(), ALU.subtract)
            fw.act(s["Ex"].v(), s["t1"].v(), AF.Exp)
            yield
            fw.tt(s["t2"].v(), ps[2][0:64, :], s["cws"].v(), ALU.subtract)
            fw.act(s["EL"].v(), s["t2"].v(), AF.Exp)
            yield
            fw.tt(s["Bt"].v(), s["b"].v(), s["Em"].v(), ALU.mult)
            yield
            fw.tt(s["Kt"].v(), s["kp"].v(), s["Em"].v(), ALU.mult)
            yield
            fw.tt(s["Rt"].v(), r, s["Ep"].v(), ALU.mult)
            yield
            fw.stt(s["At"].v(), s["kk"].v(), -1.0, s["Ex"].v(), ALU.mult, ALU.mult)
            yield
            fw.tt(s["Bb"].v(), s["b"].v(), s["EL"].v(), ALU.mult)
            yield
            fw.tt(s["Kb"].v(), s["kp"].v(), s["EL"].v(), ALU.mult)
            yield
            for j, (src_, dst_) in enumerate([("Bt", "BtT"), ("Kt", "KtT"), ("Rt", "RtT"), ("At", "AtT")]):
                pt = P8((1 + j % 2) if not os.environ.get('TRB') else (4 + j))
                for h in range(8):
                    fw.transpose(pt[:, h, :], s[src_][:, h * 64:(h + 1) * 64], ident64)
                yield
                fw.copy(mm_[dst_].v(), pt, eng="act" if j % 2 else "dve")
                yield

        def phaseBC(ch):
            i = ch % 2
            t0 = ch * 64
            prw = prws[i]
            s = {k_: v_[i] for k_, v_ in S.items()}
            mm_ = {k_: v_[i] for k_, v_ in M.items()}
            sm = {k_: v_[i] for k_, v_ in small.items()}
            r = prw[:, 0:512]
            vp = s["vp"]
            AtT, BtT, KtT, RtT = mm_["AtT"], mm_["BtT"], mm_["KtT"], mm_["RtT"]
            for h in range(8):
                fw.mm(P8(3)[:, h, :], BtT[:, h, :], AtT[:, h, :])
            for h in range(8):
                fw.mm(P8(4)[:, h, :], AtT[:, h, :], BtT[:, h, :])
            yield
            fw.tt(mm_["XT1"].v(), P8(3), m3(C_LT), ALU.mult)
            fw.tt(mm_["X1"].v(), P8(4), m3(C_GT), ALU.mult)
            for h in range(8):
                fw.mm(P8(5)[:, h, :], BtT[:, h, :], RtT[:, h, :])
            for h in range(8):
                fw.mm(P8(6)[:, h, :], KtT[:, h, :], AtT[:, h, :])
            for h in range(8):
                fw.mm(P8(7)[:, h, :], KtT[:, h, :], RtT[:, h, :])
            yield
            fw.tt(mm_["TT"].v(), mm_["XT1"].v(), m3(C_ID), ALU.add)
            yield
            fw.tt(mm_["LrbT"].v(), P8(5), m3(C_LE), ALU.mult)
            yield
            fw.tt(mm_["AakT"].v(), P8(6), m3(C_LT), ALU.mult)
            yield
            fw.tt(mm_["LrkT"].v(), P8(7), m3(C_LE), ALU.mult)
            yield
            yield from self.inv_chain(8, mm_["X1"], mm_["XT1"], mm_["TT"], mm_["X2"], mm_["XT2"], P8(3), P8(4), P8(5), gen=True)
            TT = mm_["TT"]
            for h in range(8):
                hs_ = slice(h * 64, (h + 1) * 64)
                fw.mm(P8(6)[:, h, :], AtT[:, h, :], H[:, h, :], start=True, stop=False)
                fw.mm(P8(6)[:, h, :], mm_["AakT"][:, h, :], vp[:, hs_], start=False, stop=True)
            yield
            fw.copy(s["X"].v(), ps[6][0:64, :], eng="dve")
            yield
            for h in range(8):
                hs_ = slice(h * 64, (h + 1) * 64)
                fw.mm(P8(7)[:, h, :], TT[:, h, :], s["X"][:, hs_])
            yield
            fw.copy(s["U"].v(), ps[7][0:64, :], eng="act")
            yield
            for h in range(8):
                hs_ = slice(h * 64, (h + 1) * 64)
                fw.mm(P8(4)[:, h, :], s["Bb"][:, hs_], s["U"][:, hs_], start=True, stop=False)
                fw.mm(P8(4)[:, h, :], s["Kb"][:, hs_], vp[:, hs_], start=False, stop=True)
            for h in range(8):
                hs_ = slice(h * 64, (h + 1) * 64)
                fw.mm(P8(3)[:, h, :], RtT[:, h, :], H[:, h, :], start=True, stop=False)
                fw.mm(P8(3)[:, h, :], mm_["LrbT"][:, h, :], s["U"][:, hs_], start=False, stop=False)
                fw.mm(P8(3)[:, h, :], mm_["LrkT"][:, h, :], vp[:, hs_], start=False, stop=True)
            yield
            fw.tt(H.v(), H.v(), col8(sm["pct"].v()), ALU.mult)
            yield
            fw.tt(H.v(), H.v(), P8(4), ALU.add)
            yield
            yo = s["yo"]
            fw.reduce(sm["mean"].v(), P8(3))
            fw.ts(sm["mean"].v(), sm["mean"].v(), 1.0 / 64, ALU.mult)
            yield
            fw.tt(h3(yo.v()), P8(3), col8(sm["mean"].v()), ALU.subtract)
            yield
            fw.tt(s["t3"].v(), yo.v(), yo.v(), ALU.mult)
            yield
            fw.reduce(sm["var"].v(), h3(s["t3"].v()))
            fw.act(sm["var"].v(), sm["var"].v(), AF.Ln, bias=64e-5, scale=1.0 / 64)
            fw.act(sm["var"].v(), sm["var"].v(), AF.Exp, scale=-0.5)
            yield
            fw.tt(h3(yo.v()), h3(yo.v()), col8(sm["var"].v()), ALU.mult)
            yield
            fw.tt(yo.v(), yo.v(), lnw.v(), ALU.mult)
            yield
            fw.tt(yo.v(), yo.v(), lnb.v(), ALU.add)
            yield
            fw.tt(s["t3"].v(), r, s["kp"].v(), ALU.mult)
            yield
            fw.tt(s["t3"].v(), s["t3"].v(), rkb.v(), ALU.mult)
            yield
            fw.reduce(sm["bon"].v(), h3(s["t3"].v()))
            yield
            fw.tt(h3(s["t4"].v()), h3(vp.v()), col8(sm["bon"].v()), ALU.mult)
            yield
            fw.tt(yo.v(), yo.v(), s["t4"].v(), ALU.add)
            yield
            fw.tt(yo.v(), yo.v(), s["g"].v(), ALU.mult)
            fw.dma(self.MIX[t0:t0 + 64, 0:512].k(("r", ch)), yo.v())
            yield

        for _ in phaseA(0):
            pass
        for ch in range(nch):
            gens = [phaseBC(ch)]
            if ch + 1 < nch:
                gens.append(phaseA(ch + 1))
            if os.environ.get('SEQ'):
                for g_ in gens:
                    for _ in g_:
                        pass
            else:
                run_interleaved(gens)
        fw.end_stage()

    def gdn(self, l):
        fw = self.fw
        T = self.T
        fw.begin_stage()
        self.load_consts()
        c = self.c
        ident64 = c[0:64, C_ID:C_ID + 64]
        m4 = lambda col: bcv(View(c, c.ap[0:64, col:col + 64].unsqueeze(1)), [64, 4, 64])
        cw = fw.sb("g_cw", [64, 4, 1536], F32)
        for i in range(4):
            fw.dma(cw[:, i, :], rowbc(self.gdn_conv_w, l * 4 + i, 0, 1536, 64))
        nA = fw.sb("g_nA", [64, 4], F32)
        fw.dma(nA.v(), rowbc(self.gdn_A_log, l, 0, 4, 64))
        fw.act(nA.v(), nA.v(), AF.Exp)
        fw.ts(nA.v(), nA.v(), -1.0, ALU.mult)
        dtb = fw.sb("g_dtb", [64, 4], F32)
        fw.dma(dtb.v(), rowbc(self.gdn_dt_bias, l, 0, 4, 64))
        nw = fw.sb("g_nw", [64, 128], F32)
        fw.dma(nw.v(), rowbc(self.gdn_norm_w, l, 0, 128, 64))
        S = fw.sb("g_S", [128, 4, 128], F32R)
        fw.ts(S.v().re("p h n -> p (h n)"), c[:, 0:512], 0.0, ALU.mult)
        xs = [fw.sb("g_x%d" % i, [64, 1536], F32) for i in range(4)]
        zab = [fw.sb("g_zab%d" % b, [64, 520], F32) for b in range(2)]
        R32s = ("vb", "kbg", "kdec", "vn")
        dbl_s = ("vb", "kbg", "kdec")
        names = ["kb", "vb", "kbg", "kdec", "u", "vn", "o", "t1"]
        s2 = {nm: [fw.sb("g_%s%d" % (nm, i), [64, 512], F32R if nm in R32s else F32) for i in range(2 if nm in dbl_s else 1)] for nm in names}
        qkv = fw.sb("g_qkv", [64, 1536], F32)
        sq = fw.sb("g_sq", [64, 1024], F32)
        sm = {nm: fw.sb("g_" + nm, [64, 8], F32) for nm in ["ss", "g", "beta", "gc", "eg", "egl", "sp", "oss"]}
        egl128s = [fw.sb("g_egl128_%d" % i, [128, 4], F32) for i in range(2)]
        R32m = ("X1", "XT1", "X2", "XT2", "TT", "attnT")
        dbl_m = ("Egt", "ETlt", "ETle")
        M2 = {nm: [fw.sb("g_%s%d" % (nm, i), [64, 4, 64], F32R if nm in R32m else F32) for i in range(2 if nm in dbl_m else 1)]
              for nm in ["dg", "tmp", "E", "ET", "Egt", "ETlt", "ETle", "X1", "XT1", "X2", "XT2", "TT", "attnT"]}
        dbl_f = ("KT", "QT", "KBT", "QgT")
        F2 = {nm: [fw.sb("g_%s%d" % (nm, i), [128, 4, 64], F32 if nm == "EGR" else F32R) for i in range(2 if nm in dbl_f else 1)]
              for nm in ["KT", "QT", "KBT", "QgT", "wT", "EGR"]}
        ps = [fw.ps("g_ps%d" % i, [128, 512], F32) for i in range(8)]
        P4 = lambda i: View(ps[i], ps[i].ap[0:64, 0:256].rearrange("p (h n) -> p h n", h=4))
        F4 = lambda i, hf=0: View(ps[i], ps[i].ap[:, hf * 256:(hf + 1) * 256].rearrange("p (h n) -> p h n", h=4))
        T4 = lambda i: View(ps[i], ps[i].ap[0:64, :].rearrange("p (h n) -> p h n", h=4))
        h4 = lambda v: v.re("p (h n) -> p h n", h=4)
        col = lambda t, n: bcv(View(t.tile, t.ap.unsqueeze(2)), [64, t.ap.shape[1], n])
        sel = lambda d, i: {k_: v_[i % len(v_)] for k_, v_ in d.items()}
        nch = T // 64

        def phaseA(ch):
            b = ch % 2
            t0 = ch * 64
            X = xs
            s = sel(s2, b)
            M = sel(M2, b)
            F = sel(F2, b)
            egl128 = egl128s[b]
            for i in range(4):
                sh = 3 - i
                if t0 - sh < 0:
                    fw.memset(X[i].v(), 0.0)
                    fw.dma(X[i][sh:64, :], self.P[0:64 - sh, GDN0:GDN0 + 1536])
                else:
                    fw.dma(X[i].v(), self.P[t0 - sh:t0 - sh + 64, GDN0:GDN0 + 1536])
            fw.dma(zab[b].v(), self.P[t0:t0 + 64, GDN0 + 1536:GDN0 + 2056])
            a_ = zab[b][:, 512:516]
            b_ = zab[b][:, 516:520]
            yield
            acc = qkv
            fw.tt(acc.v(), X[0].v(), cw[:, 0, :], ALU.mult)
            yield
            for i in range(1, 4):
                fw.tt(X[i].v(), X[i].v(), cw[:, i, :], ALU.mult)
                yield
                fw.tt(acc.v(), acc.v(), X[i].v(), ALU.add)
                yield
            fw.act(qkv.v(), acc.v(), AF.Silu)
            yield
            fw.tt(sq.v(), qkv[:, 0:1024], qkv[:, 0:1024], ALU.mult)
            yield
            fw.reduce(sm["ss"].v(), sq.v().re("p (h n) -> p h n", h=8))
            fw.act(sm["ss"].v(), sm["ss"].v(), AF.Ln, bias=EPS)
            fw.act(sm["ss"].v(), sm["ss"].v(), AF.Exp, scale=-0.5)
            fw.ts(sm["ss"][:, 0:4], sm["ss"][:, 0:4], 128.0 ** -0.5, ALU.mult)
            yield
            qk3 = qkv[:, 0:1024].re("p (h n) -> p h n", h=8)
            fw.tt(qk3, qk3, col(sm["ss"].v(), 128), ALU.mult)
            yield
            qn, kn, vv = qkv[:, 0:512], qkv[:, 512:1024], qkv[:, 1024:1536]
            fw.act(sm["beta"][:, 0:4], b_, AF.Sigmoid)
            fw.tt(sm["sp"][:, 0:4], a_, dtb.v(), ALU.add)
            fw.act(sm["sp"][:, 0:4], sm["sp"][:, 0:4], AF.Exp)
            fw.act(sm["sp"][:, 0:4], sm["sp"][:, 0:4], AF.Ln, bias=1.0)
            fw.tt(sm["g"][:, 0:4], sm["sp"][:, 0:4], nA.v(), ALU.mult)
            yield
            g4 = sm["g"][:, 0:4]
            beta = sm["beta"][:, 0:4]
            gcp = View(ps[0], ps[0].ap[0:64, 256:260])
            glp = View(ps[0], ps[0].ap[:, 260:264])
            fw.mm(gcp, c[0:64, C_LE:C_LE + 64], g4)
            fw.mm(glp, c[0:64, C_ONE:C_ONE + 128], g4)
            yield
            gc = sm["gc"][:, 0:4]
            fw.copy(gc, gcp, eng="dve")
            fw.act(sm["eg"][:, 0:4], gcp, AF.Exp)
            fw.tt(sm["egl"][:, 0:4], glp[0:64, :], gc, ALU.subtract)
            fw.act(sm["egl"][:, 0:4], sm["egl"][:, 0:4], AF.Exp)
            fw.act(egl128.v(), glp, AF.Exp)
            yield
            for h in range(4):
                fw.ts(M["dg"][:, h, :], ident64, sm["gc"][:, h:h + 1], ALU.mult)
            yield
            fw.mm(ps[1][:, 0:256], c[0:64, C_ONE:C_ONE + 128], M["dg"].v().re("p h n -> p (h n)"))
            fw.ts(M["tmp"].v(), col(gc, 64), -1.0, ALU.mult)
            yield
            fw.act(F["EGR"].v(), F4(1), AF.Exp)
            tps = View(ps[1], ps[1].ap[0:64, 256:512])
            fw.mm(tps, c[0:64, C_ONE:C_ONE + 64], M["dg"].v().re("p h n -> p (h n)"), start=True, stop=False)
            fw.mm(tps, ident64, M["tmp"].v().re("p h n -> p (h n)"), start=False, stop=True)
            tps3 = tps.re("p (h n) -> p h n", h=4)
            yield
            fw.ts(M["E"].v(), tps3, 0.0, ALU.max)
            fw.act(M["E"].v(), M["E"].v(), AF.Exp, scale=-1.0)
            yield
            fw.ts(M["ET"].v(), tps3, 0.0, ALU.min)
            fw.act(M["ET"].v(), M["ET"].v(), AF.Exp)
            yield
            fw.tt(M["Egt"].v(), M["E"].v(), m4(C_NGT), ALU.mult)
            yield
            fw.tt(M["ETlt"].v(), M["ET"].v(), m4(C_NLT), ALU.mult)
            yield
            fw.tt(M["ETle"].v(), M["ET"].v(), m4(C_LE), ALU.mult)
            yield
            fw.tt(h4(s["kb"].v()), h4(kn), col(beta, 128), ALU.mult)
            yield
            fw.tt(h4(s["vb"].v()), h4(vv), col(beta, 128), ALU.mult)
            yield
            fw.tt(h4(s["kbg"].v()), h4(s["kb"].v()), col(sm["eg"][:, 0:4], 128), ALU.mult)
            yield
            fw.tt(h4(s["kdec"].v()), h4(kn), col(sm["egl"][:, 0:4], 128), ALU.mult)
            yield
            for h in range(4):
                hs_ = slice(h * 128, (h + 1) * 128)
                fw.transpose(F4(2, 0)[:, h, :], kn[:, hs_], ident64)
                fw.transpose(F4(2, 1)[:, h, :], qn[:, hs_], ident64)
            yield
            fw.copy(F["KT"].v(), F4(2, 0), eng="dve")
            fw.copy(F["QT"].v(), F4(2, 1), eng="act")
            yield
            fw.tt(F["QgT"].v(), F4(2, 1), F["EGR"].v(), ALU.mult)
            for h in range(4):
                hs_ = slice(h * 128, (h + 1) * 128)
                fw.transpose(F4(0, 0)[:, h, :], s["kb"][:, hs_], ident64)
            yield
            fw.copy(F["KBT"].v(), F4(0, 0), eng="act")
            yield

        def phaseBC(ch):
            b = ch % 2
            t0 = ch * 64
            s = sel(s2, b)
            M = sel(M2, b)
            F = sel(F2, b)
            egl128 = egl128s[b]
            z = zab[b][:, 0:512]
            for h in range(4):
                fw.mm(P4(3)[:, h, :], F["KBT"][:, h, :], F["KT"][:, h, :])
                fw.mm(P4(4)[:, h, :], F["KT"][:, h, :], F["KBT"][:, h, :])
                fw.mm(P4(5)[:, h, :], F["KT"][:, h, :], F["QT"][:, h, :])
            yield
            fw.tt(M["X1"].v(), P4(3), M["Egt"].v(), ALU.mult)
            yield
            fw.tt(M["XT1"].v(), P4(4), M["ETlt"].v(), ALU.mult)
            yield
            fw.tt(M["attnT"].v(), P4(5), M["ETle"].v(), ALU.mult)
            yield
            fw.tt(M["TT"].v(), M["XT1"].v(), m4(C_ID), ALU.add)
            yield
            yield from self.inv_chain(4, M["X1"], M["XT1"], M["TT"], M["X2"], M["XT2"], P4(3), P4(4), P4(5), gen=True)
            TT = M["TT"]
            for h in range(4):
                hs_ = slice(h * 128, (h + 1) * 128)
                fw.mm(T4(6)[:, h, :], TT[:, h, :], s["vb"][:, hs_])
                fw.mm(F4(7)[:, h, :], s["kbg"][:, hs_], TT[:, h, :])
            yield
            fw.copy(s["u"].v(), ps[6][0:64, :], eng="dve")
            fw.copy(F["wT"].v(), F4(7), eng="act")
            yield
            for h in range(4):
                fw.mm(T4(3)[:, h, :], F["wT"][:, h, :], S[:, h, :])
            yield
            fw.tt(s["vn"].v(), s["u"].v(), ps[3][0:64, :], ALU.subtract)
            yield
            for h in range(4):
                hs_ = slice(h * 128, (h + 1) * 128)
                fw.mm(ps[5][:, hs_], s["kdec"][:, hs_], s["vn"][:, hs_])
            for h in range(4):
                hs_ = slice(h * 128, (h + 1) * 128)
                fw.mm(T4(4)[:, h, :], F["QgT"][:, h, :], S[:, h, :], start=True, stop=False)
                fw.mm(T4(4)[:, h, :], M["attnT"][:, h, :], s["vn"][:, hs_], start=False, stop=True)
            yield
            for h in range(4):
                hs_ = slice(h * 128, (h + 1) * 128)
                fw.stt(S[:, h, :], S[:, h, :], egl128[:, h:h + 1], ps[5][:, hs_], ALU.mult, ALU.add)
                yield
            o = s["o"]
            fw.copy(o.v(), ps[4][0:64, :], eng="act")
            yield
            fw.tt(s["t1"].v(), o.v(), o.v(), ALU.mult)
            yield
            fw.reduce(sm["oss"][:, 0:4], h4(s["t1"].v()))
            fw.act(sm["oss"][:, 0:4], sm["oss"][:, 0:4], AF.Ln, bias=EPS, scale=1.0 / 128)
            fw.act(sm["oss"][:, 0:4], sm["oss"][:, 0:4], AF.Exp, scale=-0.5)
            yield
            fw.tt(h4(o.v()), h4(o.v()), col(sm["oss"][:, 0:4], 128), ALU.mult)
            yield
            fw.tt(h4(o.v()), h4(o.v()), bcv(View(nw, nw.ap.unsqueeze(1)), [64, 4, 128]), ALU.mult)
            fw.act(s["t1"].v(), z, AF.Silu)
            yield
            fw.tt(o.v(), o.v(), s["t1"].v(), ALU.mult)
            fw.dma(self.MIX[t0:t0 + 64, 512:1024].k(("g", ch)), o.v())
            yield

        for _ in phaseA(0):
            pass
        for ch in range(nch):
            gens = [phaseBC(ch)]
            if ch + 1 < nch:
                gens.append(phaseA(ch + 1))
            iln = int(os.environ.get('GDN_ILN', '0'))
            if len(gens) == 2 and iln > 0:
                gb, ga = gens
                na = 0
                a_alive = b_alive = True
                while a_alive and na < iln:
                    if b_alive:
                        try:
                            next(gb)
                        except StopIteration:
                            b_alive = False
                    try:
                        next(ga)
                        na += 1
                    except StopIteration:
                        a_alive = False
                for _ in gb:
                    pass
                for _ in ga:
                    pass
            else:
                for g_ in gens:
                    for _ in g_:
                        pass
        fw.end_stage()

    def attn(self, l):
        fw = self.fw
        T = self.T
        NT = T // 128
        QG = min(T, 512)
        NQB = QG // 128
        lam_init = 0.8 - 0.6 * math.exp(-0.3 * l)
        fw.begin_stage()
        self.load_consts(bf=True)
        c = self.c
        cm = fw.sb("a_cm", [128, 4, 512], BF16)
        fw.dma(cm.v().re("p r n -> p (r n)"), self.cmask.v(), q="pool")
        onesb = fw.sb("a_ones", [128, 1], BF16)
        fw.memset(onesb.v(), 1.0)
        qnw = fw.sb("a_qnw", [128, 64], F32)
        fw.dma(qnw.v(), rowbc(self.diff_q_norm_w, l, 0, 64, 128))
        fw.ts(qnw.v(), qnw.v(), 0.125, ALU.mult)
        knw = fw.sb("a_knw", [128, 64], F32)
        fw.dma(knw.v(), rowbc(self.diff_k_norm_w, l, 0, 64, 128))
        subw = fw.sb("a_subw", [128, 128], F32)
        fw.dma(subw.v(), rowbc(self.diff_subln_w, l, 0, 128, 128))
        fw.ts(subw.v(), subw.v(), 1.0 - lam_init, ALU.mult)
        lv = fw.sb("a_lv", [128, 4, 64], F32)
        for i, dt_ in enumerate([self.diff_lambda_q1, self.diff_lambda_k1, self.diff_lambda_q2, self.diff_lambda_k2]):
            fw.dma(lv[:, i, :], rowbc(dt_, l, 0, 64, 128))
        ls = fw.sb("a_ls", [128, 4], F32)
        fw.tt(lv[:, 0, :], lv[:, 0, :], lv[:, 1, :], ALU.mult)
        fw.tt(lv[:, 2, :], lv[:, 2, :], lv[:, 3, :], ALU.mult)
        fw.reduce(ls[:, 0:1], lv[:, 0, :])
        fw.reduce(ls[:, 1:2], lv[:, 2, :])
        fw.act(ls[:, 0:2], ls[:, 0:2], AF.Exp)
        fw.tt(ls[:, 2:3], ls[:, 1:2], ls[:, 0:1], ALU.subtract)
        fw.ts(ls[:, 3:4], ls[:, 2:3], -lam_init, ALU.add)
        nlam = ls[:, 3:4]
        qT = fw.sb("a_qT", [64, 16, T], BF16)
        kT = fw.sb("a_kT", [64, 16, T], BF16)
        vx = fw.sb("a_vx", [128, NT, 8, 132], BF16)
        fw.memset(vx.v(), 1.0)
        xin = [fw.sb("a_xin%d" % i, [128, 1024], F32) for i in range(2)]
        xsq = fw.sb("a_xsq", [128, 1024], F32)
        xn = [fw.sb("a_xn%d" % i, [128, 1024], BF16) for i in range(2)]
        ss = fw.sb("a_ss", [128, 16], F32)
        ps = [fw.ps("a_ps%d" % i, [128, 512], F32) for i in range(8)]
        col16 = lambda t: bcv(View(t.tile, t.ap.unsqueeze(2)), [128, 16, 64])
        nrm = 0
        for t in range(NT):
            rows = slice(t * 128, (t + 1) * 128)
            for which, (c0, wt_, dstT) in enumerate([(DIFF0, qnw, qT), (DIFF0 + 1024, knw, kT)]):
                xi = xin[nrm % 2]
                xo = xn[nrm % 2]
                nrm += 1
                fw.dma(xi.v(), self.P[rows, c0:c0 + 1024])
                fw.tt(xsq.v(), xi.v(), xi.v(), ALU.mult)
                fw.reduce(ss.v(), xsq.v().re("p (g n) -> p g n", g=16))
                fw.act(ss.v(), ss.v(), AF.Sqrt, bias=EPS, scale=1.0 / 64)
                fw.recip(ss.v(), ss.v())
                x3 = xi.v().re("p (g n) -> p g n", g=16)
                fw.tt(x3, x3, col16(ss.v()), ALU.mult)
                fw.tt(xo.v().re("p (g n) -> p g n", g=16), x3, bcv(View(wt_, wt_.ap.unsqueeze(1)), [128, 16, 64]), ALU.mult)
                for half in range(2):
                    pt = View(ps[half], ps[half].ap[0:64, :].bitcast(BF16)[:, 0:1024].rearrange("p (g n) -> p g n", g=8))
                    for g in range(8):
                        gg = half * 8 + g
                        fw.transpose(pt[:, g, :], xo[:, gg * 64:(gg + 1) * 64], self.idb.v())
                    fw.copy(dstT[:, half * 8:(half + 1) * 8, t * 128:(t + 1) * 128], pt, eng=self.ev())
            xi = xin[nrm % 2]
            nrm += 1
            fw.dma(xi.v(), self.P[rows, DIFF0 + 2048:DIFF0 + 3072])
            fw.copy(vx[:, t, :, 0:128], xi.v().re("p (h e) -> p h e", h=8), eng="act")
        pex = [fw.sb("a_pex%d" % i, [128, 512], BF16) for i in range(4)]
        NBK = max(1, NQB // 2)
        osb = [fw.sb("a_osb%d" % i, [128, 2, NBK, 264], F32) for i in range(2)]
        eo = [fw.sb("a_eo%d" % i, [128, NQB, 128], F32) for i in range(2)]
        et = fw.sb("a_et", [128, NQB, 128], F32)
        esm = fw.sb("a_esm", [128, 4, NQB], F32)
        npx = 0
        nst = 0
        nq = 0
        for h in range(8):
            for qg in range(T // QG):
                nkb = (qg + 1) * NQB
                ob_sb = osb[nq % 2]
                o_ = eo[nq % 2]
                nq += 1
                items = [(m, kb) for m in range(2) for kb in range(nkb)]
                sts = {}

                def emit_st(i):
                    nonlocal nst
                    m, kb = items[i]
                    hm = h * 2 + m
                    st = ps[(0, 1, 6, 7)[nst % 4]]
                    nst += 1
                    r = kb - qg * NQB
                    fw.mm(st[:, 0:QG], kT[:, hm, kb * 128:(kb + 1) * 128], qT[:, hm, qg * QG:(qg + 1) * QG], start=True, stop=(r < 0))
                    if r >= 0:
                        fw.mm(st[:, 0:QG], self.idb.v(), cm[:, r, 0:QG], start=False, stop=True)
                    sts[i] = st

                PD = 3
                for i0 in range(min(PD, len(items))):
                    emit_st(i0)
                for i, (m, kb) in enumerate(items):
                    if i + PD < len(items):
                        emit_st(i + PD)
                    st = sts.pop(i)
                    px = pex[npx % 4]
                    npx += 1
                    fw.act(px[:, 0:QG], st[:, 0:QG], AF.Exp)
                    for qb in range(NQB):
                        if kb <= qg * NQB + qb:
                            ob_ = ps[2 + m * 2 + qb // 2]
                            oc = (qb % 2) * 132
                            fw.mm(ob_[:, oc:oc + 129], px[:, qb * 128:(qb + 1) * 128], vx[:, kb, h, 0:129],
                                  start=(kb == 0 and qb % 2 == 0), stop=(kb == qg * NQB + qb), skip_group_check=True)
                    if kb == nkb - 1:
                        for bk in range(NBK):
                            fw.copy(ob_sb[:, m, bk, :], ps[2 + m * 2 + bk][:, 0:264], eng=("dve" if bk % 2 == 0 else "act"))
                O = [ob_sb[:, m].re("p b (t c) -> p (b t) c", t=2) for m in range(2)]
                O = [o[:, 0:NQB, :] for o in O]
                rs = [o[:, :, 128:129].re("p q c -> p (q c)") for o in O]
                fw.recip(esm[:, 0, :], rs[0])
                fw.recip(esm[:, 1, :], rs[1])
                fw.ts(esm[:, 1, :], esm[:, 1, :], nlam, ALU.mult)
                bq = lambda v: bcv(View(v.tile, v.ap.unsqueeze(2)), [128, NQB, 128])
                fw.tt(o_.v(), O[0][:, :, 0:128], bq(esm[:, 0, :]), ALU.mult)
                fw.tt(et.v(), O[1][:, :, 0:128], bq(esm[:, 1, :]), ALU.mult)
                fw.tt(o_.v(), o_.v(), et.v(), ALU.add)
                fw.tt(et.v(), o_.v(), o_.v(), ALU.mult)
                fw.reduce(esm[:, 2, :], et.v())
                fw.act(esm[:, 2, :], esm[:, 2, :], AF.Ln, bias=EPS, scale=1.0 / 128)
                fw.act(esm[:, 2, :], esm[:, 2, :], AF.Exp, scale=-0.5)
                fw.tt(o_.v(), o_.v(), bq(esm[:, 2, :]), ALU.mult)
                fw.tt(o_.v(), o_.v(), bcv(View(subw, subw.ap.unsqueeze(1)), [128, NQB, 128]), ALU.mult)
                q0 = qg * QG
                dst = self.MIX[q0:q0 + QG, 1024 + h * 128:1024 + (h + 1) * 128].re("(q p) e -> p q e", p=128)
                fw.dma(dst.k(("a", h, qg)), o_.v())
        fw.end_stage()

    def build(self, stages=None):
        for l in range(self.L):
            for nm in ["in_proj", "rwkv", "gdn", "attn", "out_proj", "ffn"]:
                if stages is None or (l, nm) in stages:
                    getattr(self, nm)(l)
        self.fw.finish()
        return self.nc


PARAM_SHAPES2D = None


def make_inputs(inputs, b, T):
    L = 2
    r = lambda a, shp: np.ascontiguousarray(np.asarray(a, dtype=np.float32).reshape(shp))
    d = {}
    d["x"] = r(inputs["x"][b, :T], (T, D))
    d["attn_norm_w"] = r(inputs["attn_norm_w"], (L, D))
    d["w_in"] = r(inputs["w_in"], (L * D, NIN))
    d["w_vres_a"] = r(inputs["w_vres_a"], (D, 32))
    d["mu_rwkv"] = r(inputs["mu_rwkv"], (L, 1792))
    d["mu_vres"] = r(inputs["mu_vres"], (1, 32))
    d["rwkv_w0"] = r(inputs["rwkv_w0"], (L, 512))
    d["rwkv_w_lora_b"] = r(inputs["rwkv_w_lora_b"], (L * 64, 512))
    d["rwkv_a0"] = r(inputs["rwkv_a0"], (L, 512))
    d["rwkv_a_lora_b"] = r(inputs["rwkv_a_lora_b"], (L * 64, 512))
    d["rwkv_g_lora_b"] = r(inputs["rwkv_g_lora_b"], (L * 128, 512))
    d["rwkv_v0"] = r(inputs["rwkv_v0"], (1, 512))
    d["rwkv_v_lora_b"] = r(inputs["rwkv_v_lora_b"], (32, 512))
    for nm in ["rwkv_k_k", "rwkv_k_a", "rwkv_r_k", "rwkv_ln_w", "rwkv_ln_b"]:
        d[nm] = r(inputs[nm], (L, 512))
    d["gdn_conv_w"] = r(inputs["gdn_conv_w"], (L * 4, 1536))
    d["gdn_A_log"] = r(inputs["gdn_A_log"], (L, 4))
    d["gdn_dt_bias"] = r(inputs["gdn_dt_bias"], (L, 4))
    d["gdn_norm_w"] = r(inputs["gdn_norm_w"], (L, 128))
    for nm in ["diff_q_norm_w", "diff_k_norm_w", "diff_lambda_q1", "diff_lambda_k1", "diff_lambda_q2", "diff_lambda_k2"]:
        d[nm] = r(inputs[nm], (L, 64))
    d["diff_subln_w"] = r(inputs["diff_subln_w"], (L, 128))
    d["w_out"] = r(inputs["w_out"], (L * D, D))
    d["ffn_norm_w"] = r(inputs["ffn_norm_w"], (L, D))
    d["w_ffn_in"] = r(inputs["w_ffn_in"], (L * D, 2 * FH))
    d["w_ffn_out"] = r(inputs["w_ffn_out"], (L * FH, D))
    c, cm = make_consts()
    d["consts"] = c
    d["cmask"] = cm
    return d


_CACHE = {}


def kernel(**inputs):
    from concourse.bass_utils import run_bass_kernel_spmd
    T = 2048
    n = 8
    if "nc" not in _CACHE:
        _CACHE["nc"] = Prog(T, debug=False).build()
    nc = _CACHE["nc"]
    in_maps = [make_inputs(inputs, b, T) for b in range(n)]
    res = run_bass_kernel_spmd(nc, in_maps, core_ids=list(range(n)))
    out = np.stack([np.asarray(res.results[b]["y"], dtype=np.float32) for b in range(n)], axis=0)
    return out
```
